# Optimizing a Trainium2 kernel written in Bass

```python
import jax, jax.numpy as jnp
from jax import lax
import numpy as np

D_MODEL = 2048
BATCH = 4
SEQ = 2048
DEPTH = 4
DEC_BATCH = 8
DEC_SEQ = 4
PAST_LEN = 16384
PAGE_SIZE = 128

N_MIXERS = 2
N_ATTN = (DEPTH + 1) // 2
N_CONV = DEPTH // 2
N_HEADS = 16
HEAD_DIM = D_MODEL // N_HEADS
N_KV_HEADS = 4
ROT_DIM = HEAD_DIM // 4
ROPE_THETA = 500000.0
N_IDX_HEADS = 16
IDX_DIM = 64
IDX_ROT_DIM = IDX_DIM // 4
IDX_W_SCALE = (N_IDX_HEADS * IDX_DIM) ** -0.5
TOPK_MAX = 256
Q_BLOCK = 128
CONV_WIDTH = 31
D_FF = 5632
FFN_CONV_WIDTH = 3
LN_EPS = 1e-5
DN_ALPHA = (2 * DEPTH) ** 0.25
DN_BETA = (8 * DEPTH) ** -0.25

Q_COLS = N_HEADS * HEAD_DIM
KV_COLS = N_KV_HEADS * HEAD_DIM
QI_COLS = N_IDX_HEADS * IDX_DIM
IN_COLS = Q_COLS + 2 * KV_COLS + QI_COLS + IDX_DIM + N_IDX_HEADS

kernel_name = 'dsa_conformer_convffn_deepnorm_step'


def layer_norm(x, g, b):
    xf = x.astype(jnp.float32)
    mu = jnp.mean(xf, axis=-1, keepdims=True)
    var = jnp.mean(jnp.square(xf - mu), axis=-1, keepdims=True)
    return ((xf - mu) * lax.rsqrt(var + LN_EPS) * g.astype(jnp.float32) + b.astype(jnp.float32)).astype(x.dtype)


def rope_angles(pos, rot_dim):
    inv = ROPE_THETA ** (-jnp.arange(0, rot_dim, 2, dtype=jnp.float32) / rot_dim)
    ang = pos.astype(jnp.float32)[:, None] * inv[None, :]
    return jnp.cos(ang), jnp.sin(ang)


def apply_partial_rope(x, cos, sin):
    half = cos.shape[-1]
    x1 = x[..., :half].astype(jnp.float32)
    x2 = x[..., half:2 * half].astype(jnp.float32)
    r1 = (x1 * cos - x2 * sin).astype(x.dtype)
    r2 = (x2 * cos + x1 * sin).astype(x.dtype)
    return jnp.concatenate([r1, r2, x[..., 2 * half:]], axis=-1)


def project_attn(x, w_in):
    p = x @ w_in
    lead = x.shape[:-1]
    o1 = Q_COLS
    o2 = o1 + KV_COLS
    o3 = o2 + KV_COLS
    o4 = o3 + QI_COLS
    o5 = o4 + IDX_DIM
    q = p[..., :o1].reshape(*lead, N_HEADS, HEAD_DIM)
    k = p[..., o1:o2].reshape(*lead, N_KV_HEADS, HEAD_DIM)
    v = p[..., o2:o3].reshape(*lead, N_KV_HEADS, HEAD_DIM)
    qi = p[..., o3:o4].reshape(*lead, N_IDX_HEADS, IDX_DIM)
    ki = p[..., o4:o5]
    wi = p[..., o5:]
    return q, k, v, qi, ki, wi


def rope_all(q, k, qi, ki, pos):
    c, s = rope_angles(pos, ROT_DIM)
    ci, si = rope_angles(pos, IDX_ROT_DIM)
    return (apply_partial_rope(q, c[:, None], s[:, None]),
            apply_partial_rope(k, c[:, None], s[:, None]),
            apply_partial_rope(qi, ci[:, None], si[:, None]),
            apply_partial_rope(ki, ci, si))


def index_select(qi, wi, ki, q_pos, key_pos, k_sel):
    dots = jnp.einsum('thd,sd->ths', qi, ki, preferred_element_type=jnp.float32)
    score = jnp.einsum('th,ths->ts', wi.astype(jnp.float32) * IDX_W_SCALE, jax.nn.relu(dots))
    causal = key_pos[None, :] <= q_pos[:, None]
    _, idx = lax.top_k(jnp.where(causal, score, -jnp.inf), k_sel)
    valid = key_pos[idx] <= q_pos[:, None]
    return idx, valid


def attend_selected(q, k_sel, v_sel, valid):
    tq = q.shape[0]
    qg = q.reshape(tq, N_KV_HEADS, N_HEADS // N_KV_HEADS, HEAD_DIM)
    s = jnp.einsum('tngd,tsnd->tngs', qg, k_sel, preferred_element_type=jnp.float32) * (HEAD_DIM ** -0.5)
    s = jnp.where(valid[:, None, None, :], s, -jnp.inf)
    p = jax.nn.softmax(s, axis=-1)
    o = jnp.einsum('tngs,tsnd->tngd', p.astype(v_sel.dtype), v_sel)
    return o.reshape(tq, N_HEADS * HEAD_DIM)


def attn_layer_prompt(x, w_in, w_out):
    B, T, _ = x.shape
    q, k, v, qi, ki, wi = project_attn(x, w_in)
    pos = jnp.arange(T, dtype=jnp.int32)
    q, k, qi, ki = rope_all(q, k, qi, ki, pos)
    k_sel = min(TOPK_MAX, T // 4)
    nb = T // Q_BLOCK
    qb = q.reshape(B * nb, Q_BLOCK, N_HEADS, HEAD_DIM)
    qib = qi.reshape(B * nb, Q_BLOCK, N_IDX_HEADS, IDX_DIM)
    wib = wi.reshape(B * nb, Q_BLOCK, N_IDX_HEADS)
    b_ids = jnp.repeat(jnp.arange(B, dtype=jnp.int32), nb)
    blk_ids = jnp.tile(jnp.arange(nb, dtype=jnp.int32), B)

    def one_block(args):
        q_blk, qi_blk, wi_blk, b, blk = args
        q_pos = blk * Q_BLOCK + jnp.arange(Q_BLOCK, dtype=jnp.int32)
        idx, valid = index_select(qi_blk, wi_blk, ki[b], q_pos, pos, k_sel)
        return attend_selected(q_blk, k[b][idx], v[b][idx], valid)

    o = lax.map(one_block, (qb, qib, wib, b_ids, blk_ids))
    y = o.reshape(B, T, Q_COLS) @ w_out
    return y, k, v, ki


def attn_layer_sample(x, cache_k, cache_v, cache_ki, page_table, w_in, w_out):
    Bd, Tn, _ = x.shape
    q, k, v, qi, ki, wi = project_attn(x, w_in)
    pos_new = PAST_LEN + jnp.arange(Tn, dtype=jnp.int32)
    q, k, qi, ki = rope_all(q, k, qi, ki, pos_new)
    ki_past = cache_ki[page_table].reshape(Bd, PAST_LEN, IDX_DIM)
    ki_all = jnp.concatenate([ki_past, ki], axis=1)
    key_pos = jnp.arange(PAST_LEN + Tn, dtype=jnp.int32)
    k_sel = min(TOPK_MAX, (PAST_LEN + Tn) // 4)

    def one_seq(q_s, qi_s, wi_s, ki_s, k_new, v_new, pages):
        idx, valid = index_select(qi_s, wi_s, ki_s, pos_new, key_pos, k_sel)
        in_past = (idx < PAST_LEN)[..., None, None]
        past_idx = jnp.minimum(idx, PAST_LEN - 1)
        phys = pages[past_idx // PAGE_SIZE]
        off = past_idx % PAGE_SIZE
        new_idx = jnp.clip(idx - PAST_LEN, 0, Tn - 1)
        k_s = jnp.where(in_past, cache_k[phys, off], k_new[new_idx])
        v_s = jnp.where(in_past, cache_v[phys, off], v_new[new_idx])
        return attend_selected(q_s, k_s, v_s, valid)

    o = jax.vmap(one_seq)(q, qi, wi, ki_all, k, v, page_table)
    y = o.reshape(Bd, Tn, Q_COLS) @ w_out
    return y, k, v, ki


def causal_dwconv(x_pad, w, b):
    c = x_pad.shape[-1]
    y = lax.conv_general_dilated(x_pad, w[:, None, :], window_strides=(1,), padding='VALID',
                                 dimension_numbers=('NWC', 'WIO', 'NWC'), feature_group_count=c)
    return y + b


def conformer_conv(x, hist, w_pw1, b_pw1, w_dw, b_dw, ln_g, ln_b, w_pw2, b_pw2):
    a, gate = jnp.split(x @ w_pw1 + b_pw1, 2, axis=-1)
    a = a * jax.nn.sigmoid(gate)
    a_pad = jnp.concatenate([hist, a], axis=1)
    h = jax.nn.silu(layer_norm(causal_dwconv(a_pad, w_dw, b_dw), ln_g, ln_b))
    return h @ w_pw2 + b_pw2, a_pad[:, -(CONV_WIDTH - 1):]


def conv_ffn(x, hist, w_gate, w_up, w_c, b_c, w_down):
    g = x @ w_gate
    g_pad = jnp.concatenate([hist, g], axis=1)
    h = jax.nn.silu(causal_dwconv(g_pad, w_c, b_c)) * (x @ w_up)
    return h @ w_down, g_pad[:, -(FFN_CONV_WIDTH - 1):]


def run_trunk(x, attn_fn, conv_hist, ffn_hist, w_pw1, b_pw1, w_dw, b_dw, ln_conv_g, ln_conv_b,
              w_pw2, b_pw2, w_ffn_gate, w_ffn_up, w_ffn_conv, b_ffn_conv, w_ffn_down,
              ln_mix_g, ln_mix_b, ln_ffn_g, ln_ffn_b):
    ks, vs, kis, convs, ffns = [], [], [], [], []
    for i in range(DEPTH):
        j = i // N_MIXERS
        if i % N_MIXERS == 0:
            m, k, v, ki = attn_fn(j, x)
            ks.append(k)
            vs.append(v)
            kis.append(ki)
        else:
            m, c_state = conformer_conv(x, conv_hist[j], w_pw1[j], b_pw1[j], w_dw[j], b_dw[j],
                                        ln_conv_g[j], ln_conv_b[j], w_pw2[j], b_pw2[j])
            convs.append(c_state)
        x = layer_norm(DN_ALPHA * x + m, ln_mix_g[i], ln_mix_b[i])
        f, g_state = conv_ffn(x, ffn_hist[i], w_ffn_gate[i], w_ffn_up[i], w_ffn_conv[i],
                              b_ffn_conv[i], w_ffn_down[i])
        ffns.append(g_state)
        x = layer_norm(DN_ALPHA * x + f, ln_ffn_g[i], ln_ffn_b[i])
    return x, jnp.stack(ks), jnp.stack(vs), jnp.stack(kis), jnp.stack(convs), jnp.stack(ffns)


def setup_inputs(seed: int = 0) -> dict:
    key = jax.random.key(seed)
    keys = iter(jax.random.split(key, 40))

    def nrm(shape, scale):
        return jax.random.normal(next(keys), shape, jnp.float32) * scale

    n_pages = PAST_LEN // PAGE_SIZE
    n_used = DEC_BATCH * n_pages
    n_pool = n_used + (n_used + 3) // 4
    perm = jax.random.permutation(next(keys), n_pool)
    page_table = perm[:n_used].reshape(DEC_BATCH, n_pages).astype(jnp.int32)

    col_scale = jnp.concatenate([jnp.ones((Q_COLS + KV_COLS,), jnp.float32),
                                 jnp.full((KV_COLS,), DN_BETA, jnp.float32),
                                 jnp.ones((QI_COLS + IDX_DIM + N_IDX_HEADS,), jnp.float32)])
    d = D_MODEL
    return {
        'x_prompt': nrm((BATCH, SEQ, d), 1.0),
        'x_sample': nrm((DEC_BATCH, DEC_SEQ, d), 1.0),
        'cache_k': nrm((N_ATTN, n_pool, PAGE_SIZE, N_KV_HEADS, HEAD_DIM), 1.0),
        'cache_v': nrm((N_ATTN, n_pool, PAGE_SIZE, N_KV_HEADS, HEAD_DIM), DN_BETA),
        'cache_kidx': nrm((N_ATTN, n_pool, PAGE_SIZE, IDX_DIM), 1.0),
        'state_conv': nrm((N_CONV, DEC_BATCH, CONV_WIDTH - 1, d), 0.5),
        'state_ffn': nrm((DEPTH, DEC_BATCH, FFN_CONV_WIDTH - 1, D_FF), 1.0),
        'page_table': page_table,
        'w_attn_in': nrm((N_ATTN, d, IN_COLS), d ** -0.5) * col_scale,
        'w_attn_out': nrm((N_ATTN, Q_COLS, d), DN_BETA * Q_COLS ** -0.5),
        'w_pw1': nrm((N_CONV, d, 2 * d), d ** -0.5),
        'b_pw1': nrm((N_CONV, 2 * d), 0.01),
        'w_dw': nrm((N_CONV, CONV_WIDTH, d), CONV_WIDTH ** -0.5),
        'b_dw': nrm((N_CONV, d), 0.01),
        'ln_conv_g': 1.0 + nrm((N_CONV, d), 0.01),
        'ln_conv_b': nrm((N_CONV, d), 0.01),
        'w_pw2': nrm((N_CONV, d, d), DN_BETA * d ** -0.5),
        'b_pw2': nrm((N_CONV, d), 0.01),
        'w_ffn_gate': nrm((DEPTH, d, D_FF), d ** -0.5),
        'w_ffn_up': nrm((DEPTH, d, D_FF), d ** -0.5),
        'w_ffn_conv': nrm((DEPTH, FFN_CONV_WIDTH, D_FF), FFN_CONV_WIDTH ** -0.5),
        'b_ffn_conv': nrm((DEPTH, D_FF), 0.01),
        'w_ffn_down': nrm((DEPTH, D_FF, d), DN_BETA * D_FF ** -0.5),
        'ln_mix_g': 1.0 + nrm((DEPTH, d), 0.01),
        'ln_mix_b': nrm((DEPTH, d), 0.01),
        'ln_ffn_g': 1.0 + nrm((DEPTH, d), 0.01),
        'ln_ffn_b': nrm((DEPTH, d), 0.01),
    }


def reference(x_prompt, x_sample, cache_k, cache_v, cache_kidx, state_conv, state_ffn, page_table,
              w_attn_in, w_attn_out, w_pw1, b_pw1, w_dw, b_dw, ln_conv_g, ln_conv_b, w_pw2, b_pw2,
              w_ffn_gate, w_ffn_up, w_ffn_conv, b_ffn_conv, w_ffn_down,
              ln_mix_g, ln_mix_b, ln_ffn_g, ln_ffn_b):
    shared = (w_pw1, b_pw1, w_dw, b_dw, ln_conv_g, ln_conv_b, w_pw2, b_pw2,
              w_ffn_gate, w_ffn_up, w_ffn_conv, b_ffn_conv, w_ffn_down,
              ln_mix_g, ln_mix_b, ln_ffn_g, ln_ffn_b)

    bp = x_prompt.shape[0]
    conv_zero = jnp.zeros((N_CONV, bp, CONV_WIDTH - 1, D_MODEL), x_prompt.dtype)
    ffn_zero = jnp.zeros((DEPTH, bp, FFN_CONV_WIDTH - 1, D_FF), x_prompt.dtype)
    y_prompt, k_p, v_p, ki_p, conv_p, ffn_p = run_trunk(
        x_prompt, lambda j, h: attn_layer_prompt(h, w_attn_in[j], w_attn_out[j]),
        conv_zero, ffn_zero, *shared)

    y_sample, k_s, v_s, ki_s, conv_s, ffn_s = run_trunk(
        x_sample, lambda j, h: attn_layer_sample(h, cache_k[j], cache_v[j], cache_kidx[j], page_table,
                                                 w_attn_in[j], w_attn_out[j]),
        state_conv, state_ffn, *shared)

    return (y_prompt, y_sample, k_p, v_p, ki_p, conv_p, ffn_p, k_s, v_s, ki_s, conv_s, ffn_s)
```

```python
import numpy as np
import concourse.bass as bass
import concourse.mybir as mybir
from concourse.bass_utils import run_bass_kernel_spmd

F32 = mybir.dt.float32
BF16 = mybir.dt.bfloat16
I32 = mybir.dt.int32
ALU = mybir.AluOpType
AF = mybir.ActivationFunctionType
AX = mybir.AxisListType

ENGS = ("pe", "act", "dve", "pool", "sp")
EPOCH = 12000
NDMA = 24

D = 2048
NCH = 16
DFF = 5632
NFF = 44
SEQ = 2048
PAST = 16384
NPG = 128
ALPHA = float((2 * 4) ** 0.25)
EPS = 1e-5
NEG = -2048.0
IN_COLS = 4176


class Prog:
    def __init__(self, nc, dry=False):
        self.nc = nc
        self.dry = dry
        self.q = {e: [] for e in ENGS}
        self.cnt = {e: 0 for e in ENGS}
        self.last_w = {}
        self.readers = {}
        self.seen = {e: {} for e in ENGS}
        self.dma_i = 0
        self.dma_val = [0] * NDMA
        self.sems = {}
        self.h = {"pe": nc.tensor, "act": nc.scalar, "dve": nc.vector, "pool": nc.gpsimd, "sp": nc.sync}

    def sem(self, key):
        if key not in self.sems:
            self.sems[key] = self.nc.alloc_semaphore("s_%s_%s" % (key[0], key[1]))
        return self.sems[key]

    def _ev_sem(self, ev):
        return self.sem((ev[0], ev[1])), ev[2]

    def _need(self, eng, ev):
        key = (ev[0], ev[1])
        if self.seen[eng].get(key, 0) >= ev[2]:
            return False
        self.seen[eng][key] = ev[2]
        return True

    def _deps(self, eng, reads, writes):
        evs = []
        for r in reads:
            if r in self.last_w:
                evs.append(self.last_w[r])
        for w in writes:
            if w in self.last_w:
                evs.append(self.last_w[w])
            evs.extend(self.readers.get(w, ()))
        out = []
        for ev in evs:
            if ev[3] == eng and eng == "pe":
                continue
            if self._need(eng, ev):
                out.append(ev)
        return out

    def _record(self, ev, reads, writes):
        for r in reads:
            self.readers.setdefault(r, []).append(ev)
        for w in writes:
            self.last_w[w] = ev
            self.readers[w] = []

    def op(self, eng, fn, reads=(), writes=()):
        if self.dry:
            return
        psr = [r for r in reads if r.startswith("ps")]
        if psr:
            writes = list(writes) + [r for r in psr if r not in writes]
            reads = [r for r in reads if not r.startswith("ps")]
        deps = self._deps(eng, reads, writes)
        c = self.cnt[eng]
        ep, val = c // EPOCH, c % EPOCH + 1
        self.cnt[eng] = c + 1
        ev = (eng, ep, val, eng)
        waits = [self._ev_sem(d) for d in deps]
        mysem = self.sem((eng, ep))

        def run(e):
            for s, v in waits:
                e.wait_ge(s, v)
            ins = fn(e)
            ins.then_inc(mysem, 1)

        run(self.h[eng])
        self._record(ev, reads, writes)

    def dma(self, qeng, fn, reads=(), writes=()):
        if self.dry:
            return
        slot = self.dma_i % NDMA
        self.dma_i += 1
        deps = self._deps(qeng, reads, writes)
        prev = self.dma_val[slot]
        if prev > 0:
            pe = ("dma", slot, prev, "dma")
            if self._need(qeng, pe):
                deps.append(pe)
        val = prev + 16
        self.dma_val[slot] = val
        ev = ("dma", slot, val, "dma")
        waits = [self._ev_sem(d) for d in deps]
        mysem = self.sem(("dma", slot))

        def run(e):
            for s, v in waits:
                e.wait_ge(s, v)
            ins = fn(e)
            ins.then_inc(mysem, 16)

        run(self.h[qeng])
        self._record(ev, reads, writes)

    def barrier(self):
        if self.dry:
            return
        evs = []
        for e in ENGS:
            c = self.cnt[e]
            if c > 0:
                evs.append((e, (c - 1) // EPOCH, (c - 1) % EPOCH + 1, e))
        for s in range(NDMA):
            if self.dma_val[s] > 0:
                evs.append(("dma", s, self.dma_val[s], "dma"))
        for e in ENGS:
            waits = []
            for ev in evs:
                if ev[3] == e:
                    continue
                if self._need(e, ev):
                    waits.append(self._ev_sem(ev))
            if waits:
                def run(h, waits=waits):
                    for s, v in waits:
                        h.wait_ge(s, v)
                run(self.h[e])

    def emit(self):
        pass


def mm(out, lhsT, rhs, st, sp):
    return lambda e: e.matmul(out, lhsT=lhsT, rhs=rhs, start=st, stop=sp)


def seq(fns):
    def run(e):
        ins = None
        for f in fns:
            ins = f(e)
        return ins
    return run


class Arena:
    def __init__(self, nc):
        self.nc = nc
        rem = nc.sbuf_bytes_remaining
        size = (rem - 512) // 64 * 64
        r = nc.bump_sbuf(size)
        self.base = r[0]
        self.end = r[0] + size
        self.cur = self.base
        self.uid = 0

    def alloc(self, name, shape, dt):
        esz = 4 if dt in (F32, I32) else 2
        n = 1
        for s in shape[1:]:
            n *= s
        nbytes = (n * esz + 63) // 64 * 64
        assert self.cur + nbytes <= self.end, "SBUF arena overflow at %s (%d over)" % (name, self.cur + nbytes - self.end)
        self.uid += 1
        t = self.nc.alloc_sbuf_tensor_at("%s_%d" % (name, self.uid), list(shape), dt, offset=self.cur)
        self.cur += nbytes
        return t

    def mark(self):
        return self.cur

    def reset(self, m):
        self.cur = m


def build(cfg):
    if "_wrec" not in cfg and not cfg.get("_dry"):
        recs = build(dict(cfg, _dry=True))
        cfg = dict(cfg, _wrec=recs)
    DRY = bool(cfg.get("_dry"))
    WREC = cfg.get("_wrec")
    npass = cfg.get("npass", 5)
    nlayers = cfg.get("nlayers", 4)
    pass_list = cfg.get("passes", list(range(npass)))
    nc = bass.Bass("TRN2", target_bir_lowering=False)
    P = Prog(nc, dry=DRY)

    def din(name, shape, dt=F32):
        return nc.dram_tensor(name, list(shape), dt, kind="ExternalInput").ap()

    def dout(name, shape):
        return nc.dram_tensor(name, list(shape), F32, kind="ExternalOutput").ap()

    xp = din("xp", [SEQ, D]); xs = din("xs", [4, D])
    NPOOL = cfg.get("npool", 1280)
    ck = din("ck", [2 * NPOOL * 128, 512]); cv = din("cv", [2 * NPOOL * 128, 512]); cki = din("cki", [2 * NPOOL * 4, 2048])
    stc = din("stc", [2, 30, D]); stf = din("stf", [4, 2, DFF]); pt = din("pt", [128], I32)
    w_in = din("w_in", [2, D, IN_COLS]); w_out = din("w_out", [2, D, D]); w_pw1 = din("w_pw1", [2, D, 2 * D])
    w_pw2 = din("w_pw2", [2, D, D]); w_g = din("w_g", [4, D, DFF]); w_u = din("w_u", [4, D, DFF]); w_d = din("w_d", [4, DFF, D])
    WD = {"w_in": w_in, "w_out": w_out, "w_pw1": w_pw1, "w_pw2": w_pw2, "w_g": w_g, "w_u": w_u, "w_d": w_d}
    vecD = din("vecD", [28, D]); vecF = din("vecF", [16, DFF]); wdw = din("wdw", [62, D])
    c_ident = din("c_ident", [128, 128]); c_pq = din("c_pq", [128, 128]); c_pi = din("c_pi", [128, 128])
    c_ropeq = din("c_ropeq", [5, 2, 128, 512]); c_ropei = din("c_ropei", [5, 2, 128, 512])
    c_ttab = din("c_ttab", [SEQ + 128, 48]); c_cmask = din("c_cmask", [128, 128]); c_iota = din("c_iota", [128, 128])

    y_p = dout("y_p", [SEQ, D]); y_s = dout("y_s", [4, D])
    k_p = dout("k_p", [2, SEQ, 512]); v_p = dout("v_p", [2, SEQ, 512]); ki_p = dout("ki_p", [2, SEQ, 64])
    conv_p = dout("conv_p", [2, 30, D]); ffn_p = dout("ffn_p", [4, 2, DFF])
    k_s = dout("k_s", [2, 4, 512]); v_s = dout("v_s", [2, 4, 512]); ki_s = dout("ki_s", [2, 4, 64])
    conv_s = dout("conv_s", [2, 30, D]); ffn_s = dout("ffn_s", [4, 2, DFF])

    scrKT = nc.dram_tensor("scrKT", [2, 128, 4, SEQ], BF16).ap()
    scrV = nc.dram_tensor("scrV", [2, 128, 16, 512], BF16).ap()
    scrKI = nc.dram_tensor("scrKI", [2, 128, SEQ], BF16).ap()

    A = Arena(nc)
    PSN = ["psA", "psB", "psC", "psD", "psE", "psF", "psG", "psH"]
    ps = {n: nc.alloc_psum_tensor(n, [128, 512], F32) for n in PSN}

    identf = A.alloc("identf", [128, 128], F32)
    identb = A.alloc("identb", [128, 128], BF16)
    onesf = A.alloc("onesf", [128, 128], F32)
    onesb = A.alloc("onesb", [128, 128], BF16)
    pqb = A.alloc("pqb", [128, 128], BF16)
    pib = A.alloc("pib", [128, 128], BF16)
    cmask = A.alloc("cmask", [128, 128], F32)
    epsT = A.alloc("epsT", [128, 1], F32)
    vD = A.alloc("vD", [128, NCH, 28], F32)
    vF = A.alloc("vF", [128, NFF, 16], F32)
    wdwT = A.alloc("wdwT", [128, NCH, 62], F32)
    Wt = [A.alloc("W0", [128, 8192], BF16), A.alloc("W1", [128, 8192], BF16)]
    cmark = A.mark()
    xres = A.alloc("xres", [128, NCH, 512], F32)
    xbf = A.alloc("xbf", [128, NCH, 512], BF16)
    rq = A.alloc("rq", [128, 2, 512], F32)
    ri = A.alloc("ri", [128, 2, 512], F32)
    ttq = A.alloc("ttq", [128, 4, 48], F32)
    ghalo = A.alloc("ghalo", [128, 4, NFF, 2], F32)
    ahalo = A.alloc("ahalo", [128, 2, NCH, 30], BF16)
    pmark = A.mark()

    wst = {"ptr": 0, "issued": 0}
    wrecs = []

    def _wissue(i):
        rec = WREC[i]
        sl = i % 2
        wn_ = "W%d" % sl
        if rec[0] == "kiwi":
            wj_ = w_in[rec[1]]
            vkw_ = Wt[sl][:, 0:16 * 256].rearrange("p (k n) -> p k n", k=16)
            srcki = wv(wj_, 0, 16, 4096, 64)
            P.dma("pool", lambda e: e.dma_start(out=vkw_[:, :, 0:64], in_=srcki), writes=[wn_])
            P.dma("pool", lambda e: e.dma_start(out=vkw_[:, :, 64:128], in_=srcki), writes=[wn_])
            P.dma("pool", lambda e: e.dma_start(out=vkw_[:, :, 128:208], in_=wv(wj_, 0, 16, 4096, 80)), writes=[wn_])
        else:
            kind, li, k0, kc, c0, ncol = rec
            view_ = Wt[sl][:, 0:kc * ncol].rearrange("p (k n) -> p k n", k=kc)
            src_ = wv(WD[kind][li], k0, kc, c0, ncol)
            P.dma("pool", lambda e: e.dma_start(out=view_, in_=src_), writes=[wn_])

    def wload(kind, li, k0=0, kc=16, c0=0, ncol=512):
        rec = (kind, li, k0, kc, c0, ncol)
        i = wst["ptr"]
        wst["ptr"] += 1
        wrecs.append(rec)
        sl = i % 2
        if kind == "kiwi":
            view = Wt[sl][:, 0:16 * 256].rearrange("p (k n) -> p k n", k=16)
        else:
            view = Wt[sl][:, 0:kc * ncol].rearrange("p (k n) -> p k n", k=kc)
        if DRY:
            return view, "W%d" % sl
        assert WREC[i] == rec, (i, WREC[i], rec)
        while wst["issued"] < min(i + 2, len(WREC)):
            _wissue(wst["issued"])
            wst["issued"] += 1
        return view, "W%d" % sl

    def wv(w2d, k0, kc, c0, ncol):
        return w2d.rearrange("(k p) n -> p k n", p=128)[:, k0:k0 + kc, c0:c0 + ncol]

    tmpc = A.alloc("tmpc", [128, 128], F32)
    P.dma("sp", lambda e: e.dma_start(out=identf[:, :], in_=c_ident[:, :]), writes=["identf"])
    P.dma("sp", lambda e: e.dma_start(out=cmask[:, :], in_=c_cmask[:, :]), writes=["cmask"])
    P.op("dve", lambda e: e.tensor_copy(out=identb[:, :], in_=identf[:, :]), reads=["identf"], writes=["identb"])
    P.op("dve", lambda e: e.memset(onesf[:, :], 1.0), writes=["onesf"])
    P.op("dve", lambda e: e.memset(onesb[:, :], 1.0), writes=["onesb"])
    P.op("dve", lambda e: e.memset(epsT[:, :], EPS), writes=["epsT"])
    P.dma("sp", lambda e: e.dma_start(out=tmpc[:, :], in_=c_pq[:, :]), writes=["tmpc"])
    P.op("dve", lambda e: e.tensor_copy(out=pqb[:, :], in_=tmpc[:, :]), reads=["tmpc"], writes=["pqb"])
    P.dma("sp", lambda e: e.dma_start(out=tmpc[:, :], in_=c_pi[:, :]), reads=["tmpc"], writes=["tmpc"])
    P.op("dve", lambda e: e.tensor_copy(out=pib[:, :], in_=tmpc[:, :]), reads=["tmpc"], writes=["pib"])

    def rows_to_fm(rows_dram, R, L, dst, stage):
        P.dma("sp", lambda e: e.dma_start(out=stage[0:R, 0:L], in_=rows_dram), writes=["stage"])
        nchunk = L // 128
        per = 512 // R
        c = 0
        bi = 0
        while c < nchunk:
            n = min(per, nchunk - c)
            bank = ["psA", "psB"][bi % 2]
            bi += 1
            fns = []
            for i in range(n):
                fns.append(lambda e, i=i, c=c: e.transpose(out=ps[bank][:, i * R:(i + 1) * R],
                                                          in_=stage[0:R, (c + i) * 128:(c + i + 1) * 128],
                                                          identity=identf[0:R, 0:R]))
            P.op("pe", seq(fns), reads=["stage", "identf"], writes=[bank])
            P.op("dve", lambda e, c=c, n=n, bank=bank: e.tensor_copy(
                out=dst[:, c:c + n, 0:R], in_=ps[bank][:, 0:n * R].rearrange("p (i r) -> p i r", r=R)),
                reads=[bank], writes=[dst_name(dst)])
            c += n

    names = {}

    def dst_name(t):
        return names[id(t)]

    m0 = A.mark()
    stage = A.alloc("stage", [64, DFF], F32)
    names[id(vD)] = "vD"; names[id(vF)] = "vF"; names[id(wdwT)] = "wdwT"
    rows_to_fm(vecD[:, :], 28, D, vD, stage)
    rows_to_fm(vecF[:, :], 16, DFF, vF, stage)
    rows_to_fm(wdw[:, :], 62, D, wdwT, stage)
    P.barrier()
    A.reset(m0)

    def vcol(row, c):
        return vD[:, c, row:row + 1]

    PS_ROT = {"i": 0}

    def load_x(p, N):
        m = A.mark()
        xtm = A.alloc("xtm", [128, D], F32)
        ntt = (N + 127) // 128
        for tt in range(ntt):
            nt = min(128, N - tt * 128)
            src = xp[p * 512 + tt * 128: p * 512 + tt * 128 + nt, :] if p < 4 else xs[0:nt, :]
            P.dma("sp", lambda e, src=src, nt=nt: e.dma_start(out=xtm[0:nt, :], in_=src), writes=["xtm"])
            for c4 in range(4):
                bank = ["psA", "psB"][c4 % 2]
                fns = [lambda e, i=i, c4=c4, nt=nt: e.transpose(out=ps[bank][:, i * 128:i * 128 + nt],
                                                                 in_=xtm[0:nt, (c4 * 4 + i) * 128:(c4 * 4 + i + 1) * 128],
                                                                 identity=identf[0:nt, 0:nt]) for i in range(4)]
                P.op("pe", seq(fns), reads=["xtm", "identf"], writes=[bank])
                srcv = ps[bank][:, :].rearrange("p (i t) -> p i t", i=4)[:, :, 0:nt]
                P.op("dve", lambda e, c4=c4, tt=tt, nt=nt, srcv=srcv: e.tensor_copy(
                    out=xres[:, c4 * 4:c4 * 4 + 4, tt * 128:tt * 128 + nt], in_=srcv), reads=[bank], writes=["xres"])
                P.op("act", lambda e, c4=c4, tt=tt, nt=nt, srcv=srcv: e.copy(
                    out=xbf[:, c4 * 4:c4 * 4 + 4, tt * 128:tt * 128 + nt], in_=srcv), reads=[bank], writes=["xbf"])
        P.barrier()
        A.reset(m)

    def store_y(p, N):
        m = A.mark()
        ytm = A.alloc("ytm", [128, D], F32)
        ntt = (N + 127) // 128
        for tt in range(ntt):
            nt = min(128, N - tt * 128)
            for c4 in range(4):
                bank = ["psA", "psB"][c4 % 2]
                fns = [lambda e, i=i, c4=c4, nt=nt, tt=tt: e.transpose(out=ps[bank][0:nt, i * 128:(i + 1) * 128],
                                                                        in_=xres[:, c4 * 4 + i, tt * 128:tt * 128 + nt],
                                                                        identity=identf[:, :]) for i in range(4)]
                P.op("pe", seq(fns), reads=["xres", "identf"], writes=[bank])
                P.op("dve", lambda e, c4=c4, nt=nt: e.tensor_copy(out=ytm[0:nt, c4 * 512:(c4 + 1) * 512], in_=ps[bank][0:nt, :]),
                     reads=[bank], writes=["ytm"])
            dst = y_p[p * 512 + tt * 128: p * 512 + tt * 128 + nt, :] if p < 4 else y_s[0:nt, :]
            P.dma("sp", lambda e, dst=dst, nt=nt: e.dma_start(out=dst, in_=ytm[0:nt, :]), reads=["ytm"])
        P.barrier()
        A.reset(m)

    class LN:
        def __init__(self, N, src, resname):
            self.N = N
            self.src = src
            self.res = resname
            self.sq = [A.alloc("lnsq0", [128, 512], F32), A.alloc("lnsq1", [128, 512], F32)]
            self.mean = A.alloc("lnmean", [128, 512], F32)
            self.rstd = A.alloc("lnrstd", [128, 512], F32)
            self.nmr = A.alloc("lnnmr", [128, 512], F32)

        def stats(self, c):
            N = self.N
            sq = self.sq[c % 2]
            sqn = "lnsq%d" % (c % 2)
            z = self.src[:, c, 0:N]
            P.op("act", lambda e: e.activation(out=sq[:, 0:N], in_=z, func=AF.Square), reads=[self.res], writes=[sqn])
            P.op("pe", seq([mm(ps["psG"][:, 0:N], onesf[:, :], z, c == 0, c == NCH - 1),
                            mm(ps["psH"][:, 0:N], onesf[:, :], sq[:, 0:N], c == 0, c == NCH - 1)]),
                 reads=[self.res, sqn, "onesf"], writes=["psG", "psH"])

        def finalize(self):
            N = self.N
            mean, rstd, nmr = self.mean, self.rstd, self.nmr
            P.op("dve", lambda e: e.tensor_scalar(out=mean[:, 0:N], in0=ps["psG"][:, 0:N], scalar1=1.0 / D, scalar2=None, op0=ALU.mult),
                 reads=["psG"], writes=["lnmean"])
            P.op("dve", lambda e: e.tensor_tensor(out=nmr[:, 0:N], in0=mean[:, 0:N], in1=mean[:, 0:N], op=ALU.mult),
                 reads=["lnmean"], writes=["lnnmr"])
            P.op("dve", lambda e: e.scalar_tensor_tensor(out=rstd[:, 0:N], in0=ps["psH"][:, 0:N], scalar=1.0 / D, in1=nmr[:, 0:N],
                                                         op0=ALU.mult, op1=ALU.subtract), reads=["psH", "lnnmr"], writes=["lnrstd"])
            P.op("act", lambda e: e.activation(out=rstd[:, 0:N], in_=rstd[:, 0:N], func=AF.Sqrt, bias=epsT[:, 0:1], scale=1.0),
                 reads=["lnrstd", "epsT"], writes=["lnrstd"])
            P.op("dve", lambda e: e.reciprocal(out=rstd[:, 0:N], in_=rstd[:, 0:N]), reads=["lnrstd"], writes=["lnrstd"])
            P.op("dve", lambda e: e.tensor_tensor(out=nmr[:, 0:N], in0=mean[:, 0:N], in1=rstd[:, 0:N], op=ALU.mult),
                 reads=["lnmean", "lnrstd"], writes=["lnnmr"])

        def apply(self, c, g_ap, b_ap, out_f32=None, out_f32_name=None, out_bf=None, out_bf_name=None, func=None):
            N = self.N
            z = self.src[:, c, 0:N]
            t = self.sq[c % 2]
            tn = "lnsq%d" % (c % 2)
            P.op("dve", lambda e: e.tensor_tensor(out=t[:, 0:N], in0=z, in1=self.rstd[:, 0:N], op=ALU.mult),
                 reads=[self.res, "lnrstd"], writes=[tn])
            P.op("dve", lambda e: e.tensor_tensor(out=t[:, 0:N], in0=t[:, 0:N], in1=self.nmr[:, 0:N], op=ALU.subtract),
                 reads=[tn, "lnnmr"], writes=[tn])
            if out_f32 is not None:
                P.op("act", lambda e: e.activation(out=out_f32, in_=t[:, 0:N], func=AF.Identity, bias=b_ap, scale=g_ap),
                     reads=[tn, "vD"], writes=[out_f32_name])
                if out_bf is not None:
                    P.op("pool", lambda e: e.tensor_copy(out=out_bf, in_=out_f32), reads=[out_f32_name], writes=[out_bf_name])
            else:
                P.op("act", lambda e: e.activation(out=out_bf, in_=t[:, 0:N], func=func, bias=b_ap, scale=g_ap),
                     reads=[tn, "vD"], writes=[out_bf_name])

    def out_proj_ln(N, wkind, wli, rhs_t, rhs_name, nk, g_row, b_row, bias_row=None):
        m = A.mark()
        ln = LN(N, xres, "xres")
        btmp = A.alloc("btmp", [128, 512], F32)
        for g4 in range(4):
            k0 = 0
            banks = ["psA", "psB", "psC", "psD"]
            while k0 < nk:
                kc = min(16, nk - k0)
                view, wn = wload(wkind, wli, k0, kc, g4 * 512, 512)
                for i in range(4):
                    fns = [mm(ps[banks[i]][:, 0:N], view[:, k, i * 128:(i + 1) * 128], rhs_t[:, k0 + k, 0:N],
                              (k0 + k) == 0, (k0 + k) == nk - 1) for k in range(kc)]
                    P.op("pe", seq(fns), reads=[wn, rhs_name], writes=[banks[i]])
                k0 += kc
            for i in range(4):
                c = g4 * 4 + i
                bank = banks[i]
                if bias_row is not None:
                    P.op("act", lambda e, bank=bank, c=c: e.activation(out=btmp[:, 0:N], in_=ps[bank][:, 0:N], func=AF.Identity,
                                                                       bias=vcol(bias_row, c), scale=1.0),
                         reads=[bank, "vD"], writes=["btmp"])
                    P.op("dve", lambda e, c=c: e.scalar_tensor_tensor(out=xres[:, c, 0:N], in0=xres[:, c, 0:N], scalar=ALPHA,
                                                                      in1=btmp[:, 0:N], op0=ALU.mult, op1=ALU.add),
                         reads=["btmp", "xres"], writes=["xres"])
                else:
                    P.op("dve", lambda e, bank=bank, c=c: e.scalar_tensor_tensor(out=xres[:, c, 0:N], in0=xres[:, c, 0:N], scalar=ALPHA,
                                                                                 in1=ps[bank][:, 0:N], op0=ALU.mult, op1=ALU.add),
                         reads=[bank, "xres"], writes=["xres"])
                ln.stats(c)
        ln.finalize()
        for c in range(NCH):
            ln.apply(c, vcol(g_row, c), vcol(b_row, c), out_f32=xres[:, c, 0:N], out_f32_name="xres",
                     out_bf=xbf[:, c, 0:N], out_bf_name="xbf")
        P.barrier()
        A.reset(m)

    def ffn_layer(l, p, N):
        m = A.mark()
        hT = A.alloc("hT", [128, NFF, 512], BF16)
        gsb = [A.alloc("gsb0", [128, 516], F32), A.alloc("gsb1", [128, 516], F32)]
        cc = [A.alloc("cc%d" % i, [128, 512], F32) for i in range(4)]
        for ft in range(11):
            vg, gn = wload("w_g", l, 0, 16, ft * 512, 512)
            for i in range(4):
                f = ft * 4 + i
                bg = ["psA", "psC"][i % 2]
                P.op("pe", seq([mm(ps[bg][:, 0:N], vg[:, k, i * 128:(i + 1) * 128], xbf[:, k, 0:N], k == 0, k == 15) for k in range(16)]),
                     reads=[gn, "xbf"], writes=[bg])
                g = gsb[f % 2]
                gnm = "gsb%d" % (f % 2)
                c_ = cc[i]
                cn = "cc%d" % i
                P.op("dve", lambda e, g=g, f=f: e.tensor_copy(out=g[:, 0:2], in_=ghalo[:, l, f, :]), reads=["ghalo"], writes=[gnm])
                P.op("act", lambda e, g=g, bg=bg: e.copy(out=g[:, 2:2 + N], in_=ps[bg][:, 0:N]), reads=[bg], writes=[gnm])
                P.op("dve", lambda e, g=g, f=f: e.tensor_copy(out=ghalo[:, l, f, :], in_=g[:, N:N + 2]), reads=[gnm], writes=["ghalo"])
                P.op("dve", lambda e, g=g, c_=c_, f=f: e.tensor_scalar(out=c_[:, 0:N], in0=g[:, 2:2 + N], scalar1=vF[:, f, l * 4 + 2:l * 4 + 3],
                                                                      scalar2=vF[:, f, l * 4 + 3:l * 4 + 4], op0=ALU.mult, op1=ALU.add),
                     reads=[gnm, "vF"], writes=[cn])
                P.op("dve", lambda e, g=g, c_=c_, f=f: e.scalar_tensor_tensor(out=c_[:, 0:N], in0=g[:, 1:1 + N], scalar=vF[:, f, l * 4 + 1:l * 4 + 2],
                                                                             in1=c_[:, 0:N], op0=ALU.mult, op1=ALU.add),
                     reads=[gnm, "vF", cn], writes=[cn])
                P.op("dve", lambda e, g=g, c_=c_, f=f: e.scalar_tensor_tensor(out=c_[:, 0:N], in0=g[:, 0:N], scalar=vF[:, f, l * 4:l * 4 + 1],
                                                                             in1=c_[:, 0:N], op0=ALU.mult, op1=ALU.add),
                     reads=[gnm, "vF", cn], writes=[cn])
                P.op("act", lambda e, c_=c_: e.activation(out=c_[:, 0:N], in_=c_[:, 0:N], func=AF.Silu), reads=[cn], writes=[cn])
            vu, un = wload("w_u", l, 0, 16, ft * 512, 512)
            for i in range(4):
                f = ft * 4 + i
                bu = ["psB", "psD"][i % 2]
                c_ = cc[i]
                cn = "cc%d" % i
                P.op("pe", seq([mm(ps[bu][:, 0:N], vu[:, k, i * 128:(i + 1) * 128], xbf[:, k, 0:N], k == 0, k == 15) for k in range(16)]),
                     reads=[un, "xbf"], writes=[bu])
                P.op("dve", lambda e, c_=c_, f=f, bu=bu: e.tensor_tensor(out=hT[:, f, 0:N], in0=c_[:, 0:N], in1=ps[bu][:, 0:N], op=ALU.mult),
                     reads=[cn, bu], writes=["hT"])
        if p >= 3:
            dstt = ffn_p[l] if p == 3 else ffn_s[l]
            for t in range(2):
                dv = dstt[t, :].rearrange("(f q) -> q f", q=128)
                for q4 in range(4):
                    P.dma("sp", lambda e, q4=q4, dv=dv, t=t: e.dma_start(out=dv[:, q4 * 11:(q4 + 1) * 11], in_=ghalo[:, l, q4 * 11:(q4 + 1) * 11, t],
                                                                         allow_slow_non_contiguous=True), reads=["ghalo"])
        out_proj_ln(N, "w_d", l, hT, "hT", NFF, 16 + l, 20 + l)
        A.reset(m)

    def conv_layer(j, i_layer, p, N):
        m = A.mark()
        aTb = A.alloc("aTb", [128, NCH, 544], BF16)
        cv_ = A.alloc("cv", [128, NCH, 512], F32)
        sig = [A.alloc("sig%d" % i, [128, 512], F32) for i in range(4)]
        dgc = [A.alloc("dgc0", [128, 31, 128], BF16), A.alloc("dgc1", [128, 31, 128], BF16)]
        alast = A.alloc("alast", [128, NCH, 32], F32)
        atm = A.alloc("atm", [32, D], F32)
        P.op("pool", lambda e: e.tensor_copy(out=aTb[:, :, 0:30], in_=ahalo[:, j, :, :]), reads=["ahalo"], writes=["aTb"])
        nl = min(30, N)
        for g4 in range(4):
            vg, gn = wload("w_pw1", j, 0, 16, D + g4 * 512, 512)
            for i in range(4):
                c = g4 * 4 + i
                bg = ["psB", "psD"][i % 2]
                P.op("pe", seq([mm(ps[bg][:, 0:N], vg[:, k, i * 128:(i + 1) * 128], xbf[:, k, 0:N], k == 0, k == 15) for k in range(16)]),
                     reads=[gn, "xbf"], writes=[bg])
                s_ = sig[i]
                sn = "sig%d" % i
                P.op("act", lambda e, s_=s_, bg=bg, c=c: e.activation(out=s_[:, 0:N], in_=ps[bg][:, 0:N], func=AF.Sigmoid,
                                                                      bias=vcol(25 + 2 * j, c), scale=1.0), reads=[bg, "vD"], writes=[sn])
            va, an = wload("w_pw1", j, 0, 16, g4 * 512, 512)
            for i in range(4):
                c = g4 * 4 + i
                ba = ["psA", "psC"][i % 2]
                s_ = sig[i]
                sn = "sig%d" % i
                P.op("pe", seq([mm(ps[ba][:, 0:N], va[:, k, i * 128:(i + 1) * 128], xbf[:, k, 0:N], k == 0, k == 15) for k in range(16)]),
                     reads=[an, "xbf"], writes=[ba])
                P.op("dve", lambda e, s_=s_, ba=ba, c=c: e.scalar_tensor_tensor(out=aTb[:, c, 30:30 + N], in0=ps[ba][:, 0:N], scalar=vcol(24 + 2 * j, c),
                                                                                in1=s_[:, 0:N], op0=ALU.add, op1=ALU.mult),
                     reads=[ba, sn, "vD"], writes=["aTb"])
                if p >= 3:
                    P.op("dve", lambda e, s_=s_, ba=ba, c=c: e.scalar_tensor_tensor(out=alast[:, c, 0:nl], in0=ps[ba][:, N - nl:N], scalar=vcol(24 + 2 * j, c),
                                                                                    in1=s_[:, N - nl:N], op0=ALU.add, op1=ALU.mult),
                         reads=[ba, sn, "vD"], writes=["alast"])
        P.op("pool", lambda e: e.tensor_copy(out=ahalo[:, j, :, :], in_=aTb[:, :, N:N + 30]), reads=["aTb"], writes=["ahalo"])
        if p >= 3:
            for c4 in range(4):
                bank = ["psE", "psF"][c4 % 2]
                fns = [lambda e, ii=ii, c4=c4: e.transpose(out=ps[bank][0:nl, ii * 128:(ii + 1) * 128], in_=alast[:, c4 * 4 + ii, 0:nl],
                                                           identity=identf[:, :]) for ii in range(4)]
                P.op("pe", seq(fns), reads=["alast", "identf"], writes=[bank])
                P.op("act", lambda e, c4=c4, bank=bank: e.copy(out=atm[0:nl, c4 * 512:(c4 + 1) * 512], in_=ps[bank][0:nl, :]),
                     reads=[bank], writes=["atm"])
            if p == 3:
                P.dma("sp", lambda e: e.dma_start(out=conv_p[j, :, :], in_=atm[0:30, :]), reads=["atm"])
            else:
                P.dma("sp", lambda e: e.dma_start(out=conv_s[j, 26:30, :], in_=atm[0:4, :]), reads=["atm"])
                P.dma("sp", lambda e: e.dma_start(out=conv_s[j, 0:26, :], in_=stc[j, 4:30, :]))
        ln = LN(N, cv_, "cv")
        for c in range(NCH):
            dg = dgc[c % 2]
            dn = "dgc%d" % (c % 2)
            P.op("dve", lambda e, dg=dg, c=c: e.tensor_tensor(out=dg[:, :, :], in0=identf[:, :].unsqueeze(1).broadcast_to([128, 31, 128]),
                                                              in1=wdwT[:, c, j * 31:j * 31 + 31].unsqueeze(2).broadcast_to([128, 31, 128]), op=ALU.mult),
                 reads=["identf", "wdwT"], writes=[dn])
            bank = ["psA", "psB"][c % 2]
            P.op("pe", seq([mm(ps[bank][:, 0:N], dg[:, w, :], aTb[:, c, w:w + N], w == 0, w == 30) for w in range(31)]),
                 reads=[dn, "aTb"], writes=[bank])
            P.op("act", lambda e, c=c, bank=bank: e.activation(out=cv_[:, c, 0:N], in_=ps[bank][:, 0:N], func=AF.Identity,
                                                               bias=vcol(0 + j, c), scale=1.0), reads=[bank, "vD"], writes=["cv"])
            ln.stats(c)
        ln.finalize()
        for c in range(NCH):
            ln.apply(c, vcol(2 + j, c), vcol(4 + j, c), out_bf=xbf[:, c, 0:N], out_bf_name="xbf", func=AF.Silu)
        P.barrier()
        A.reset(m)
        out_proj_ln(N, "w_pw2", j, xbf, "xbf", 16, 8 + i_layer, 12 + i_layer, bias_row=6 + j)

    def rope_fm(bank, N, tab, pmat, dst_ap, dst_name, rawb, sw_bank, t1, t2, rows=128, tabC=None, tabS=None):
        C = tabC if tabC is not None else tab[0:rows, 0, 0:N]
        S = tabS if tabS is not None else tab[0:rows, 1, 0:N]
        P.op("act", lambda e: e.copy(out=rawb[0:rows, 0:N], in_=ps[bank][0:rows, 0:N]), reads=[bank], writes=["rawb"])
        P.op("pe", mm(ps[sw_bank][0:rows, 0:N], pmat[0:rows, 0:rows], rawb[0:rows, 0:N], True, True), reads=["rawb", "pqb", "pib"], writes=[sw_bank])
        P.op("dve", lambda e: e.tensor_tensor(out=t1[0:rows, 0:N], in0=ps[bank][0:rows, 0:N], in1=C, op=ALU.mult), reads=[bank, "rq", "ri"], writes=["t1"])
        P.op("dve", lambda e: e.tensor_tensor(out=t2[0:rows, 0:N], in0=ps[sw_bank][0:rows, 0:N], in1=S, op=ALU.mult), reads=[sw_bank, "rq", "ri"], writes=["t2"])
        P.op("pool", lambda e: e.tensor_tensor(out=dst_ap, in0=t1[0:rows, 0:N], in1=t2[0:rows, 0:N], op=ALU.add), reads=["t1", "t2"], writes=[dst_name])

    def rope_tm(x3, nt, tt, nh, half, coff, resname):
        cosb = ttq[0:nt, tt, coff:coff + half].unsqueeze(1).broadcast_to([nt, nh, half])
        sinb = ttq[0:nt, tt, coff + half:coff + 2 * half].unsqueeze(1).broadcast_to([nt, nh, half])
        x1 = x3[:, :, 0:half]
        x2 = x3[:, :, half:2 * half]
        rtmp = RT["rtmp"]
        ta, tb, tc_, td = (rtmp[0:nt, k, 0:nh * half].rearrange("p (h d) -> p h d", h=nh) for k in range(4))
        P.op("dve", lambda e: e.tensor_tensor(out=ta, in0=x1, in1=cosb, op=ALU.mult), reads=[resname, "ttq"], writes=["rtmp"])
        P.op("dve", lambda e: e.tensor_tensor(out=tb, in0=x2, in1=sinb, op=ALU.mult), reads=[resname, "ttq"], writes=["rtmp"])
        P.op("dve", lambda e: e.tensor_tensor(out=tc_, in0=x2, in1=cosb, op=ALU.mult), reads=[resname, "ttq"], writes=["rtmp"])
        P.op("dve", lambda e: e.tensor_tensor(out=td, in0=x1, in1=sinb, op=ALU.mult), reads=[resname, "ttq"], writes=["rtmp"])
        P.op("dve", lambda e: e.tensor_tensor(out=x1, in0=ta, in1=tb, op=ALU.subtract), reads=["rtmp"], writes=[resname])
        P.op("dve", lambda e: e.tensor_tensor(out=x2, in0=tc_, in1=td, op=ALU.add), reads=["rtmp"], writes=[resname])

    RT = {}

    def bisect(scores, sname, nq, nk, theta, work):
        lo, wd, mid, cntt, sel, junk, cnt4 = work["lo"], work["wd"], work["mid"], work["cnt"], work["sel"], work["junk"], work["cnt4"]
        jw = work["jw"]
        chunks = []
        c0 = 0
        while c0 < nk:
            w = min(jw, nk - c0)
            chunks.append((c0, w))
            c0 += w
        assert len(chunks) <= 8
        P.op("dve", lambda e: e.tensor_reduce(out=mid[0:nq, :], in_=scores[0:nq, 0:nk], axis=AX.X, op=ALU.max), reads=[sname], writes=["bs_mid"])
        P.op("dve", lambda e: e.memset(lo[0:nq, :], NEG - 1.0), writes=["bs_lo"])
        P.op("dve", lambda e: e.tensor_scalar(out=wd[0:nq, :], in0=mid[0:nq, :], scalar1=-(NEG - 1.0), scalar2=None, op0=ALU.add), reads=["bs_mid"], writes=["bs_wd"])
        nch = len(chunks)
        for it in range(23):
            hf = 0.5 ** (it + 1)
            P.op("dve", lambda e, hf=hf: e.tensor_scalar(out=mid[0:nq, :], in0=wd[0:nq, :], scalar1=hf, scalar2=lo[0:nq, 0:1], op0=ALU.mult, op1=ALU.add),
                 reads=["bs_lo", "bs_wd"], writes=["bs_mid"])
            P.op("dve", lambda e: e.memset(cnt4[0:nq, 0:nch], 0.0), writes=["bs_cnt4"])
            for k, (c0, w) in enumerate(chunks):
                P.op("dve", lambda e, k=k, c0=c0, w=w: e.tensor_scalar(out=junk[0:nq, 0:w], in0=scores[0:nq, c0:c0 + w], scalar1=mid[0:nq, 0:1], scalar2=0.0,
                                                                      op0=ALU.is_gt, op1=ALU.add, accum_out=cnt4[0:nq, k:k + 1]),
                     reads=[sname, "bs_mid", "bs_cnt4"], writes=["bs_junk", "bs_cnt4"])
            if nch > 1:
                P.op("dve", lambda e: e.tensor_reduce(out=cntt[0:nq, :], in_=cnt4[0:nq, 0:nch], axis=AX.X, op=ALU.add), reads=["bs_cnt4"], writes=["bs_cnt"])
                csrc, csn = cntt, "bs_cnt"
            else:
                csrc, csn = cnt4, "bs_cnt4"
            P.op("dve", lambda e, csrc=csrc: e.scalar_tensor_tensor(out=sel[0:nq, :], in0=csrc[0:nq, 0:1], scalar=255.5, in1=wd[0:nq, :], op0=ALU.is_gt, op1=ALU.mult),
                 reads=[csn, "bs_wd"], writes=["bs_sel"])
            P.op("dve", lambda e, hf=hf: e.scalar_tensor_tensor(out=lo[0:nq, :], in0=sel[0:nq, :], scalar=hf, in1=lo[0:nq, :], op0=ALU.mult, op1=ALU.add),
                 reads=["bs_lo", "bs_sel"], writes=["bs_lo"])
        P.op("dve", lambda e: e.tensor_copy(out=theta[0:nq, :], in_=lo[0:nq, :]), reads=["bs_lo"], writes=["theta"])

    def indexer_tile(nq, qi_lhs, ki_rhs, ncols, wq_ap, dgt, Rb, sc_out, sc_name, kin_name, diag_mask_cols=None):
        banks = ["psA", "psB"]
        def dmm(h):
            P.op("pe", mm(ps[banks[h % 2]][0:nq, 0:ncols], qi_lhs(h), ki_rhs(h), True, True), reads=["qiT", kin_name], writes=[banks[h % 2]])
        dmm(0)
        for h in range(16):
            if h + 1 < 16:
                dmm(h + 1)
            R = Rb[h % 2]
            rn = "Rb%d" % (h % 2)
            P.op("act", lambda e, R=R, h=h: e.activation(out=R[0:nq, 0:ncols], in_=ps[banks[h % 2]][0:nq, 0:ncols], func=AF.Relu),
                 reads=[banks[h % 2]], writes=[rn])
            P.op("pe", mm(ps["psC"][0:nq, 0:ncols], dgt[0:nq, h, 0:nq], R[0:nq, 0:ncols], h == 0, h == 15), reads=[rn, "dgt"], writes=["psC"])
        if diag_mask_cols is None:
            P.op("act", lambda e: e.copy(out=sc_out, in_=ps["psC"][0:nq, 0:ncols]), reads=["psC"], writes=[sc_name])
        else:
            c0 = diag_mask_cols
            if c0 > 0:
                P.op("act", lambda e: e.copy(out=sc_out[:, 0:c0], in_=ps["psC"][0:nq, 0:c0]), reads=["psC"], writes=[sc_name])
            P.op("dve", lambda e: e.tensor_tensor(out=sc_out[:, c0:ncols], in0=ps["psC"][0:nq, c0:ncols], in1=cmask[0:nq, 0:ncols - c0], op=ALU.add),
                 reads=["psC", "cmask"], writes=[sc_name])

    def attn_layer(j, i_layer, p, N):
        m = A.mark()
        sample = (p == 4)
        wj = w_in[j]
        qT = A.alloc("qT", [128, 16, 512 if not sample else 4], BF16)
        KT = A.alloc("KT", [128, 4, SEQ if not sample else 4], BF16)
        Vb = A.alloc("Vb", [128, 16 if not sample else 1, 512], BF16)
        kiT = A.alloc("kiT", [128, SEQ if not sample else 4], BF16)
        qiT = A.alloc("qiT", [128, 8, 512] if not sample else [64, 16, 4], BF16)
        wsb = A.alloc("wsb", [128, 4, 16], F32)
        rawb = A.alloc("rawb", [128, 512], BF16)
        t1 = A.alloc("t1", [128, 512], F32)
        t2 = A.alloc("t2", [128, 512], F32)
        RT["rtmp"] = A.alloc("rtmp", [128, 4, 64], F32)
        ksb = A.alloc("ksb", [128, 512], F32)
        vsb = A.alloc("vsb", [128, 512], F32)
        kisb = A.alloc("kisb", [128, 80], F32)
        kbase = p * 512 if not sample else 0
        if not sample and p > 0:
            P.dma("sp", lambda e: e.dma_start(out=KT[:, :, 0:kbase], in_=scrKT[j, :, :, 0:kbase]), writes=["KT"])
            P.dma("sp", lambda e: e.dma_start(out=Vb[:, 0:4 * p, :], in_=scrV[j, :, 0:4 * p, :]), writes=["Vb"])
            P.dma("sp", lambda e: e.dma_start(out=kiT[:, 0:kbase], in_=scrKI[j, :, 0:kbase]), writes=["kiT"])
        for g4 in range(4):
            view, wn = wload("w_in", j, 0, 16, g4 * 512, 512)
            for i in range(4):
                bank = ["psA", "psB"][i % 2]
                P.op("pe", seq([mm(ps[bank][:, 0:N], view[:, k, i * 128:(i + 1) * 128], xbf[:, k, 0:N], k == 0, k == 15) for k in range(16)]),
                     reads=[wn, "xbf"], writes=[bank])
                rope_fm(bank, N, rq, pqb, qT[:, g4 * 4 + i, 0:N], "qT", rawb, ["psC", "psD"][i % 2], t1, t2)
        vk, kn = wload("w_in", j, 0, 16, 2048, 512)
        for i in range(4):
            bank = ["psA", "psB"][i % 2]
            P.op("pe", seq([mm(ps[bank][:, 0:N], vk[:, k, i * 128:(i + 1) * 128], xbf[:, k, 0:N], k == 0, k == 15) for k in range(16)]),
                 reads=[kn, "xbf"], writes=[bank])
            rope_fm(bank, N, rq, pqb, KT[:, i, kbase:kbase + N], "KT", rawb, ["psC", "psD"][i % 2], t1, t2)
        ntt = (N + 127) // 128
        kdst = k_p if not sample else k_s
        vdst = v_p if not sample else v_s
        for tt in range(ntt):
            nt = min(128, N - tt * 128)
            r0 = kbase + tt * 128
            P.op("pe", seq([mm(ps["psE"][0:nt, :], xbf[:, k, tt * 128:tt * 128 + nt], vk[:, k, :], k == 0, k == 15) for k in range(16)]),
                 reads=[kn, "xbf"], writes=["psE"])
            P.op("act", lambda e, nt=nt: e.copy(out=ksb[0:nt, :], in_=ps["psE"][0:nt, :]), reads=["psE"], writes=["ksb"])
            rope_tm(ksb[0:nt, :].rearrange("p (h d) -> p h d", h=4), nt, tt, 4, 16, 0, "ksb")
            P.dma("sp", lambda e, nt=nt, r0=r0: e.dma_start(out=kdst[j, r0:r0 + nt, :], in_=ksb[0:nt, :]), reads=["ksb"])
        vv, vn = wload("w_in", j, 0, 16, 2560, 512)
        for tt in range(ntt):
            nt = min(128, N - tt * 128)
            r0 = kbase + tt * 128
            P.op("pe", seq([mm(ps["psF"][0:nt, :], xbf[:, k, tt * 128:tt * 128 + nt], vv[:, k, :], k == 0, k == 15) for k in range(16)]),
                 reads=[vn, "xbf"], writes=["psF"])
            P.op("act", lambda e, nt=nt: e.copy(out=vsb[0:nt, :], in_=ps["psF"][0:nt, :]), reads=["psF"], writes=["vsb"])
            P.op("pool", lambda e, nt=nt, tt=tt: e.tensor_copy(out=Vb[0:nt, (kbase // 128 + tt) if not sample else 0, :], in_=vsb[0:nt, :]),
                 reads=["vsb"], writes=["Vb"])
            P.dma("sp", lambda e, nt=nt, r0=r0: e.dma_start(out=vdst[j, r0:r0 + nt, :], in_=vsb[0:nt, :]), reads=["vsb"])
        if not sample:
            for g2 in range(2):
                view, wn = wload("w_in", j, 0, 16, 3072 + g2 * 512, 512)
                for i in range(4):
                    bank = ["psA", "psB"][i % 2]
                    P.op("pe", seq([mm(ps[bank][:, 0:N], view[:, k, i * 128:(i + 1) * 128], xbf[:, k, 0:N], k == 0, k == 15) for k in range(16)]),
                         reads=[wn, "xbf"], writes=[bank])
                    rope_fm(bank, N, ri, pib, qiT[:, g2 * 4 + i, 0:N], "qiT", rawb, ["psC", "psD"][i % 2], t1, t2)
        else:
            for g2 in range(2):
                view, wn = wload("w_in", j, 0, 16, 3072 + g2 * 512, 512)
                fns = []
                for hh in range(8):
                    for k in range(16):
                        fns.append(mm(ps["psA"][0:64, (g2 * 8 + hh) * 4:(g2 * 8 + hh) * 4 + 4], view[:, k, hh * 64:(hh + 1) * 64], xbf[:, k, 0:4], k == 0, k == 15))
                P.op("pe", seq(fns), reads=[wn, "xbf"], writes=["psA"])
            Cb = ri[0:64, 0, 0:4].unsqueeze(1).broadcast_to([64, 16, 4])
            Sb = ri[0:64, 1, 0:4].unsqueeze(1).broadcast_to([64, 16, 4])
            P.op("act", lambda e: e.copy(out=rawb[0:64, 0:64], in_=ps["psA"][0:64, 0:64]), reads=["psA"], writes=["rawb"])
            P.op("pe", mm(ps["psC"][0:64, 0:64], pib[0:64, 0:64], rawb[0:64, 0:64], True, True), reads=["rawb", "pib"], writes=["psC"])
            P.op("dve", lambda e: e.tensor_tensor(out=t1[0:64, 0:64].rearrange("p (h t) -> p h t", h=16),
                                                  in0=ps["psA"][0:64, 0:64].rearrange("p (h t) -> p h t", h=16), in1=Cb, op=ALU.mult),
                 reads=["psA", "ri"], writes=["t1"])
            P.op("dve", lambda e: e.tensor_tensor(out=t2[0:64, 0:64].rearrange("p (h t) -> p h t", h=16),
                                                  in0=ps["psC"][0:64, 0:64].rearrange("p (h t) -> p h t", h=16), in1=Sb, op=ALU.mult),
                 reads=["psC", "ri"], writes=["t2"])
            P.op("pool", lambda e: e.tensor_tensor(out=qiT[0:64, :, :].rearrange("p h t -> p (h t)"), in0=t1[0:64, 0:64], in1=t2[0:64, 0:64], op=ALU.add),
                 reads=["t1", "t2"], writes=["qiT"])
        vkw, wn = wload("kiwi", j)
        P.op("pe", seq([mm(ps["psA"][:, 0:N], vkw[:, k, 0:128], xbf[:, k, 0:N], k == 0, k == 15) for k in range(16)]), reads=[wn, "xbf"], writes=["psA"])
        rope_fm("psA", N, ri, pib, kiT[:, kbase:kbase + N], "kiT", rawb, "psC", t1, t2)
        kidst = ki_p if not sample else ki_s
        for tt in range(ntt):
            nt = min(128, N - tt * 128)
            r0 = kbase + tt * 128
            P.op("pe", seq([mm(ps["psE"][0:nt, 0:80], xbf[:, k, tt * 128:tt * 128 + nt], vkw[:, k, 128:208], k == 0, k == 15) for k in range(16)]),
                 reads=[wn, "xbf"], writes=["psE"])
            P.op("act", lambda e, nt=nt: e.copy(out=kisb[0:nt, :], in_=ps["psE"][0:nt, 0:80]), reads=["psE"], writes=["kisb"])
            P.op("pool", lambda e, nt=nt, tt=tt: e.tensor_copy(out=wsb[0:nt, tt, :], in_=kisb[0:nt, 64:80]), reads=["kisb"], writes=["wsb"])
            rope_tm(kisb[0:nt, 0:64].rearrange("p (h d) -> p h d", h=1), nt, tt, 1, 8, 32, "kisb")
            P.dma("sp", lambda e, nt=nt, r0=r0: e.dma_start(out=kidst[j, r0:r0 + nt, :], in_=kisb[0:nt, 0:64]), reads=["kisb"])
        if not sample and p < 3:
            P.dma("sp", lambda e: e.dma_start(out=scrKT[j, :, :, kbase:kbase + N], in_=KT[:, :, kbase:kbase + N]), reads=["KT"])
            P.dma("sp", lambda e: e.dma_start(out=scrV[j, :, 4 * p:4 * p + 4, :], in_=Vb[:, 4 * p:4 * p + 4, :]), reads=["Vb"])
            P.dma("sp", lambda e: e.dma_start(out=scrKI[j, :, kbase:kbase + N], in_=kiT[:, kbase:kbase + N]), reads=["kiT"])

        ckpt('proj')
        dgt = A.alloc("dgt", [128, 16, 128], BF16)
        Rb = [A.alloc("Rb0", [128, 512], BF16), A.alloc("Rb1", [128, 512], BF16)]
        Eb = [A.alloc("Eb0", [128, 512], BF16), A.alloc("Eb1", [128, 512], BF16)]
        Pb = [A.alloc("Pb0", [128, 512], BF16), A.alloc("Pb1", [128, 512], BF16)]
        rden = A.alloc("rden", [128, 512], F32)
        theta = A.alloc("theta", [128, 1], F32)
        work = {k: A.alloc("bs_" + k, [128, 1], F32) for k in ("lo", "wd", "mid", "cnt", "sel")}
        work["cnt4"] = A.alloc("bs_cnt4", [128, 8], F32)
        scale = 128.0 ** -0.5
        if not sample:
            scores = A.alloc("scores", [128, SEQ], F32)
            work["junk"] = A.alloc("bs_junk", [128, SEQ], BF16)
            work["jw"] = SEQ
            maskf = A.alloc("maskf", [128, SEQ], BF16)
            maskT = A.alloc("maskT", [128, 16, 128], BF16)
            for tt in range(4):
                qt = p * 4 + tt
                nk = (qt + 1) * 128
                P.op("dve", lambda e, tt=tt: e.tensor_tensor(out=dgt[:, :, :], in0=identf[:, :].unsqueeze(1).broadcast_to([128, 16, 128]),
                                                             in1=wsb[:, tt, :].unsqueeze(2).broadcast_to([128, 16, 128]), op=ALU.mult),
                     reads=["identf", "wsb"], writes=["dgt"])
                k0 = 0
                while k0 < nk:
                    ncols = min(512, nk - k0)
                    last = (k0 + ncols == nk)
                    indexer_tile(128, lambda h, tt=tt: qiT[(h % 2) * 64:(h % 2) * 64 + 64, h // 2, tt * 128:(tt + 1) * 128],
                                 lambda h, k0=k0, ncols=ncols: kiT[(h % 2) * 64:(h % 2) * 64 + 64, k0:k0 + ncols],
                                 ncols, None, dgt, Rb, scores[:, k0:k0 + ncols], "scores", "kiT",
                                 diag_mask_cols=(ncols - 128) if last else None)
                    k0 += ncols
                if qt >= 2:
                    bisect(scores, "scores", 128, nk, theta, work)
                else:
                    P.op("dve", lambda e: e.memset(theta[:, :], NEG * 0.5), writes=["theta"])
                P.op("dve", lambda e, nk=nk: e.tensor_scalar(out=maskf[:, 0:nk], in0=scores[:, 0:nk], scalar1=theta[:, 0:1], scalar2=None, op0=ALU.is_gt),
                     reads=["scores", "theta"], writes=["maskf"])
                nk8 = qt + 1
                mtb = ps["psD"][:, :].bitcast(BF16)
                for b4 in range((nk8 + 3) // 4):
                    n4 = min(4, nk8 - b4 * 4)
                    fns = [lambda e, i=i, b4=b4: e.transpose(out=mtb[:, i * 128:(i + 1) * 128], in_=maskf[:, (b4 * 4 + i) * 128:(b4 * 4 + i + 1) * 128],
                                                             identity=identb[:, :]) for i in range(n4)]
                    P.op("pe", seq(fns), reads=["maskf", "identb"], writes=["psD"])
                    P.op("act", lambda e, b4=b4, n4=n4: e.copy(out=maskT[:, b4 * 4:b4 * 4 + n4, :], in_=mtb[:, 0:n4 * 128].rearrange("p (i q) -> p i q", i=n4)),
                         reads=["psD"], writes=["maskT"])
                for n in range(4):
                    for k8 in range(nk8):
                        sb = ["psE", "psF"][k8 % 2]
                        P.op("pe", mm(ps[sb][:, :], KT[:, n, k8 * 128:(k8 + 1) * 128], qT[:, 4 * n:4 * n + 4, tt * 128:(tt + 1) * 128], True, True),
                             reads=["KT", "qT"], writes=[sb])
                        E = Eb[k8 % 2]
                        en = "Eb%d" % (k8 % 2)
                        Pt = Pb[k8 % 2]
                        pn = "Pb%d" % (k8 % 2)
                        P.op("act", lambda e, E=E, sb=sb: e.activation(out=E[:, :], in_=ps[sb][:, :], func=AF.Exp, scale=scale), reads=[sb], writes=[en])
                        P.op("dve", lambda e, E=E, Pt=Pt, k8=k8: e.tensor_tensor(out=Pt[:, :].rearrange("p (g q) -> p g q", g=4),
                                                                                in0=E[:, :].rearrange("p (g q) -> p g q", g=4),
                                                                                in1=maskT[:, k8, :].unsqueeze(1).broadcast_to([128, 4, 128]), op=ALU.mult),
                             reads=[en, "maskT"], writes=[pn])
                        P.op("pe", seq([mm(ps["psG"][:, :], Vb[:, k8, n * 128:(n + 1) * 128], Pt[:, :], k8 == 0, k8 == nk8 - 1),
                                        mm(ps["psH"][:, :], onesb[:, :], Pt[:, :], k8 == 0, k8 == nk8 - 1)]),
                             reads=[pn, "Vb", "onesb"], writes=["psG", "psH"])
                    P.op("dve", lambda e: e.reciprocal(out=rden[:, :], in_=ps["psH"][:, :]), reads=["psH"], writes=["rden"])
                    P.op("dve", lambda e, n=n, tt=tt: e.tensor_tensor(out=xbf[:, 4 * n:4 * n + 4, tt * 128:(tt + 1) * 128],
                                                                      in0=ps["psG"][:, :].rearrange("p (g q) -> p g q", g=4),
                                                                      in1=rden[:, :].rearrange("p (g q) -> p g q", g=4), op=ALU.mult),
                         reads=["psG", "rden"], writes=["xbf"])
        else:
            sample_attn(j, qT, KT, Vb, kiT, qiT, wsb, dgt, Rb, Eb, Pb, rden, theta, work, scale)
        P.barrier()
        ckpt('attn')
        A.reset(m)
        out_proj_ln(N, "w_out", j, xbf, "xbf", 16, 8 + i_layer, 12 + i_layer)
        ckpt('attn_out')

    def sample_attn(j, qT, KT, Vb, kiT, qiT, wsb, dgt, Rb, Eb, Pb, rden, theta, work, scale):
        NKS = PAST + 4
        scores = A.alloc("scores", [4, NKS + 12], F32)
        work["junk"] = A.alloc("bs_junk", [4, 4100], BF16)
        work["jw"] = 4100
        maskc = A.alloc("maskc", [4, 512], BF16)
        maskT = A.alloc("maskT", [128, 129, 4], BF16)
        G = A.alloc("G", [128, 32, 64], F32)
        kiq = A.alloc("kiq", [64, 4096], BF16)
        ptc = A.alloc("ptc", [128, 1], I32)
        kpg = [A.alloc("kpg0", [128, 512], F32), A.alloc("kpg1", [128, 512], F32)]
        vpg = [A.alloc("vpg0", [128, 512], F32), A.alloc("vpg1", [128, 512], F32)]
        ktb = A.alloc("ktb", [128, 4, 128], BF16)
        vpb = A.alloc("vpb", [128, 512], BF16)
        oacc = A.alloc("oacc", [128, 128], F32)
        P.dma("sp", lambda e: e.dma_start(out=ptc[:, :], in_=pt.unsqueeze(1)), writes=["ptc"])
        A4 = A.alloc("A4", [4, 16, 4], BF16)
        Wsel = A.alloc("Wsel", [64, 4], BF16)
        P.op("dve", lambda e: e.tensor_tensor(out=A4[:, :, :], in0=identf[0:4, 0:4].unsqueeze(1).broadcast_to([4, 16, 4]),
                                              in1=wsb[0:4, 0, :].unsqueeze(2).broadcast_to([4, 16, 4]), op=ALU.mult),
             reads=["identf", "wsb"], writes=["A4"])
        mtb0 = ps["psD"][:, :].bitcast(BF16)
        P.op("pe", lambda e: e.transpose(out=mtb0[0:64, 0:4], in_=A4[:, :, :].rearrange("p h t -> p (h t)"), identity=identb[0:4, 0:4]),
             reads=["A4", "identb"], writes=["psD"])
        P.op("act", lambda e: e.copy(out=Wsel[:, :], in_=mtb0[0:64, 0:4]), reads=["psD"], writes=["Wsel"])
        qi64 = qiT[0:64, :, :].rearrange("p h t -> p (h t)")

        def packed_tile(ki_rhs, kin_name, ncols, sc_out, bi, mask_new=False):
            bd = ["psA", "psB"][bi % 2]
            R = Rb[bi % 2]
            rn = "Rb%d" % (bi % 2)
            P.op("pe", mm(ps[bd][0:64, 0:ncols], qi64, ki_rhs, True, True), reads=["qiT", kin_name], writes=[bd])
            P.op("act", lambda e: e.activation(out=R[0:64, 0:ncols], in_=ps[bd][0:64, 0:ncols], func=AF.Relu), reads=[bd], writes=[rn])
            P.op("pe", mm(ps["psC"][0:4, 0:ncols], Wsel[:, :], R[0:64, 0:ncols], True, True), reads=[rn, "Wsel"], writes=["psC"])
            if not mask_new:
                P.op("act", lambda e: e.copy(out=sc_out, in_=ps["psC"][0:4, 0:ncols]), reads=["psC"], writes=["scores"])
            else:
                P.op("dve", lambda e: e.tensor_tensor(out=sc_out, in0=ps["psC"][0:4, 0:ncols], in1=cmask[0:4, 0:ncols], op=ALU.add),
                     reads=["psC", "cmask"], writes=["scores"])

        ptf = A.alloc("ptf", [128, 1], F32)
        idxqf = A.alloc("idxqf", [128, 4], F32)
        idxq = A.alloc("idxq", [128, 4], I32)
        P.op("dve", lambda e: e.tensor_copy(out=ptf[:, :], in_=ptc[:, :]), reads=["ptc"], writes=["ptf"])
        for q in range(4):
            P.op("dve", lambda e, q=q: e.tensor_scalar(out=idxqf[:, q:q + 1], in0=ptf[:, :], scalar1=4.0, scalar2=float(j * NPOOL * 4 + q),
                                                       op0=ALU.mult, op1=ALU.add), reads=["ptf"], writes=["idxqf"])
        P.op("dve", lambda e: e.tensor_copy(out=idxq[:, :], in_=idxqf[:, :]), reads=["idxqf"], writes=["idxq"])
        for rq4 in range(4):
            P.dma("pool", lambda e, rq4=rq4: e.indirect_dma_start(out=G[:, :, :].rearrange("p r d -> p (r d)"), out_offset=None,
                                                                   in_=cki[:, :],
                                                                   in_offset=bass.IndirectOffsetOnAxis(ap=idxq[:, rq4:rq4 + 1], axis=0)),
                  reads=["idxq"], writes=["G"])
            for r4 in range(8):
                bank = ["psE", "psF"][r4 % 2]
                fns = [lambda e, i=i, r4=r4: e.transpose(out=ps[bank][0:64, i * 128:(i + 1) * 128], in_=G[:, r4 * 4 + i, :], identity=identf[:, :]) for i in range(4)]
                P.op("pe", seq(fns), reads=["G", "identf"], writes=[bank])
                P.op("dve", lambda e, r4=r4, bank=bank: e.tensor_copy(out=kiq[0:64, r4 * 512:(r4 + 1) * 512], in_=ps[bank][0:64, :]), reads=[bank], writes=["kiq"])
            for kt in range(8):
                packed_tile(kiq[0:64, kt * 512:(kt + 1) * 512], "kiq", 512,
                            scores[0:4, rq4 * 4096 + kt * 512: rq4 * 4096 + (kt + 1) * 512], kt)
        packed_tile(kiT[0:64, 0:4], "kiT", 4, scores[0:4, PAST:PAST + 4], 0, mask_new=True)
        bisect(scores, "scores", 4, NKS, theta, work)
        mtb = ps["psD"][:, :].bitcast(BF16)
        for kt5 in range(33):
            ncols = 512 if kt5 < 32 else 4
            c0 = kt5 * 512
            P.op("dve", lambda e, c0=c0, ncols=ncols: e.tensor_scalar(out=maskc[0:4, 0:ncols], in0=scores[0:4, c0:c0 + ncols], scalar1=theta[0:4, 0:1],
                                                                      scalar2=None, op0=ALU.is_gt), reads=["scores", "theta"], writes=["maskc"])
            nsub = (ncols + 127) // 128
            fns = [lambda e, i=i, ncols=ncols: e.transpose(out=mtb[0:min(128, ncols - i * 128), i * 4:i * 4 + 4], in_=maskc[0:4, i * 128:min((i + 1) * 128, ncols)],
                                                           identity=identb[0:4, 0:4]) for i in range(nsub)]
            P.op("pe", seq(fns), reads=["maskc", "identb"], writes=["psD"])
            rows = 128 if kt5 < 32 else 4
            P.op("act", lambda e, kt5=kt5, nsub=nsub, rows=rows: e.copy(out=maskT[0:rows, kt5 * 4:kt5 * 4 + nsub, :],
                                                                        in_=mtb[0:rows, 0:nsub * 4].rearrange("p (i q) -> p i q", i=nsub)),
                 reads=["psD"], writes=["maskT"])
        P.op("dve", lambda e: e.memset(oacc[:, :], 0.0), writes=["oacc"])
        idxT = A.alloc("idxT", [128, 128], I32)
        idxTf = A.alloc("idxTf", [128, 128], F32)
        rowi = A.alloc("rowi", [128, 128], F32)
        P.dma("sp", lambda e: e.dma_start(out=rowi[:, :], in_=c_iota[:, :]), writes=["rowi"])
        P.op("dve", lambda e: e.tensor_scalar(out=idxTf[:, 0:1], in0=ptf[:, :], scalar1=128.0, scalar2=float(j * NPOOL * 128), op0=ALU.mult, op1=ALU.add),
             reads=["ptf"], writes=["idxTf"])
        P.op("dve", lambda e: e.tensor_scalar(out=rowi[:, :], in0=rowi[:, :], scalar1=idxTf[:, 0:1], scalar2=None, op0=ALU.add), reads=["rowi", "idxTf"], writes=["rowi"])
        P.op("dve", lambda e: e.tensor_copy(out=idxT[:, :], in_=rowi[:, :]), reads=["rowi"], writes=["idxT"])
        ckj = ck
        cvj = cv
        for r in range(129):
            newt = (r == 128)
            rows = 128 if not newt else 4
            kp = kpg[r % 2]; kpn = "kpg%d" % (r % 2)
            vp = vpg[r % 2]; vpn = "vpg%d" % (r % 2)
            if not newt:
                P.dma("pool", lambda e, r=r, kp=kp: e.indirect_dma_start(out=kp[:, :], out_offset=None, in_=ckj[:, :],
                                                                          in_offset=bass.IndirectOffsetOnAxis(ap=idxT[:, r:r + 1], axis=0)),
                      reads=["idxT"], writes=[kpn])
                P.dma("pool", lambda e, r=r, vp=vp: e.indirect_dma_start(out=vp[:, :], out_offset=None, in_=cvj[:, :],
                                                                          in_offset=bass.IndirectOffsetOnAxis(ap=idxT[:, r:r + 1], axis=0)),
                      reads=["idxT"], writes=[vpn])
                fns = [lambda e, n=n, kp=kp: e.transpose(out=ps["psE"][:, n * 128:(n + 1) * 128], in_=kp[:, n * 128:(n + 1) * 128], identity=identf[:, :]) for n in range(4)]
                P.op("pe", seq(fns), reads=[kpn, "identf"], writes=["psE"])
                P.op("act", lambda e: e.copy(out=ktb[:, :, :].rearrange("p n k -> p (n k)"), in_=ps["psE"][:, :]), reads=["psE"], writes=["ktb"])
                P.op("pool", lambda e, vp=vp: e.tensor_copy(out=vpb[:, :], in_=vp[:, :]), reads=[vpn], writes=["vpb"])
                mt_idx = (r // 32) * 32 + (r % 32)
                KTn = lambda n: ktb[:, n, :]
                Vn = lambda n: vpb[:, n * 128:(n + 1) * 128]
                vname, kname = "vpb", "ktb"
            else:
                mt_idx = 128
                KTn = lambda n: KT[:, n, 0:4]
                Vn = lambda n: Vb[0:4, 0, n * 128:(n + 1) * 128]
                vname, kname = "Vb", "KT"
            fns = [mm(ps["psF"][0:rows, n * 16:(n + 1) * 16], KTn(n), qT[:, 4 * n:4 * n + 4, 0:4], True, True) for n in range(4)]
            P.op("pe", seq(fns), reads=[kname, "qT"], writes=["psF"])
            E = Eb[r % 2]; en = "Eb%d" % (r % 2)
            Pt = Pb[r % 2]; pn = "Pb%d" % (r % 2)
            P.op("act", lambda e, E=E, rows=rows: e.activation(out=E[0:rows, 0:64], in_=ps["psF"][0:rows, 0:64], func=AF.Exp, scale=scale), reads=["psF"], writes=[en])
            P.op("dve", lambda e, E=E, Pt=Pt, rows=rows, mt_idx=mt_idx: e.tensor_tensor(out=Pt[0:rows, 0:64].rearrange("p (a q) -> p a q", q=4),
                                                                                       in0=E[0:rows, 0:64].rearrange("p (a q) -> p a q", q=4),
                                                                                       in1=maskT[0:rows, mt_idx, :].unsqueeze(1).broadcast_to([rows, 16, 4]), op=ALU.mult),
                 reads=[en, "maskT"], writes=[pn])
            fns = [mm(ps["psG"][:, n * 16:(n + 1) * 16], Vn(n), Pt[0:rows, n * 16:(n + 1) * 16], True, True) for n in range(4)]
            fns.append(mm(ps["psG"][:, 64:128], onesb[0:rows, :], Pt[0:rows, 0:64], True, True))
            P.op("pe", seq(fns), reads=[pn, vname, "onesb"], writes=["psG"])
            P.op("dve", lambda e: e.tensor_tensor(out=oacc[:, :], in0=oacc[:, :], in1=ps["psG"][:, 0:128], op=ALU.add), reads=["psG", "oacc"], writes=["oacc"])
        P.op("dve", lambda e: e.reciprocal(out=rden[:, 0:64], in_=oacc[:, 64:128]), reads=["oacc"], writes=["rden"])
        P.op("dve", lambda e: e.tensor_tensor(out=xbf[:, :, 0:4], in0=oacc[:, 0:64].rearrange("p (h q) -> p h q", q=4),
                                              in1=rden[:, 0:64].rearrange("p (h q) -> p h q", q=4), op=ALU.mult), reads=["oacc", "rden"], writes=["xbf"])

    class StopBuild(Exception):
        pass

    CUR = {}

    def ckpt(name):
        if cfg.get("stop") == name:
            P.barrier()
            A.reset(CUR["mark"])
            store_y(CUR["p"], CUR["N"])
            raise StopBuild()

    try:
      for p in pass_list:
          sample = (p == 4)
          N = 4 if sample else 512
          if sample:
              P.barrier()
              A.reset(cmark)
              xres = A.alloc("xres", [128, NCH, 4], F32)
              xbf = A.alloc("xbf", [128, NCH, 4], BF16)
              rq = A.alloc("rq", [128, 2, 512], F32)
              ri = A.alloc("ri", [128, 2, 512], F32)
              ttq = A.alloc("ttq", [128, 4, 48], F32)
              ghalo = A.alloc("ghalo", [128, 4, NFF, 2], F32)
              ahalo = A.alloc("ahalo", [128, 2, NCH, 30], BF16)
          P.dma("sp", lambda e, p=p: e.dma_start(out=rq[:, :, :], in_=c_ropeq[p].rearrange("c p n -> p c n")), writes=["rq"])
          P.dma("sp", lambda e, p=p: e.dma_start(out=ri[:, :, :], in_=c_ropei[p].rearrange("c p n -> p c n")), writes=["ri"])
          r0 = p * 512 if not sample else SEQ
          if not sample:
              P.dma("sp", lambda e, r0=r0: e.dma_start(out=ttq[:, :, :], in_=c_ttab[r0:r0 + 512, :].rearrange("(t p) c -> p t c", p=128)), writes=["ttq"])
          else:
              P.dma("sp", lambda e: e.dma_start(out=ttq[:, 0, :], in_=c_ttab[SEQ:SEQ + 128, :]), writes=["ttq"])
          if p == 0:
              P.op("pool", lambda e: e.memset(ghalo[:, :, :, :].rearrange("p a b c -> p (a b c)"), 0.0), writes=["ghalo"])
              P.op("pool", lambda e: e.memset(ahalo[:, :, :, :].rearrange("p a b c -> p (a b c)"), 0.0), writes=["ahalo"])
          if sample:
              for l in range(4):
                  for t in range(2):
                      sv = stf[l, t, :].rearrange("(f q) -> q f", q=128)
                      for q4 in range(4):
                          P.dma("sp", lambda e, l=l, q4=q4, sv=sv, t=t: e.dma_start(out=ghalo[:, l, q4 * 11:(q4 + 1) * 11, t], in_=sv[:, q4 * 11:(q4 + 1) * 11],
                                                                                    allow_slow_non_contiguous=True), writes=["ghalo"])
              m = A.mark()
              stg = A.alloc("stage", [64, DFF], F32)
              atmp = A.alloc("atmp", [128, NCH, 30], F32)
              names[id(atmp)] = "atmp"
              for j in range(2):
                  rows_to_fm(stc[j, :, :], 30, D, atmp, stg)
                  P.op("dve", lambda e, j=j: e.tensor_copy(out=ahalo[:, j, :, :], in_=atmp[:, :, :]), reads=["atmp"], writes=["ahalo"])
              P.barrier()
              A.reset(m)
          CUR.update(p=p, N=N, mark=A.mark())
          load_x(p, N)
          ckpt('load_x')
          for i_layer in range(nlayers):
              j = i_layer // 2
              if i_layer % 2 == 0:
                  attn_layer(j, i_layer, p, N)
              else:
                  conv_layer(j, i_layer, p, N)
              ffn_layer(i_layer, p, N)
          store_y(p, N)
    except StopBuild:
        pass
    if DRY:
        return wrecs
    P.barrier()
    P.emit()
    return nc


def _consts():
    theta = np.float32(500000.0)
    invq = (theta ** (-np.arange(0, 32, 2, dtype=np.float32) / np.float32(32))).astype(np.float32)
    invi = (theta ** (-np.arange(0, 16, 2, dtype=np.float32) / np.float32(16))).astype(np.float32)
    pos_all = np.concatenate([np.arange(SEQ, dtype=np.float32), PAST + np.arange(128, dtype=np.float32)])
    angq = pos_all[:, None] * invq[None, :]
    angi = pos_all[:, None] * invi[None, :]
    ttab = np.concatenate([np.cos(angq), np.sin(angq), np.cos(angi), np.sin(angi)], axis=1).astype(np.float32)
    ropeq = np.zeros((5, 2, 128, 512), np.float32)
    ropei = np.zeros((5, 2, 128, 512), np.float32)
    ropeq[:, 0] = 1.0
    ropei[:, 0] = 1.0
    for p in range(5):
        rows = np.arange(p * 512, p * 512 + 512) if p < 4 else np.concatenate([np.arange(SEQ, SEQ + 128)] * 4)
        cq, sq = np.cos(angq[rows]).T, np.sin(angq[rows]).T
        ci, si = np.cos(angi[rows]).T, np.sin(angi[rows]).T
        ropeq[p, 0, 0:16] = cq; ropeq[p, 0, 16:32] = cq
        ropeq[p, 1, 0:16] = sq; ropeq[p, 1, 16:32] = sq
        for o in (0, 64):
            ropei[p, 0, o:o + 8] = ci; ropei[p, 0, o + 8:o + 16] = ci
            ropei[p, 1, o:o + 8] = si; ropei[p, 1, o + 8:o + 16] = si
    pq = np.zeros((128, 128), np.float32)
    for m in range(16):
        pq[m + 16, m] = -1.0
        pq[m, m + 16] = 1.0
    pi = np.zeros((128, 128), np.float32)
    for o in (0, 64):
        for m in range(8):
            pi[o + m + 8, o + m] = -1.0
            pi[o + m, o + m + 8] = 1.0
    cm = np.where(np.arange(128)[None, :] <= np.arange(128)[:, None], 0.0, NEG).astype(np.float32)
    return dict(c_ident=np.eye(128, dtype=np.float32), c_pq=pq, c_pi=pi, c_ropeq=ropeq, c_ropei=ropei,
                c_ttab=np.ascontiguousarray(ttab[:SEQ + 128]), c_cmask=cm,
                c_iota=np.ascontiguousarray(np.broadcast_to(np.arange(128, dtype=np.float32)[None, :], (128, 128))))


def make_in_maps(inp, ncores=8):
    f = lambda a: np.ascontiguousarray(np.asarray(a, dtype=np.float32))
    cst = _consts()
    vecD = np.concatenate([f(inp["b_dw"]), f(inp["ln_conv_g"]), f(inp["ln_conv_b"]), f(inp["b_pw2"]),
                           f(inp["ln_mix_g"]), f(inp["ln_mix_b"]), f(inp["ln_ffn_g"]), f(inp["ln_ffn_b"]),
                           f(inp["b_pw1"]).reshape(4, D)], axis=0)
    vecF = np.concatenate([np.concatenate([f(inp["w_ffn_conv"])[l], f(inp["b_ffn_conv"])[l][None]], axis=0) for l in range(4)], axis=0)
    shared = dict(
        ck=f(inp["cache_k"]).reshape(-1, 512), cv=f(inp["cache_v"]).reshape(-1, 512),
        cki=f(inp["cache_kidx"]).reshape(-1, 2048),
        w_in=f(inp["w_attn_in"]), w_out=f(inp["w_attn_out"]), w_pw1=f(inp["w_pw1"]), w_pw2=f(inp["w_pw2"]),
        w_g=f(inp["w_ffn_gate"]), w_u=f(inp["w_ffn_up"]), w_d=f(inp["w_ffn_down"]),
        vecD=np.ascontiguousarray(vecD), vecF=np.ascontiguousarray(vecF), wdw=f(inp["w_dw"]).reshape(62, D), **cst)
    maps = []
    for c in range(ncores):
        m = dict(shared)
        m["xp"] = f(inp["x_prompt"])[c % 4]
        m["xs"] = f(inp["x_sample"])[c]
        m["stc"] = np.ascontiguousarray(f(inp["state_conv"])[:, c])
        m["stf"] = np.ascontiguousarray(f(inp["state_ffn"])[:, c])
        m["pt"] = np.ascontiguousarray(np.asarray(inp["page_table"], dtype=np.int32)[c])
        maps.append(m)
    return maps


def assemble(res):
    r = res
    y_p = np.stack([r[c]["y_p"] for c in range(4)])
    y_s = np.stack([r[c]["y_s"] for c in range(8)])
    k_p = np.stack([r[c]["k_p"] for c in range(4)], axis=1).reshape(2, 4, SEQ, 4, 128)
    v_p = np.stack([r[c]["v_p"] for c in range(4)], axis=1).reshape(2, 4, SEQ, 4, 128)
    ki_p = np.stack([r[c]["ki_p"] for c in range(4)], axis=1)
    conv_p = np.stack([r[c]["conv_p"] for c in range(4)], axis=1)
    ffn_p = np.stack([r[c]["ffn_p"] for c in range(4)], axis=1)
    k_s = np.stack([r[c]["k_s"] for c in range(8)], axis=1).reshape(2, 8, 4, 4, 128)
    v_s = np.stack([r[c]["v_s"] for c in range(8)], axis=1).reshape(2, 8, 4, 4, 128)
    ki_s = np.stack([r[c]["ki_s"] for c in range(8)], axis=1)
    conv_s = np.stack([r[c]["conv_s"] for c in range(8)], axis=1)
    ffn_s = np.stack([r[c]["ffn_s"] for c in range(8)], axis=1)
    outs = (y_p, y_s, k_p, v_p, ki_p, conv_p, ffn_p, k_s, v_s, ki_s, conv_s, ffn_s)
    return tuple(np.ascontiguousarray(o, dtype=np.float32) for o in outs)


def kernel(**inputs):
    nc = build({})
    maps = make_in_maps(inputs)
    res = run_bass_kernel_spmd(nc, maps, core_ids=list(range(8)))
    return assemble(res.results)
```

```python
import numpy as np
import concourse.bass as bass
import concourse.mybir as mybir
from concourse.bass_utils import run_bass_kernel_spmd

F32 = mybir.dt.float32
BF16 = mybir.dt.bfloat16
I32 = mybir.dt.int32
ALU = mybir.AluOpType
AF = mybir.ActivationFunctionType
AX = mybir.AxisListType

ENGS = ("pe", "act", "dve", "pool", "sp")
EPOCH = 12000
NDMA = 24

D = 2048
NCH = 16
DFF = 5632
NFF = 44
SEQ = 2048
PAST = 16384
NPG = 128
ALPHA = float((2 * 4) ** 0.25)
EPS = 1e-5
NEG = -2048.0
IN_COLS = 4176


class Prog:
    def __init__(self, nc, dry=False):
        self.nc = nc
        self.dry = dry
        self.q = {e: [] for e in ENGS}
        self.cnt = {e: 0 for e in ENGS}
        self.last_w = {}
        self.readers = {}
        self.seen = {e: {} for e in ENGS}
        self.dma_i = 0
        self.dma_val = [0] * NDMA
        self.sems = {}
        self.h = {"pe": nc.tensor, "act": nc.scalar, "dve": nc.vector, "pool": nc.gpsimd, "sp": nc.sync}

    def sem(self, key):
        if key not in self.sems:
            self.sems[key] = self.nc.alloc_semaphore("s_%s_%s" % (key[0], key[1]))
        return self.sems[key]

    def _ev_sem(self, ev):
        return self.sem((ev[0], ev[1])), ev[2]

    def _need(self, eng, ev):
        key = (ev[0], ev[1])
        if self.seen[eng].get(key, 0) >= ev[2]:
            return False
        self.seen[eng][key] = ev[2]
        return True

    def _deps(self, eng, reads, writes):
        evs = []
        for r in reads:
            if r in self.last_w:
                evs.append(self.last_w[r])
        for w in writes:
            if w in self.last_w:
                evs.append(self.last_w[w])
            evs.extend(self.readers.get(w, ()))
        out = []
        for ev in evs:
            if ev[3] == eng and eng == "pe":
                continue
            if self._need(eng, ev):
                out.append(ev)
        return out

    def _record(self, ev, reads, writes):
        for r in reads:
            self.readers.setdefault(r, []).append(ev)
        for w in writes:
            self.last_w[w] = ev
            self.readers[w] = []

    def op(self, eng, fn, reads=(), writes=()):
        if self.dry:
            return
        psr = [r for r in reads if r.startswith("ps")]
        if psr:
            writes = list(writes) + [r for r in psr if r not in writes]
            reads = [r for r in reads if not r.startswith("ps")]
        deps = self._deps(eng, reads, writes)
        c = self.cnt[eng]
        ep, val = c // EPOCH, c % EPOCH + 1
        self.cnt[eng] = c + 1
        ev = (eng, ep, val, eng)
        waits = [self._ev_sem(d) for d in deps]
        mysem = self.sem((eng, ep))

        def run(e):
            for s, v in waits:
                e.wait_ge(s, v)
            ins = fn(e)
            ins.then_inc(mysem, 1)

        run(self.h[eng])
        self._record(ev, reads, writes)

    def dma(self, qeng, fn, reads=(), writes=()):
        if self.dry:
            return
        slot = self.dma_i % NDMA
        self.dma_i += 1
        deps = self._deps(qeng, reads, writes)
        prev = self.dma_val[slot]
        if prev > 0:
            pe = ("dma", slot, prev, "dma")
            if self._need(qeng, pe):
                deps.append(pe)
        val = prev + 16
        self.dma_val[slot] = val
        ev = ("dma", slot, val, "dma")
        waits = [self._ev_sem(d) for d in deps]
        mysem = self.sem(("dma", slot))

        def run(e):
            for s, v in waits:
                e.wait_ge(s, v)
            ins = fn(e)
            ins.then_inc(mysem, 16)

        run(self.h[qeng])
        self._record(ev, reads, writes)

    def barrier(self):
        if self.dry:
            return
        evs = []
        for e in ENGS:
            c = self.cnt[e]
            if c > 0:
                evs.append((e, (c - 1) // EPOCH, (c - 1) % EPOCH + 1, e))
        for s in range(NDMA):
            if self.dma_val[s] > 0:
                evs.append(("dma", s, self.dma_val[s], "dma"))
        for e in ENGS:
            waits = []
            for ev in evs:
                if ev[3] == e:
                    continue
                if self._need(e, ev):
                    waits.append(self._ev_sem(ev))
            if waits:
                def run(h, waits=waits):
                    for s, v in waits:
                        h.wait_ge(s, v)
                run(self.h[e])

    def emit(self):
        pass


def mm(out, lhsT, rhs, st, sp):
    return lambda e: e.matmul(out, lhsT=lhsT, rhs=rhs, start=st, stop=sp)


def seq(fns):
    def run(e):
        ins = None
        for f in fns:
            ins = f(e)
        return ins
    return run


class Arena:
    def __init__(self, nc):
        self.nc = nc
        rem = nc.sbuf_bytes_remaining
        size = (rem - 512) // 64 * 64
        r = nc.bump_sbuf(size)
        self.base = r[0]
        self.end = r[0] + size
        self.cur = self.base
        self.uid = 0

    def alloc(self, name, shape, dt):
        esz = 4 if dt in (F32, I32) else 2
        n = 1
        for s in shape[1:]:
            n *= s
        nbytes = (n * esz + 63) // 64 * 64
        assert self.cur + nbytes <= self.end, "SBUF arena overflow at %s (%d over)" % (name, self.cur + nbytes - self.end)
        self.uid += 1
        t = self.nc.alloc_sbuf_tensor_at("%s_%d" % (name, self.uid), list(shape), dt, offset=self.cur)
        self.cur += nbytes
        return t

    def mark(self):
        return self.cur

    def reset(self, m):
        self.cur = m


def build(cfg):
    if "_wrec" not in cfg and not cfg.get("_dry"):
        recs = build(dict(cfg, _dry=True))
        cfg = dict(cfg, _wrec=recs)
    DRY = bool(cfg.get("_dry"))
    WREC = cfg.get("_wrec")
    npass = cfg.get("npass", 5)
    nlayers = cfg.get("nlayers", 4)
    pass_list = cfg.get("passes", list(range(npass)))
    nc = bass.Bass("TRN2", target_bir_lowering=False)
    P = Prog(nc, dry=DRY)

    def din(name, shape, dt=F32):
        return nc.dram_tensor(name, list(shape), dt, kind="ExternalInput").ap()

    def dout(name, shape):
        return nc.dram_tensor(name, list(shape), F32, kind="ExternalOutput").ap()

    xp = din("xp", [SEQ, D]); xs = din("xs", [4, D])
    NPOOL = cfg.get("npool", 1280)
    ck = din("ck", [2 * NPOOL * 128, 512]); cv = din("cv", [2 * NPOOL * 128, 512]); cki = din("cki", [2 * NPOOL * 4, 2048])
    stc = din("stc", [2, 30, D]); stf = din("stf", [4, 2, DFF]); pt = din("pt", [128], I32)
    w_in = din("w_in", [2, D, IN_COLS]); w_out = din("w_out", [2, D, D]); w_pw1 = din("w_pw1", [2, D, 2 * D])
    w_pw2 = din("w_pw2", [2, D, D]); w_g = din("w_g", [4, D, DFF]); w_u = din("w_u", [4, D, DFF]); w_d = din("w_d", [4, DFF, D])
    WD = {"w_in": w_in, "w_out": w_out, "w_pw1": w_pw1, "w_pw2": w_pw2, "w_g": w_g, "w_u": w_u, "w_d": w_d}
    vecD = din("vecD", [28, D]); vecF = din("vecF", [16, DFF]); wdw = din("wdw", [62, D])
    c_ident = din("c_ident", [128, 128]); c_pq = din("c_pq", [128, 128]); c_pi = din("c_pi", [128, 128])
    c_ropeq = din("c_ropeq", [5, 2, 128, 512]); c_ropei = din("c_ropei", [5, 2, 128, 512])
    c_ttab = din("c_ttab", [SEQ + 128, 48]); c_cmask = din("c_cmask", [128, 128]); c_iota = din("c_iota", [128, 128])

    y_p = dout("y_p", [SEQ, D]); y_s = dout("y_s", [4, D])
    k_p = dout("k_p", [2, SEQ, 512]); v_p = dout("v_p", [2, SEQ, 512]); ki_p = dout("ki_p", [2, SEQ, 64])
    conv_p = dout("conv_p", [2, 30, D]); ffn_p = dout("ffn_p", [4, 2, DFF])
    k_s = dout("k_s", [2, 4, 512]); v_s = dout("v_s", [2, 4, 512]); ki_s = dout("ki_s", [2, 4, 64])
    conv_s = dout("conv_s", [2, 30, D]); ffn_s = dout("ffn_s", [4, 2, DFF])

    scrKT = nc.dram_tensor("scrKT", [2, 128, 4, SEQ], BF16).ap()
    scrV = nc.dram_tensor("scrV", [2, 128, 16, 512], BF16).ap()
    scrKI = nc.dram_tensor("scrKI", [2, 128, SEQ], BF16).ap()

    A = Arena(nc)
    PSN = ["psA", "psB", "psC", "psD", "psE", "psF", "psG", "psH"]
    ps = {n: nc.alloc_psum_tensor(n, [128, 512], F32) for n in PSN}

    identf = A.alloc("identf", [128, 128], F32)
    identb = A.alloc("identb", [128, 128], BF16)
    onesf = A.alloc("onesf", [128, 128], F32)
    onesb = A.alloc("onesb", [128, 128], BF16)
    pqb = A.alloc("pqb", [128, 128], BF16)
    pib = A.alloc("pib", [128, 128], BF16)
    cmask = A.alloc("cmask", [128, 128], F32)
    epsT = A.alloc("epsT", [128, 1], F32)
    cmaskp = A.alloc("cmaskp", [128, 512], BF16)
    vD = A.alloc("vD", [128, NCH, 28], F32)
    vF = A.alloc("vF", [128, NFF, 16], F32)
    wdwT = A.alloc("wdwT", [128, NCH, 62], F32)
    Wt = [A.alloc("W0", [128, 8192], BF16), A.alloc("W1", [128, 8192], BF16)]
    cmark = A.mark()
    xres = A.alloc("xres", [128, NCH, 512], F32)
    xbf = A.alloc("xbf", [128, NCH, 512], BF16)
    rq = A.alloc("rq", [128, 2, 512], F32)
    ri = A.alloc("ri", [128, 2, 512], F32)
    ttq = A.alloc("ttq", [128, 4, 48], F32)
    ghalo = A.alloc("ghalo", [128, 4, NFF, 2], F32)
    ahalo = A.alloc("ahalo", [128, 2, NCH, 30], BF16)
    pmark = A.mark()

    wst = {"ptr": 0, "issued": 0}
    wrecs = []

    def _wissue(i):
        rec = WREC[i]
        sl = i % 2
        wn_ = "W%d" % sl
        if rec[0] == "kiwi":
            wj_ = w_in[rec[1]]
            vkw_ = Wt[sl][:, 0:16 * 256].rearrange("p (k n) -> p k n", k=16)
            srcki = wv(wj_, 0, 16, 4096, 64)
            P.dma("pool", lambda e: e.dma_start(out=vkw_[:, :, 0:64], in_=srcki), writes=[wn_])
            P.dma("pool", lambda e: e.dma_start(out=vkw_[:, :, 64:128], in_=srcki), writes=[wn_])
            P.dma("pool", lambda e: e.dma_start(out=vkw_[:, :, 128:208], in_=wv(wj_, 0, 16, 4096, 80)), writes=[wn_])
        else:
            kind, li, k0, kc, c0, ncol = rec
            view_ = Wt[sl][:, 0:kc * ncol].rearrange("p (k n) -> p k n", k=kc)
            src_ = wv(WD[kind][li], k0, kc, c0, ncol)
            P.dma("pool", lambda e: e.dma_start(out=view_, in_=src_), writes=[wn_])

    def wload(kind, li, k0=0, kc=16, c0=0, ncol=512):
        rec = (kind, li, k0, kc, c0, ncol)
        i = wst["ptr"]
        wst["ptr"] += 1
        wrecs.append(rec)
        sl = i % 2
        if kind == "kiwi":
            view = Wt[sl][:, 0:16 * 256].rearrange("p (k n) -> p k n", k=16)
        else:
            view = Wt[sl][:, 0:kc * ncol].rearrange("p (k n) -> p k n", k=kc)
        if DRY:
            return view, "W%d" % sl
        assert WREC[i] == rec, (i, WREC[i], rec)
        while wst["issued"] < min(i + 2, len(WREC)):
            _wissue(wst["issued"])
            wst["issued"] += 1
        return view, "W%d" % sl

    def wv(w2d, k0, kc, c0, ncol):
        return w2d.rearrange("(k p) n -> p k n", p=128)[:, k0:k0 + kc, c0:c0 + ncol]

    tmpc = A.alloc("tmpc", [128, 128], F32)
    P.dma("sp", lambda e: e.dma_start(out=identf[:, :], in_=c_ident[:, :]), writes=["identf"])
    P.dma("sp", lambda e: e.dma_start(out=cmask[:, :], in_=c_cmask[:, :]), writes=["cmask"])
    P.op("dve", lambda e: e.tensor_copy(out=identb[:, :], in_=identf[:, :]), reads=["identf"], writes=["identb"])
    P.op("dve", lambda e: e.memset(onesf[:, :], 1.0), writes=["onesf"])
    P.op("dve", lambda e: e.memset(onesb[:, :], 1.0), writes=["onesb"])
    P.op("dve", lambda e: e.memset(epsT[:, :], EPS), writes=["epsT"])
    P.op("dve", lambda e: e.memset(cmaskp[:, :], 0.0), writes=["cmaskp"])
    P.op("dve", lambda e: e.tensor_copy(out=cmaskp[:, 384:512], in_=cmask[:, :]), reads=["cmask"], writes=["cmaskp"])
    P.dma("sp", lambda e: e.dma_start(out=tmpc[:, :], in_=c_pq[:, :]), writes=["tmpc"])
    P.op("dve", lambda e: e.tensor_copy(out=pqb[:, :], in_=tmpc[:, :]), reads=["tmpc"], writes=["pqb"])
    P.dma("sp", lambda e: e.dma_start(out=tmpc[:, :], in_=c_pi[:, :]), reads=["tmpc"], writes=["tmpc"])
    P.op("dve", lambda e: e.tensor_copy(out=pib[:, :], in_=tmpc[:, :]), reads=["tmpc"], writes=["pib"])

    def rows_to_fm(rows_dram, R, L, dst, stage):
        P.dma("sp", lambda e: e.dma_start(out=stage[0:R, 0:L], in_=rows_dram), writes=["stage"])
        nchunk = L // 128
        per = 512 // R
        c = 0
        bi = 0
        while c < nchunk:
            n = min(per, nchunk - c)
            bank = ["psA", "psB"][bi % 2]
            bi += 1
            fns = []
            for i in range(n):
                fns.append(lambda e, i=i, c=c: e.transpose(out=ps[bank][:, i * R:(i + 1) * R],
                                                          in_=stage[0:R, (c + i) * 128:(c + i + 1) * 128],
                                                          identity=identf[0:R, 0:R]))
            P.op("pe", seq(fns), reads=["stage", "identf"], writes=[bank])
            P.op("dve", lambda e, c=c, n=n, bank=bank: e.tensor_copy(
                out=dst[:, c:c + n, 0:R], in_=ps[bank][:, 0:n * R].rearrange("p (i r) -> p i r", r=R)),
                reads=[bank], writes=[dst_name(dst)])
            c += n

    names = {}

    def dst_name(t):
        return names[id(t)]

    m0 = A.mark()
    stage = A.alloc("stage", [64, DFF], F32)
    names[id(vD)] = "vD"; names[id(vF)] = "vF"; names[id(wdwT)] = "wdwT"
    rows_to_fm(vecD[:, :], 28, D, vD, stage)
    rows_to_fm(vecF[:, :], 16, DFF, vF, stage)
    rows_to_fm(wdw[:, :], 62, D, wdwT, stage)
    P.barrier()
    A.reset(m0)

    def vcol(row, c):
        return vD[:, c, row:row + 1]

    PS_ROT = {"i": 0}

    def load_x(p, N):
        m = A.mark()
        xtm = A.alloc("xtm", [128, D], F32)
        ntt = (N + 127) // 128
        for tt in range(ntt):
            nt = min(128, N - tt * 128)
            src = xp[p * 512 + tt * 128: p * 512 + tt * 128 + nt, :] if p < 4 else xs[0:nt, :]
            P.dma("sp", lambda e, src=src, nt=nt: e.dma_start(out=xtm[0:nt, :], in_=src), writes=["xtm"])
            for c4 in range(4):
                bank = ["psA", "psB"][c4 % 2]
                fns = [lambda e, i=i, c4=c4, nt=nt: e.transpose(out=ps[bank][:, i * 128:i * 128 + nt],
                                                                 in_=xtm[0:nt, (c4 * 4 + i) * 128:(c4 * 4 + i + 1) * 128],
                                                                 identity=identf[0:nt, 0:nt]) for i in range(4)]
                P.op("pe", seq(fns), reads=["xtm", "identf"], writes=[bank])
                srcv = ps[bank][:, :].rearrange("p (i t) -> p i t", i=4)[:, :, 0:nt]
                P.op("dve", lambda e, c4=c4, tt=tt, nt=nt, srcv=srcv: e.tensor_copy(
                    out=xres[:, c4 * 4:c4 * 4 + 4, tt * 128:tt * 128 + nt], in_=srcv), reads=[bank], writes=["xres"])
                P.op("act", lambda e, c4=c4, tt=tt, nt=nt, srcv=srcv: e.copy(
                    out=xbf[:, c4 * 4:c4 * 4 + 4, tt * 128:tt * 128 + nt], in_=srcv), reads=[bank], writes=["xbf"])
        P.barrier()
        A.reset(m)

    def store_y(p, N):
        m = A.mark()
        ytm = A.alloc("ytm", [128, D], F32)
        ntt = (N + 127) // 128
        for tt in range(ntt):
            nt = min(128, N - tt * 128)
            for c4 in range(4):
                bank = ["psA", "psB"][c4 % 2]
                fns = [lambda e, i=i, c4=c4, nt=nt, tt=tt: e.transpose(out=ps[bank][0:nt, i * 128:(i + 1) * 128],
                                                                        in_=xres[:, c4 * 4 + i, tt * 128:tt * 128 + nt],
                                                                        identity=identf[:, :]) for i in range(4)]
                P.op("pe", seq(fns), reads=["xres", "identf"], writes=[bank])
                P.op("dve", lambda e, c4=c4, nt=nt: e.tensor_copy(out=ytm[0:nt, c4 * 512:(c4 + 1) * 512], in_=ps[bank][0:nt, :]),
                     reads=[bank], writes=["ytm"])
            dst = y_p[p * 512 + tt * 128: p * 512 + tt * 128 + nt, :] if p < 4 else y_s[0:nt, :]
            P.dma("sp", lambda e, dst=dst, nt=nt: e.dma_start(out=dst, in_=ytm[0:nt, :]), reads=["ytm"])
        P.barrier()
        A.reset(m)

    class LN:
        def __init__(self, N, src, resname):
            self.N = N
            self.src = src
            self.res = resname
            self.sq = [A.alloc("lnsq0", [128, 512], F32), A.alloc("lnsq1", [128, 512], F32)]
            self.mean = A.alloc("lnmean", [128, 512], F32)
            self.rstd = A.alloc("lnrstd", [128, 512], F32)
            self.nmr = A.alloc("lnnmr", [128, 512], F32)

        def stats(self, c):
            N = self.N
            sq = self.sq[c % 2]
            sqn = "lnsq%d" % (c % 2)
            z = self.src[:, c, 0:N]
            P.op("act", lambda e: e.activation(out=sq[:, 0:N], in_=z, func=AF.Square), reads=[self.res], writes=[sqn])
            P.op("pe", seq([mm(ps["psG"][:, 0:N], onesf[:, :], z, c == 0, c == NCH - 1),
                            mm(ps["psH"][:, 0:N], onesf[:, :], sq[:, 0:N], c == 0, c == NCH - 1)]),
                 reads=[self.res, sqn, "onesf"], writes=["psG", "psH"])

        def finalize(self):
            N = self.N
            mean, rstd, nmr = self.mean, self.rstd, self.nmr
            P.op("dve", lambda e: e.tensor_scalar(out=mean[:, 0:N], in0=ps["psG"][:, 0:N], scalar1=1.0 / D, scalar2=None, op0=ALU.mult),
                 reads=["psG"], writes=["lnmean"])
            P.op("dve", lambda e: e.tensor_tensor(out=nmr[:, 0:N], in0=mean[:, 0:N], in1=mean[:, 0:N], op=ALU.mult),
                 reads=["lnmean"], writes=["lnnmr"])
            P.op("dve", lambda e: e.scalar_tensor_tensor(out=rstd[:, 0:N], in0=ps["psH"][:, 0:N], scalar=1.0 / D, in1=nmr[:, 0:N],
                                                         op0=ALU.mult, op1=ALU.subtract), reads=["psH", "lnnmr"], writes=["lnrstd"])
            P.op("act", lambda e: e.activation(out=rstd[:, 0:N], in_=rstd[:, 0:N], func=AF.Sqrt, bias=epsT[:, 0:1], scale=1.0),
                 reads=["lnrstd", "epsT"], writes=["lnrstd"])
            P.op("dve", lambda e: e.reciprocal(out=rstd[:, 0:N], in_=rstd[:, 0:N]), reads=["lnrstd"], writes=["lnrstd"])
            P.op("dve", lambda e: e.tensor_tensor(out=nmr[:, 0:N], in0=mean[:, 0:N], in1=rstd[:, 0:N], op=ALU.mult),
                 reads=["lnmean", "lnrstd"], writes=["lnnmr"])

        def apply(self, c, g_ap, b_ap, out_f32=None, out_f32_name=None, out_bf=None, out_bf_name=None, func=None):
            N = self.N
            z = self.src[:, c, 0:N]
            t = self.sq[c % 2]
            tn = "lnsq%d" % (c % 2)
            P.op("dve", lambda e: e.tensor_tensor(out=t[:, 0:N], in0=z, in1=self.rstd[:, 0:N], op=ALU.mult),
                 reads=[self.res, "lnrstd"], writes=[tn])
            P.op("dve", lambda e: e.tensor_tensor(out=t[:, 0:N], in0=t[:, 0:N], in1=self.nmr[:, 0:N], op=ALU.subtract),
                 reads=[tn, "lnnmr"], writes=[tn])
            if out_f32 is not None:
                P.op("act", lambda e: e.activation(out=out_f32, in_=t[:, 0:N], func=AF.Identity, bias=b_ap, scale=g_ap),
                     reads=[tn, "vD"], writes=[out_f32_name])
                if out_bf is not None:
                    P.op("pool", lambda e: e.tensor_copy(out=out_bf, in_=out_f32), reads=[out_f32_name], writes=[out_bf_name])
            else:
                P.op("act", lambda e: e.activation(out=out_bf, in_=t[:, 0:N], func=func, bias=b_ap, scale=g_ap),
                     reads=[tn, "vD"], writes=[out_bf_name])

    def out_proj_ln(N, wkind, wli, rhs_t, rhs_name, nk, g_row, b_row, bias_row=None):
        m = A.mark()
        ln = LN(N, xres, "xres")
        btmp = A.alloc("btmp", [128, 512], F32)
        for g4 in range(4):
            k0 = 0
            banks = ["psA", "psB", "psC", "psD"]
            while k0 < nk:
                kc = min(16, nk - k0)
                view, wn = wload(wkind, wli, k0, kc, g4 * 512, 512)
                for i in range(4):
                    fns = [mm(ps[banks[i]][:, 0:N], view[:, k, i * 128:(i + 1) * 128], rhs_t[:, k0 + k, 0:N],
                              (k0 + k) == 0, (k0 + k) == nk - 1) for k in range(kc)]
                    P.op("pe", seq(fns), reads=[wn, rhs_name], writes=[banks[i]])
                k0 += kc
            for i in range(4):
                c = g4 * 4 + i
                bank = banks[i]
                if bias_row is not None:
                    P.op("act", lambda e, bank=bank, c=c: e.activation(out=btmp[:, 0:N], in_=ps[bank][:, 0:N], func=AF.Identity,
                                                                       bias=vcol(bias_row, c), scale=1.0),
                         reads=[bank, "vD"], writes=["btmp"])
                    P.op("dve", lambda e, c=c: e.scalar_tensor_tensor(out=xres[:, c, 0:N], in0=xres[:, c, 0:N], scalar=ALPHA,
                                                                      in1=btmp[:, 0:N], op0=ALU.mult, op1=ALU.add),
                         reads=["btmp", "xres"], writes=["xres"])
                else:
                    P.op("dve", lambda e, bank=bank, c=c: e.scalar_tensor_tensor(out=xres[:, c, 0:N], in0=xres[:, c, 0:N], scalar=ALPHA,
                                                                                 in1=ps[bank][:, 0:N], op0=ALU.mult, op1=ALU.add),
                         reads=[bank, "xres"], writes=["xres"])
                ln.stats(c)
        ln.finalize()
        for c in range(NCH):
            ln.apply(c, vcol(g_row, c), vcol(b_row, c), out_f32=xres[:, c, 0:N], out_f32_name="xres",
                     out_bf=xbf[:, c, 0:N], out_bf_name="xbf")
        P.barrier()
        A.reset(m)

    def ffn_layer(l, p, N):
        m = A.mark()
        hT = A.alloc("hT", [128, NFF, 512], BF16)
        gsb = [A.alloc("gsb0", [128, 516], F32), A.alloc("gsb1", [128, 516], F32)]
        cc = [A.alloc("cc%d" % i, [128, 512], F32) for i in range(4)]
        for ft in range(11):
            vg, gn = wload("w_g", l, 0, 16, ft * 512, 512)
            for i in range(4):
                f = ft * 4 + i
                bg = ["psA", "psC"][i % 2]
                P.op("pe", seq([mm(ps[bg][:, 0:N], vg[:, k, i * 128:(i + 1) * 128], xbf[:, k, 0:N], k == 0, k == 15) for k in range(16)]),
                     reads=[gn, "xbf"], writes=[bg])
                g = gsb[f % 2]
                gnm = "gsb%d" % (f % 2)
                c_ = cc[i]
                cn = "cc%d" % i
                P.op("dve", lambda e, g=g, f=f: e.tensor_copy(out=g[:, 0:2], in_=ghalo[:, l, f, :]), reads=["ghalo"], writes=[gnm])
                P.op("act", lambda e, g=g, bg=bg: e.copy(out=g[:, 2:2 + N], in_=ps[bg][:, 0:N]), reads=[bg], writes=[gnm])
                P.op("dve", lambda e, g=g, f=f: e.tensor_copy(out=ghalo[:, l, f, :], in_=g[:, N:N + 2]), reads=[gnm], writes=["ghalo"])
                P.op("dve", lambda e, g=g, c_=c_, f=f: e.tensor_scalar(out=c_[:, 0:N], in0=g[:, 2:2 + N], scalar1=vF[:, f, l * 4 + 2:l * 4 + 3],
                                                                      scalar2=vF[:, f, l * 4 + 3:l * 4 + 4], op0=ALU.mult, op1=ALU.add),
                     reads=[gnm, "vF"], writes=[cn])
                P.op("dve", lambda e, g=g, c_=c_, f=f: e.scalar_tensor_tensor(out=c_[:, 0:N], in0=g[:, 1:1 + N], scalar=vF[:, f, l * 4 + 1:l * 4 + 2],
                                                                             in1=c_[:, 0:N], op0=ALU.mult, op1=ALU.add),
                     reads=[gnm, "vF", cn], writes=[cn])
                P.op("dve", lambda e, g=g, c_=c_, f=f: e.scalar_tensor_tensor(out=c_[:, 0:N], in0=g[:, 0:N], scalar=vF[:, f, l * 4:l * 4 + 1],
                                                                             in1=c_[:, 0:N], op0=ALU.mult, op1=ALU.add),
                     reads=[gnm, "vF", cn], writes=[cn])
                P.op("act", lambda e, c_=c_: e.activation(out=c_[:, 0:N], in_=c_[:, 0:N], func=AF.Silu), reads=[cn], writes=[cn])
            vu, un = wload("w_u", l, 0, 16, ft * 512, 512)
            for i in range(4):
                f = ft * 4 + i
                bu = ["psB", "psD"][i % 2]
                c_ = cc[i]
                cn = "cc%d" % i
                P.op("pe", seq([mm(ps[bu][:, 0:N], vu[:, k, i * 128:(i + 1) * 128], xbf[:, k, 0:N], k == 0, k == 15) for k in range(16)]),
                     reads=[un, "xbf"], writes=[bu])
                P.op("dve", lambda e, c_=c_, f=f, bu=bu: e.tensor_tensor(out=hT[:, f, 0:N], in0=c_[:, 0:N], in1=ps[bu][:, 0:N], op=ALU.mult),
                     reads=[cn, bu], writes=["hT"])
        if p >= 3:
            dstt = ffn_p[l] if p == 3 else ffn_s[l]
            for t in range(2):
                dv = dstt[t, :].rearrange("(f q) -> q f", q=128)
                for q4 in range(4):
                    P.dma("sp", lambda e, q4=q4, dv=dv, t=t: e.dma_start(out=dv[:, q4 * 11:(q4 + 1) * 11], in_=ghalo[:, l, q4 * 11:(q4 + 1) * 11, t],
                                                                         allow_slow_non_contiguous=True), reads=["ghalo"])
        out_proj_ln(N, "w_d", l, hT, "hT", NFF, 16 + l, 20 + l)
        A.reset(m)

    def conv_layer(j, i_layer, p, N):
        m = A.mark()
        aTb = A.alloc("aTb", [128, NCH, 544], BF16)
        cv_ = A.alloc("cv", [128, NCH, 512], F32)
        sig = [A.alloc("sig%d" % i, [128, 512], F32) for i in range(4)]
        dgc = [A.alloc("dgc0", [128, 31, 128], BF16), A.alloc("dgc1", [128, 31, 128], BF16)]
        alast = A.alloc("alast", [128, NCH, 32], F32)
        atm = A.alloc("atm", [32, D], F32)
        P.op("pool", lambda e: e.tensor_copy(out=aTb[:, :, 0:30], in_=ahalo[:, j, :, :]), reads=["ahalo"], writes=["aTb"])
        nl = min(30, N)
        for g4 in range(4):
            vg, gn = wload("w_pw1", j, 0, 16, D + g4 * 512, 512)
            for i in range(4):
                c = g4 * 4 + i
                bg = ["psB", "psD"][i % 2]
                P.op("pe", seq([mm(ps[bg][:, 0:N], vg[:, k, i * 128:(i + 1) * 128], xbf[:, k, 0:N], k == 0, k == 15) for k in range(16)]),
                     reads=[gn, "xbf"], writes=[bg])
                s_ = sig[i]
                sn = "sig%d" % i
                P.op("act", lambda e, s_=s_, bg=bg, c=c: e.activation(out=s_[:, 0:N], in_=ps[bg][:, 0:N], func=AF.Sigmoid,
                                                                      bias=vcol(25 + 2 * j, c), scale=1.0), reads=[bg, "vD"], writes=[sn])
            va, an = wload("w_pw1", j, 0, 16, g4 * 512, 512)
            for i in range(4):
                c = g4 * 4 + i
                ba = ["psA", "psC"][i % 2]
                s_ = sig[i]
                sn = "sig%d" % i
                P.op("pe", seq([mm(ps[ba][:, 0:N], va[:, k, i * 128:(i + 1) * 128], xbf[:, k, 0:N], k == 0, k == 15) for k in range(16)]),
                     reads=[an, "xbf"], writes=[ba])
                P.op("dve", lambda e, s_=s_, ba=ba, c=c: e.scalar_tensor_tensor(out=aTb[:, c, 30:30 + N], in0=ps[ba][:, 0:N], scalar=vcol(24 + 2 * j, c),
                                                                                in1=s_[:, 0:N], op0=ALU.add, op1=ALU.mult),
                     reads=[ba, sn, "vD"], writes=["aTb"])
                if p >= 3:
                    P.op("dve", lambda e, s_=s_, ba=ba, c=c: e.scalar_tensor_tensor(out=alast[:, c, 0:nl], in0=ps[ba][:, N - nl:N], scalar=vcol(24 + 2 * j, c),
                                                                                    in1=s_[:, N - nl:N], op0=ALU.add, op1=ALU.mult),
                         reads=[ba, sn, "vD"], writes=["alast"])
        P.op("pool", lambda e: e.tensor_copy(out=ahalo[:, j, :, :], in_=aTb[:, :, N:N + 30]), reads=["aTb"], writes=["ahalo"])
        if p >= 3:
            for c4 in range(4):
                bank = ["psE", "psF"][c4 % 2]
                fns = [lambda e, ii=ii, c4=c4: e.transpose(out=ps[bank][0:nl, ii * 128:(ii + 1) * 128], in_=alast[:, c4 * 4 + ii, 0:nl],
                                                           identity=identf[:, :]) for ii in range(4)]
                P.op("pe", seq(fns), reads=["alast", "identf"], writes=[bank])
                P.op("act", lambda e, c4=c4, bank=bank: e.copy(out=atm[0:nl, c4 * 512:(c4 + 1) * 512], in_=ps[bank][0:nl, :]),
                     reads=[bank], writes=["atm"])
            if p == 3:
                P.dma("sp", lambda e: e.dma_start(out=conv_p[j, :, :], in_=atm[0:30, :]), reads=["atm"])
            else:
                P.dma("sp", lambda e: e.dma_start(out=conv_s[j, 26:30, :], in_=atm[0:4, :]), reads=["atm"])
                P.dma("sp", lambda e: e.dma_start(out=conv_s[j, 0:26, :], in_=stc[j, 4:30, :]))
        ln = LN(N, cv_, "cv")
        for c in range(NCH):
            dg = dgc[c % 2]
            dn = "dgc%d" % (c % 2)
            P.op("dve", lambda e, dg=dg, c=c: e.tensor_tensor(out=dg[:, :, :], in0=identf[:, :].unsqueeze(1).broadcast_to([128, 31, 128]),
                                                              in1=wdwT[:, c, j * 31:j * 31 + 31].unsqueeze(2).broadcast_to([128, 31, 128]), op=ALU.mult),
                 reads=["identf", "wdwT"], writes=[dn])
            bank = ["psA", "psB"][c % 2]
            P.op("pe", seq([mm(ps[bank][:, 0:N], dg[:, w, :], aTb[:, c, w:w + N], w == 0, w == 30) for w in range(31)]),
                 reads=[dn, "aTb"], writes=[bank])
            P.op("act", lambda e, c=c, bank=bank: e.activation(out=cv_[:, c, 0:N], in_=ps[bank][:, 0:N], func=AF.Identity,
                                                               bias=vcol(0 + j, c), scale=1.0), reads=[bank, "vD"], writes=["cv"])
            ln.stats(c)
        ln.finalize()
        for c in range(NCH):
            ln.apply(c, vcol(2 + j, c), vcol(4 + j, c), out_bf=xbf[:, c, 0:N], out_bf_name="xbf", func=AF.Silu)
        P.barrier()
        A.reset(m)
        out_proj_ln(N, "w_pw2", j, xbf, "xbf", 16, 8 + i_layer, 12 + i_layer, bias_row=6 + j)

    def rope_fm(bank, N, tab, pmat, dst_ap, dst_name, bset, sw_bank, rows=128):
        bi, rawb, t1, t2 = bset
        rn_, t1n, t2n = "rawb%d" % bi, "t1%d" % bi, "t2%d" % bi
        C = tab[0:rows, 0, 0:N]
        S = tab[0:rows, 1, 0:N]
        P.op("act", lambda e: e.copy(out=rawb[0:rows, 0:N], in_=ps[bank][0:rows, 0:N]), reads=[bank], writes=[rn_])
        P.op("pe", mm(ps[sw_bank][0:rows, 0:N], pmat[0:rows, 0:rows], rawb[0:rows, 0:N], True, True), reads=[rn_, "pqb", "pib"], writes=[sw_bank])
        P.op("dve", lambda e: e.tensor_tensor(out=t1[0:rows, 0:N], in0=ps[bank][0:rows, 0:N], in1=C, op=ALU.mult), reads=[bank, "rq", "ri"], writes=[t1n])
        P.op("dve", lambda e: e.tensor_tensor(out=t2[0:rows, 0:N], in0=ps[sw_bank][0:rows, 0:N], in1=S, op=ALU.mult), reads=[sw_bank, "rq", "ri"], writes=[t2n])
        P.op("pool", lambda e: e.tensor_tensor(out=dst_ap, in0=t1[0:rows, 0:N], in1=t2[0:rows, 0:N], op=ALU.add), reads=[t1n, t2n], writes=[dst_name])

    def rope_tm(x3, nt, tt, nh, half, coff, resname):
        cosb = ttq[0:nt, tt, coff:coff + half].unsqueeze(1).broadcast_to([nt, nh, half])
        sinb = ttq[0:nt, tt, coff + half:coff + 2 * half].unsqueeze(1).broadcast_to([nt, nh, half])
        x1 = x3[:, :, 0:half]
        x2 = x3[:, :, half:2 * half]
        rtmp = RT["rtmp"]
        ta, tb, tc_, td = (rtmp[0:nt, k, 0:nh * half].rearrange("p (h d) -> p h d", h=nh) for k in range(4))
        P.op("dve", lambda e: e.tensor_tensor(out=ta, in0=x1, in1=cosb, op=ALU.mult), reads=[resname, "ttq"], writes=["rtmp"])
        P.op("dve", lambda e: e.tensor_tensor(out=tb, in0=x2, in1=sinb, op=ALU.mult), reads=[resname, "ttq"], writes=["rtmp"])
        P.op("dve", lambda e: e.tensor_tensor(out=tc_, in0=x2, in1=cosb, op=ALU.mult), reads=[resname, "ttq"], writes=["rtmp"])
        P.op("dve", lambda e: e.tensor_tensor(out=td, in0=x1, in1=sinb, op=ALU.mult), reads=[resname, "ttq"], writes=["rtmp"])
        P.op("dve", lambda e: e.tensor_tensor(out=x1, in0=ta, in1=tb, op=ALU.subtract), reads=["rtmp"], writes=[resname])
        P.op("dve", lambda e: e.tensor_tensor(out=x2, in0=tc_, in1=td, op=ALU.add), reads=["rtmp"], writes=[resname])

    RT = {}

    def bisect(scores, sname, nq, nk, theta, work):
        lo, wd, mid, cntt, sel, junk, cnt4 = work["lo"], work["wd"], work["mid"], work["cnt"], work["sel"], work["junk"], work["cnt4"]
        jw = work["jw"]
        chunks = []
        c0 = 0
        while c0 < nk:
            w = min(jw, nk - c0)
            chunks.append((c0, w))
            c0 += w
        assert len(chunks) <= 8
        P.op("dve", lambda e: e.tensor_reduce(out=mid[0:nq, :], in_=scores[0:nq, 0:nk], axis=AX.X, op=ALU.max), reads=[sname], writes=["bs_mid"])
        P.op("dve", lambda e: e.memset(lo[0:nq, :], NEG - 1.0), writes=["bs_lo"])
        P.op("dve", lambda e: e.tensor_scalar(out=wd[0:nq, :], in0=mid[0:nq, :], scalar1=-(NEG - 1.0), scalar2=None, op0=ALU.add), reads=["bs_mid"], writes=["bs_wd"])
        nch = len(chunks)
        for it in range(23):
            hf = 0.5 ** (it + 1)
            P.op("dve", lambda e, hf=hf: e.tensor_scalar(out=mid[0:nq, :], in0=wd[0:nq, :], scalar1=hf, scalar2=lo[0:nq, 0:1], op0=ALU.mult, op1=ALU.add),
                 reads=["bs_lo", "bs_wd"], writes=["bs_mid"])
            P.op("dve", lambda e: e.memset(cnt4[0:nq, 0:nch], 0.0), writes=["bs_cnt4"])
            for k, (c0, w) in enumerate(chunks):
                P.op("dve", lambda e, k=k, c0=c0, w=w: e.tensor_scalar(out=junk[0:nq, 0:w], in0=scores[0:nq, c0:c0 + w], scalar1=mid[0:nq, 0:1], scalar2=0.0,
                                                                      op0=ALU.is_gt, op1=ALU.add, accum_out=cnt4[0:nq, k:k + 1]),
                     reads=[sname, "bs_mid", "bs_cnt4"], writes=["bs_junk", "bs_cnt4"])
            if nch > 1:
                P.op("dve", lambda e: e.tensor_reduce(out=cntt[0:nq, :], in_=cnt4[0:nq, 0:nch], axis=AX.X, op=ALU.add), reads=["bs_cnt4"], writes=["bs_cnt"])
                csrc, csn = cntt, "bs_cnt"
            else:
                csrc, csn = cnt4, "bs_cnt4"
            P.op("dve", lambda e, csrc=csrc: e.scalar_tensor_tensor(out=sel[0:nq, :], in0=csrc[0:nq, 0:1], scalar=255.5, in1=wd[0:nq, :], op0=ALU.is_gt, op1=ALU.mult),
                 reads=[csn, "bs_wd"], writes=["bs_sel"])
            P.op("dve", lambda e, hf=hf: e.scalar_tensor_tensor(out=lo[0:nq, :], in0=sel[0:nq, :], scalar=hf, in1=lo[0:nq, :], op0=ALU.mult, op1=ALU.add),
                 reads=["bs_lo", "bs_sel"], writes=["bs_lo"])
        P.op("dve", lambda e: e.tensor_copy(out=theta[0:nq, :], in_=lo[0:nq, :]), reads=["bs_lo"], writes=["theta"])

    def indexer_tile(nq, qi_lhs, ki_rhs, ncols, wq_ap, dgt, Rb, sc_out, sc_name, kin_name, diag_mask_cols=None):
        banks = ["psA", "psB"]
        def dmm(h):
            P.op("pe", mm(ps[banks[h % 2]][0:nq, 0:ncols], qi_lhs(h), ki_rhs(h), True, True), reads=["qiT", kin_name], writes=[banks[h % 2]])
        dmm(0)
        masked = diag_mask_cols is not None
        if masked:
            P.op("pe", mm(ps["psC"][0:nq, 0:ncols], identb[0:nq, 0:nq], cmaskp[0:nq, 512 - ncols:512], True, False), reads=["identb", "cmaskp"], writes=["psC"])
        for h in range(16):
            if h + 1 < 16:
                dmm(h + 1)
            R = Rb[h % 2]
            rn = "Rb%d" % (h % 2)
            P.op("act", lambda e, R=R, h=h: e.activation(out=R[0:nq, 0:ncols], in_=ps[banks[h % 2]][0:nq, 0:ncols], func=AF.Relu),
                 reads=[banks[h % 2]], writes=[rn])
            P.op("pe", mm(ps["psC"][0:nq, 0:ncols], dgt[0:nq, h, 0:nq], R[0:nq, 0:ncols], (h == 0) and not masked, h == 15), reads=[rn, "dgt"], writes=["psC"])
        P.op("act", lambda e: e.copy(out=sc_out, in_=ps["psC"][0:nq, 0:ncols]), reads=["psC"], writes=[sc_name])

    def attn_layer(j, i_layer, p, N):
        m = A.mark()
        sample = (p == 4)
        wj = w_in[j]
        qT = A.alloc("qT", [128, 16, 512 if not sample else 4], BF16)
        KT = A.alloc("KT", [128, 4, SEQ if not sample else 4], BF16)
        Vb = A.alloc("Vb", [128, 16 if not sample else 1, 512], BF16)
        kiT = A.alloc("kiT", [128, SEQ if not sample else 4], BF16)
        qiT = A.alloc("qiT", [128, 8, 512] if not sample else [64, 16, 4], BF16)
        wsb = A.alloc("wsb", [128, 4, 16], F32)
        mtmp = A.mark()
        rawb = A.alloc("rawb", [128, 512], BF16)
        t1 = A.alloc("t1", [128, 512], F32)
        t2 = A.alloc("t2", [128, 512], F32)
        BS = [(0, rawb, t1, t2), (1, A.alloc("rawb1", [128, 512], BF16), A.alloc("t11", [128, 512], F32), A.alloc("t21", [128, 512], F32))]
        RT["rtmp"] = A.alloc("rtmp", [128, 4, 64], F32)
        ksb = A.alloc("ksb", [128, 512], F32)
        vsb = A.alloc("vsb", [128, 512], F32)
        kisb = A.alloc("kisb", [128, 80], F32)
        kbase = p * 512 if not sample else 0
        if not sample and p > 0:
            P.dma("sp", lambda e: e.dma_start(out=KT[:, :, 0:kbase], in_=scrKT[j, :, :, 0:kbase]), writes=["KT"])
            P.dma("sp", lambda e: e.dma_start(out=Vb[:, 0:4 * p, :], in_=scrV[j, :, 0:4 * p, :]), writes=["Vb"])
            P.dma("sp", lambda e: e.dma_start(out=kiT[:, 0:kbase], in_=scrKI[j, :, 0:kbase]), writes=["kiT"])
        for g4 in range(4):
            view, wn = wload("w_in", j, 0, 16, g4 * 512, 512)
            for i in range(4):
                bank = ["psA", "psB"][i % 2]
                P.op("pe", seq([mm(ps[bank][:, 0:N], view[:, k, i * 128:(i + 1) * 128], xbf[:, k, 0:N], k == 0, k == 15) for k in range(16)]),
                     reads=[wn, "xbf"], writes=[bank])
                rope_fm(bank, N, rq, pqb, qT[:, g4 * 4 + i, 0:N], "qT", BS[i % 2], ["psC", "psD"][i % 2])
        vk, kn = wload("w_in", j, 0, 16, 2048, 512)
        for i in range(4):
            bank = ["psA", "psB"][i % 2]
            P.op("pe", seq([mm(ps[bank][:, 0:N], vk[:, k, i * 128:(i + 1) * 128], xbf[:, k, 0:N], k == 0, k == 15) for k in range(16)]),
                 reads=[kn, "xbf"], writes=[bank])
            rope_fm(bank, N, rq, pqb, KT[:, i, kbase:kbase + N], "KT", BS[i % 2], ["psC", "psD"][i % 2])
        ntt = (N + 127) // 128
        kdst = k_p if not sample else k_s
        vdst = v_p if not sample else v_s
        for tt in range(ntt):
            nt = min(128, N - tt * 128)
            r0 = kbase + tt * 128
            P.op("pe", seq([mm(ps["psE"][0:nt, :], xbf[:, k, tt * 128:tt * 128 + nt], vk[:, k, :], k == 0, k == 15) for k in range(16)]),
                 reads=[kn, "xbf"], writes=["psE"])
            P.op("act", lambda e, nt=nt: e.copy(out=ksb[0:nt, :], in_=ps["psE"][0:nt, :]), reads=["psE"], writes=["ksb"])
            rope_tm(ksb[0:nt, :].rearrange("p (h d) -> p h d", h=4), nt, tt, 4, 16, 0, "ksb")
            P.dma("sp", lambda e, nt=nt, r0=r0: e.dma_start(out=kdst[j, r0:r0 + nt, :], in_=ksb[0:nt, :]), reads=["ksb"])
        vv, vn = wload("w_in", j, 0, 16, 2560, 512)
        for tt in range(ntt):
            nt = min(128, N - tt * 128)
            r0 = kbase + tt * 128
            P.op("pe", seq([mm(ps["psF"][0:nt, :], xbf[:, k, tt * 128:tt * 128 + nt], vv[:, k, :], k == 0, k == 15) for k in range(16)]),
                 reads=[vn, "xbf"], writes=["psF"])
            P.op("act", lambda e, nt=nt: e.copy(out=vsb[0:nt, :], in_=ps["psF"][0:nt, :]), reads=["psF"], writes=["vsb"])
            P.op("pool", lambda e, nt=nt, tt=tt: e.tensor_copy(out=Vb[0:nt, (kbase // 128 + tt) if not sample else 0, :], in_=vsb[0:nt, :]),
                 reads=["vsb"], writes=["Vb"])
            P.dma("sp", lambda e, nt=nt, r0=r0: e.dma_start(out=vdst[j, r0:r0 + nt, :], in_=vsb[0:nt, :]), reads=["vsb"])
        if not sample:
            for g2 in range(2):
                view, wn = wload("w_in", j, 0, 16, 3072 + g2 * 512, 512)
                for i in range(4):
                    bank = ["psA", "psB"][i % 2]
                    P.op("pe", seq([mm(ps[bank][:, 0:N], view[:, k, i * 128:(i + 1) * 128], xbf[:, k, 0:N], k == 0, k == 15) for k in range(16)]),
                         reads=[wn, "xbf"], writes=[bank])
                    rope_fm(bank, N, ri, pib, qiT[:, g2 * 4 + i, 0:N], "qiT", BS[i % 2], ["psC", "psD"][i % 2])
        else:
            for g2 in range(2):
                view, wn = wload("w_in", j, 0, 16, 3072 + g2 * 512, 512)
                fns = []
                for hh in range(8):
                    for k in range(16):
                        fns.append(mm(ps["psA"][0:64, (g2 * 8 + hh) * 4:(g2 * 8 + hh) * 4 + 4], view[:, k, hh * 64:(hh + 1) * 64], xbf[:, k, 0:4], k == 0, k == 15))
                P.op("pe", seq(fns), reads=[wn, "xbf"], writes=["psA"])
            Cb = ri[0:64, 0, 0:4].unsqueeze(1).broadcast_to([64, 16, 4])
            Sb = ri[0:64, 1, 0:4].unsqueeze(1).broadcast_to([64, 16, 4])
            P.op("act", lambda e: e.copy(out=rawb[0:64, 0:64], in_=ps["psA"][0:64, 0:64]), reads=["psA"], writes=["rawb0"])
            P.op("pe", mm(ps["psC"][0:64, 0:64], pib[0:64, 0:64], rawb[0:64, 0:64], True, True), reads=["rawb0", "pib"], writes=["psC"])
            P.op("dve", lambda e: e.tensor_tensor(out=t1[0:64, 0:64].rearrange("p (h t) -> p h t", h=16),
                                                  in0=ps["psA"][0:64, 0:64].rearrange("p (h t) -> p h t", h=16), in1=Cb, op=ALU.mult),
                 reads=["psA", "ri"], writes=["t10"])
            P.op("dve", lambda e: e.tensor_tensor(out=t2[0:64, 0:64].rearrange("p (h t) -> p h t", h=16),
                                                  in0=ps["psC"][0:64, 0:64].rearrange("p (h t) -> p h t", h=16), in1=Sb, op=ALU.mult),
                 reads=["psC", "ri"], writes=["t20"])
            P.op("pool", lambda e: e.tensor_tensor(out=qiT[0:64, :, :].rearrange("p h t -> p (h t)"), in0=t1[0:64, 0:64], in1=t2[0:64, 0:64], op=ALU.add),
                 reads=["t10", "t20"], writes=["qiT"])
        vkw, wn = wload("kiwi", j)
        P.op("pe", seq([mm(ps["psA"][:, 0:N], vkw[:, k, 0:128], xbf[:, k, 0:N], k == 0, k == 15) for k in range(16)]), reads=[wn, "xbf"], writes=["psA"])
        rope_fm("psA", N, ri, pib, kiT[:, kbase:kbase + N], "kiT", BS[0], "psC")
        kidst = ki_p if not sample else ki_s
        for tt in range(ntt):
            nt = min(128, N - tt * 128)
            r0 = kbase + tt * 128
            P.op("pe", seq([mm(ps["psE"][0:nt, 0:80], xbf[:, k, tt * 128:tt * 128 + nt], vkw[:, k, 128:208], k == 0, k == 15) for k in range(16)]),
                 reads=[wn, "xbf"], writes=["psE"])
            P.op("act", lambda e, nt=nt: e.copy(out=kisb[0:nt, :], in_=ps["psE"][0:nt, 0:80]), reads=["psE"], writes=["kisb"])
            P.op("pool", lambda e, nt=nt, tt=tt: e.tensor_copy(out=wsb[0:nt, tt, :], in_=kisb[0:nt, 64:80]), reads=["kisb"], writes=["wsb"])
            rope_tm(kisb[0:nt, 0:64].rearrange("p (h d) -> p h d", h=1), nt, tt, 1, 8, 32, "kisb")
            P.dma("sp", lambda e, nt=nt, r0=r0: e.dma_start(out=kidst[j, r0:r0 + nt, :], in_=kisb[0:nt, 0:64]), reads=["kisb"])
        if not sample and p < 3:
            P.dma("sp", lambda e: e.dma_start(out=scrKT[j, :, :, kbase:kbase + N], in_=KT[:, :, kbase:kbase + N]), reads=["KT"])
            P.dma("sp", lambda e: e.dma_start(out=scrV[j, :, 4 * p:4 * p + 4, :], in_=Vb[:, 4 * p:4 * p + 4, :]), reads=["Vb"])
            P.dma("sp", lambda e: e.dma_start(out=scrKI[j, :, kbase:kbase + N], in_=kiT[:, kbase:kbase + N]), reads=["kiT"])

        ckpt('proj')
        P.barrier()
        A.reset(mtmp)
        dgt = A.alloc("dgt", [128, 16, 128], BF16)
        Rb = [A.alloc("Rb0", [128, 512], BF16), A.alloc("Rb1", [128, 512], BF16)]
        Eb = [A.alloc("Eb0", [128, 512], BF16), A.alloc("Eb1", [128, 512], BF16)]
        Pb = [A.alloc("Pb0", [128, 512], BF16), A.alloc("Pb1", [128, 512], BF16)]
        rden = A.alloc("rden", [128, 512], F32)
        theta = A.alloc("theta", [128, 1], F32)
        work = {k: A.alloc("bs_" + k, [128, 1], F32) for k in ("lo", "wd", "mid", "cnt", "sel")}
        work["cnt4"] = A.alloc("bs_cnt4", [128, 8], F32)
        scale = 128.0 ** -0.5
        if not sample:
            scb = [A.alloc("scores0", [128, SEQ], F32), A.alloc("scores1", [128, SEQ], F32)]
            work["junk"] = A.alloc("bs_junk", [128, SEQ], BF16)
            work["jw"] = SEQ
            maskf = A.alloc("maskf", [128, SEQ], BF16)
            maskT = A.alloc("maskT", [128, 16, 128], BF16)

            def run_indexer(tt):
                qt = p * 4 + tt
                nk = (qt + 1) * 128
                scores = scb[tt % 2]
                sname = "scores%d" % (tt % 2)
                P.op("pool", lambda e: e.tensor_tensor(out=dgt[:, :, :], in0=identf[:, :].unsqueeze(1).broadcast_to([128, 16, 128]),
                                                       in1=wsb[:, tt, :].unsqueeze(2).broadcast_to([128, 16, 128]), op=ALU.mult),
                     reads=["identf", "wsb"], writes=["dgt"])
                k0 = 0
                while k0 < nk:
                    ncols = min(512, nk - k0)
                    last = (k0 + ncols == nk)
                    indexer_tile(128, lambda h: qiT[(h % 2) * 64:(h % 2) * 64 + 64, h // 2, tt * 128:(tt + 1) * 128],
                                 lambda h, k0=k0, ncols=ncols: kiT[(h % 2) * 64:(h % 2) * 64 + 64, k0:k0 + ncols],
                                 ncols, None, dgt, Rb, scores[:, k0:k0 + ncols], sname, "kiT",
                                 diag_mask_cols=(ncols - 128) if last else None)
                    k0 += ncols

            run_indexer(0)
            for tt in range(4):
                qt = p * 4 + tt
                nk = (qt + 1) * 128
                scores = scb[tt % 2]
                sname = "scores%d" % (tt % 2)
                if tt + 1 < 4:
                    run_indexer(tt + 1)
                if qt >= 2:
                    bisect(scores, sname, 128, nk, theta, work)
                else:
                    P.op("dve", lambda e: e.memset(theta[:, :], NEG * 0.5), writes=["theta"])
                P.op("dve", lambda e, nk=nk, scores=scores: e.tensor_scalar(out=maskf[:, 0:nk], in0=scores[:, 0:nk], scalar1=theta[:, 0:1], scalar2=None, op0=ALU.is_gt),
                     reads=[sname, "theta"], writes=["maskf"])
                nk8 = qt + 1
                mtb = ps["psD"][:, :].bitcast(BF16)
                for b4 in range((nk8 + 3) // 4):
                    n4 = min(4, nk8 - b4 * 4)
                    fns = [lambda e, i=i, b4=b4: e.transpose(out=mtb[:, i * 128:(i + 1) * 128], in_=maskf[:, (b4 * 4 + i) * 128:(b4 * 4 + i + 1) * 128],
                                                             identity=identb[:, :]) for i in range(n4)]
                    P.op("pe", seq(fns), reads=["maskf", "identb"], writes=["psD"])
                    P.op("act", lambda e, b4=b4, n4=n4: e.copy(out=maskT[:, b4 * 4:b4 * 4 + n4, :], in_=mtb[:, 0:n4 * 128].rearrange("p (i q) -> p i q", i=n4)),
                         reads=["psD"], writes=["maskT"])
                for n in range(4):
                    for k8 in range(nk8):
                        sb = ["psE", "psF"][k8 % 2]
                        P.op("pe", mm(ps[sb][:, :], KT[:, n, k8 * 128:(k8 + 1) * 128], qT[:, 4 * n:4 * n + 4, tt * 128:(tt + 1) * 128], True, True),
                             reads=["KT", "qT"], writes=[sb])
                        E = Eb[k8 % 2]
                        en = "Eb%d" % (k8 % 2)
                        Pt = Pb[k8 % 2]
                        pn = "Pb%d" % (k8 % 2)
                        P.op("act", lambda e, E=E, sb=sb: e.activation(out=E[:, :], in_=ps[sb][:, :], func=AF.Exp, scale=scale), reads=[sb], writes=[en])
                        P.op("dve", lambda e, E=E, Pt=Pt, k8=k8: e.tensor_tensor(out=Pt[:, :].rearrange("p (g q) -> p g q", g=4),
                                                                                in0=E[:, :].rearrange("p (g q) -> p g q", g=4),
                                                                                in1=maskT[:, k8, :].unsqueeze(1).broadcast_to([128, 4, 128]), op=ALU.mult),
                             reads=[en, "maskT"], writes=[pn])
                        P.op("pe", seq([mm(ps["psG"][:, :], Vb[:, k8, n * 128:(n + 1) * 128], Pt[:, :], k8 == 0, k8 == nk8 - 1),
                                        mm(ps["psH"][:, :], onesb[:, :], Pt[:, :], k8 == 0, k8 == nk8 - 1)]),
                             reads=[pn, "Vb", "onesb"], writes=["psG", "psH"])
                    P.op("dve", lambda e: e.reciprocal(out=rden[:, :], in_=ps["psH"][:, :]), reads=["psH"], writes=["rden"])
                    P.op("dve", lambda e, n=n, tt=tt: e.tensor_tensor(out=xbf[:, 4 * n:4 * n + 4, tt * 128:(tt + 1) * 128],
                                                                      in0=ps["psG"][:, :].rearrange("p (g q) -> p g q", g=4),
                                                                      in1=rden[:, :].rearrange("p (g q) -> p g q", g=4), op=ALU.mult),
                         reads=["psG", "rden"], writes=["xbf"])
        else:
            sample_attn(j, qT, KT, Vb, kiT, qiT, wsb, dgt, Rb, Eb, Pb, rden, theta, work, scale)
        P.barrier()
        ckpt('attn')
        A.reset(m)
        out_proj_ln(N, "w_out", j, xbf, "xbf", 16, 8 + i_layer, 12 + i_layer)
        ckpt('attn_out')

    def sample_attn(j, qT, KT, Vb, kiT, qiT, wsb, dgt, Rb, Eb, Pb, rden, theta, work, scale):
        NKS = PAST + 4
        scores = A.alloc("scores", [4, NKS + 12], F32)
        work["junk"] = A.alloc("bs_junk", [4, 4100], BF16)
        work["jw"] = 4100
        maskc = A.alloc("maskc", [4, 512], BF16)
        maskT = A.alloc("maskT", [128, 129, 4], BF16)
        G = A.alloc("G", [128, 32, 64], F32)
        kiq = A.alloc("kiq", [64, 4096], BF16)
        ptc = A.alloc("ptc", [128, 1], I32)
        kpg = [A.alloc("kpg0", [128, 512], F32), A.alloc("kpg1", [128, 512], F32)]
        vpg = [A.alloc("vpg0", [128, 512], F32), A.alloc("vpg1", [128, 512], F32)]
        ktb = A.alloc("ktb", [128, 4, 128], BF16)
        vpb = A.alloc("vpb", [128, 512], BF16)
        oacc = A.alloc("oacc", [128, 128], F32)
        P.dma("sp", lambda e: e.dma_start(out=ptc[:, :], in_=pt.unsqueeze(1)), writes=["ptc"])
        A4 = A.alloc("A4", [4, 16, 4], BF16)
        Wsel = A.alloc("Wsel", [64, 4], BF16)
        P.op("dve", lambda e: e.tensor_tensor(out=A4[:, :, :], in0=identf[0:4, 0:4].unsqueeze(1).broadcast_to([4, 16, 4]),
                                              in1=wsb[0:4, 0, :].unsqueeze(2).broadcast_to([4, 16, 4]), op=ALU.mult),
             reads=["identf", "wsb"], writes=["A4"])
        mtb0 = ps["psD"][:, :].bitcast(BF16)
        P.op("pe", lambda e: e.transpose(out=mtb0[0:64, 0:4], in_=A4[:, :, :].rearrange("p h t -> p (h t)"), identity=identb[0:4, 0:4]),
             reads=["A4", "identb"], writes=["psD"])
        P.op("act", lambda e: e.copy(out=Wsel[:, :], in_=mtb0[0:64, 0:4]), reads=["psD"], writes=["Wsel"])
        qi64 = qiT[0:64, :, :].rearrange("p h t -> p (h t)")

        def packed_tile(ki_rhs, kin_name, ncols, sc_out, bi, mask_new=False):
            bd = ["psA", "psB"][bi % 2]
            R = Rb[bi % 2]
            rn = "Rb%d" % (bi % 2)
            P.op("pe", mm(ps[bd][0:64, 0:ncols], qi64, ki_rhs, True, True), reads=["qiT", kin_name], writes=[bd])
            P.op("act", lambda e: e.activation(out=R[0:64, 0:ncols], in_=ps[bd][0:64, 0:ncols], func=AF.Relu), reads=[bd], writes=[rn])
            P.op("pe", mm(ps["psC"][0:4, 0:ncols], Wsel[:, :], R[0:64, 0:ncols], True, True), reads=[rn, "Wsel"], writes=["psC"])
            if not mask_new:
                P.op("act", lambda e: e.copy(out=sc_out, in_=ps["psC"][0:4, 0:ncols]), reads=["psC"], writes=["scores"])
            else:
                P.op("dve", lambda e: e.tensor_tensor(out=sc_out, in0=ps["psC"][0:4, 0:ncols], in1=cmask[0:4, 0:ncols], op=ALU.add),
                     reads=["psC", "cmask"], writes=["scores"])

        ptf = A.alloc("ptf", [128, 1], F32)
        idxqf = A.alloc("idxqf", [128, 4], F32)
        idxq = A.alloc("idxq", [128, 4], I32)
        P.op("dve", lambda e: e.tensor_copy(out=ptf[:, :], in_=ptc[:, :]), reads=["ptc"], writes=["ptf"])
        for q in range(4):
            P.op("dve", lambda e, q=q: e.tensor_scalar(out=idxqf[:, q:q + 1], in0=ptf[:, :], scalar1=4.0, scalar2=float(j * NPOOL * 4 + q),
                                                       op0=ALU.mult, op1=ALU.add), reads=["ptf"], writes=["idxqf"])
        P.op("dve", lambda e: e.tensor_copy(out=idxq[:, :], in_=idxqf[:, :]), reads=["idxqf"], writes=["idxq"])
        for rq4 in range(4):
            P.dma("pool", lambda e, rq4=rq4: e.indirect_dma_start(out=G[:, :, :].rearrange("p r d -> p (r d)"), out_offset=None,
                                                                   in_=cki[:, :],
                                                                   in_offset=bass.IndirectOffsetOnAxis(ap=idxq[:, rq4:rq4 + 1], axis=0)),
                  reads=["idxq"], writes=["G"])
            for r4 in range(8):
                bank = ["psE", "psF"][r4 % 2]
                fns = [lambda e, i=i, r4=r4: e.transpose(out=ps[bank][0:64, i * 128:(i + 1) * 128], in_=G[:, r4 * 4 + i, :], identity=identf[:, :]) for i in range(4)]
                P.op("pe", seq(fns), reads=["G", "identf"], writes=[bank])
                P.op("dve", lambda e, r4=r4, bank=bank: e.tensor_copy(out=kiq[0:64, r4 * 512:(r4 + 1) * 512], in_=ps[bank][0:64, :]), reads=[bank], writes=["kiq"])
            for kt in range(8):
                packed_tile(kiq[0:64, kt * 512:(kt + 1) * 512], "kiq", 512,
                            scores[0:4, rq4 * 4096 + kt * 512: rq4 * 4096 + (kt + 1) * 512], kt)
        packed_tile(kiT[0:64, 0:4], "kiT", 4, scores[0:4, PAST:PAST + 4], 0, mask_new=True)
        bisect(scores, "scores", 4, NKS, theta, work)
        mtb = ps["psD"][:, :].bitcast(BF16)
        for kt5 in range(33):
            ncols = 512 if kt5 < 32 else 4
            c0 = kt5 * 512
            P.op("dve", lambda e, c0=c0, ncols=ncols: e.tensor_scalar(out=maskc[0:4, 0:ncols], in0=scores[0:4, c0:c0 + ncols], scalar1=theta[0:4, 0:1],
                                                                      scalar2=None, op0=ALU.is_gt), reads=["scores", "theta"], writes=["maskc"])
            nsub = (ncols + 127) // 128
            fns = [lambda e, i=i, ncols=ncols: e.transpose(out=mtb[0:min(128, ncols - i * 128), i * 4:i * 4 + 4], in_=maskc[0:4, i * 128:min((i + 1) * 128, ncols)],
                                                           identity=identb[0:4, 0:4]) for i in range(nsub)]
            P.op("pe", seq(fns), reads=["maskc", "identb"], writes=["psD"])
            rows = 128 if kt5 < 32 else 4
            P.op("act", lambda e, kt5=kt5, nsub=nsub, rows=rows: e.copy(out=maskT[0:rows, kt5 * 4:kt5 * 4 + nsub, :],
                                                                        in_=mtb[0:rows, 0:nsub * 4].rearrange("p (i q) -> p i q", i=nsub)),
                 reads=["psD"], writes=["maskT"])
        P.op("dve", lambda e: e.memset(oacc[:, :], 0.0), writes=["oacc"])
        idxT = A.alloc("idxT", [128, 128], I32)
        idxTf = A.alloc("idxTf", [128, 128], F32)
        rowi = A.alloc("rowi", [128, 128], F32)
        P.dma("sp", lambda e: e.dma_start(out=rowi[:, :], in_=c_iota[:, :]), writes=["rowi"])
        P.op("dve", lambda e: e.tensor_scalar(out=idxTf[:, 0:1], in0=ptf[:, :], scalar1=128.0, scalar2=float(j * NPOOL * 128), op0=ALU.mult, op1=ALU.add),
             reads=["ptf"], writes=["idxTf"])
        P.op("dve", lambda e: e.tensor_scalar(out=rowi[:, :], in0=rowi[:, :], scalar1=idxTf[:, 0:1], scalar2=None, op0=ALU.add), reads=["rowi", "idxTf"], writes=["rowi"])
        P.op("dve", lambda e: e.tensor_copy(out=idxT[:, :], in_=rowi[:, :]), reads=["rowi"], writes=["idxT"])
        ckj = ck
        cvj = cv
        for r in range(129):
            newt = (r == 128)
            rows = 128 if not newt else 4
            kp = kpg[r % 2]; kpn = "kpg%d" % (r % 2)
            vp = vpg[r % 2]; vpn = "vpg%d" % (r % 2)
            if not newt:
                P.dma("pool", lambda e, r=r, kp=kp: e.indirect_dma_start(out=kp[:, :], out_offset=None, in_=ckj[:, :],
                                                                          in_offset=bass.IndirectOffsetOnAxis(ap=idxT[:, r:r + 1], axis=0)),
                      reads=["idxT"], writes=[kpn])
                P.dma("pool", lambda e, r=r, vp=vp: e.indirect_dma_start(out=vp[:, :], out_offset=None, in_=cvj[:, :],
                                                                          in_offset=bass.IndirectOffsetOnAxis(ap=idxT[:, r:r + 1], axis=0)),
                      reads=["idxT"], writes=[vpn])
                fns = [lambda e, n=n, kp=kp: e.transpose(out=ps["psE"][:, n * 128:(n + 1) * 128], in_=kp[:, n * 128:(n + 1) * 128], identity=identf[:, :]) for n in range(4)]
                P.op("pe", seq(fns), reads=[kpn, "identf"], writes=["psE"])
                P.op("act", lambda e: e.copy(out=ktb[:, :, :].rearrange("p n k -> p (n k)"), in_=ps["psE"][:, :]), reads=["psE"], writes=["ktb"])
                P.op("pool", lambda e, vp=vp: e.tensor_copy(out=vpb[:, :], in_=vp[:, :]), reads=[vpn], writes=["vpb"])
                mt_idx = (r // 32) * 32 + (r % 32)
                KTn = lambda n: ktb[:, n, :]
                Vn = lambda n: vpb[:, n * 128:(n + 1) * 128]
                vname, kname = "vpb", "ktb"
            else:
                mt_idx = 128
                KTn = lambda n: KT[:, n, 0:4]
                Vn = lambda n: Vb[0:4, 0, n * 128:(n + 1) * 128]
                vname, kname = "Vb", "KT"
            fns = [mm(ps["psF"][0:rows, n * 16:(n + 1) * 16], KTn(n), qT[:, 4 * n:4 * n + 4, 0:4], True, True) for n in range(4)]
            P.op("pe", seq(fns), reads=[kname, "qT"], writes=["psF"])
            E = Eb[r % 2]; en = "Eb%d" % (r % 2)
            Pt = Pb[r % 2]; pn = "Pb%d" % (r % 2)
            P.op("act", lambda e, E=E, rows=rows: e.activation(out=E[0:rows, 0:64], in_=ps["psF"][0:rows, 0:64], func=AF.Exp, scale=scale), reads=["psF"], writes=[en])
            P.op("dve", lambda e, E=E, Pt=Pt, rows=rows, mt_idx=mt_idx: e.tensor_tensor(out=Pt[0:rows, 0:64].rearrange("p (a q) -> p a q", q=4),
                                                                                       in0=E[0:rows, 0:64].rearrange("p (a q) -> p a q", q=4),
                                                                                       in1=maskT[0:rows, mt_idx, :].unsqueeze(1).broadcast_to([rows, 16, 4]), op=ALU.mult),
                 reads=[en, "maskT"], writes=[pn])
            fns = [mm(ps["psG"][:, n * 16:(n + 1) * 16], Vn(n), Pt[0:rows, n * 16:(n + 1) * 16], True, True) for n in range(4)]
            fns.append(mm(ps["psG"][:, 64:128], onesb[0:rows, :], Pt[0:rows, 0:64], True, True))
            P.op("pe", seq(fns), reads=[pn, vname, "onesb"], writes=["psG"])
            P.op("dve", lambda e: e.tensor_tensor(out=oacc[:, :], in0=oacc[:, :], in1=ps["psG"][:, 0:128], op=ALU.add), reads=["psG", "oacc"], writes=["oacc"])
        P.op("dve", lambda e: e.reciprocal(out=rden[:, 0:64], in_=oacc[:, 64:128]), reads=["oacc"], writes=["rden"])
        P.op("dve", lambda e: e.tensor_tensor(out=xbf[:, :, 0:4], in0=oacc[:, 0:64].rearrange("p (h q) -> p h q", q=4),
                                              in1=rden[:, 0:64].rearrange("p (h q) -> p h q", q=4), op=ALU.mult), reads=["oacc", "rden"], writes=["xbf"])

    class StopBuild(Exception):
        pass

    CUR = {}

    def ckpt(name):
        if cfg.get("stop") == name:
            P.barrier()
            A.reset(CUR["mark"])
            store_y(CUR["p"], CUR["N"])
            raise StopBuild()

    try:
      for p in pass_list:
          sample = (p == 4)
          N = 4 if sample else 512
          if sample:
              P.barrier()
              A.reset(cmark)
              xres = A.alloc("xres", [128, NCH, 4], F32)
              xbf = A.alloc("xbf", [128, NCH, 4], BF16)
              rq = A.alloc("rq", [128, 2, 512], F32)
              ri = A.alloc("ri", [128, 2, 512], F32)
              ttq = A.alloc("ttq", [128, 4, 48], F32)
              ghalo = A.alloc("ghalo", [128, 4, NFF, 2], F32)
              ahalo = A.alloc("ahalo", [128, 2, NCH, 30], BF16)
          P.dma("sp", lambda e, p=p: e.dma_start(out=rq[:, :, :], in_=c_ropeq[p].rearrange("c p n -> p c n")), writes=["rq"])
          P.dma("sp", lambda e, p=p: e.dma_start(out=ri[:, :, :], in_=c_ropei[p].rearrange("c p n -> p c n")), writes=["ri"])
          r0 = p * 512 if not sample else SEQ
          if not sample:
              P.dma("sp", lambda e, r0=r0: e.dma_start(out=ttq[:, :, :], in_=c_ttab[r0:r0 + 512, :].rearrange("(t p) c -> p t c", p=128)), writes=["ttq"])
          else:
              P.dma("sp", lambda e: e.dma_start(out=ttq[:, 0, :], in_=c_ttab[SEQ:SEQ + 128, :]), writes=["ttq"])
          if p == 0:
              P.op("pool", lambda e: e.memset(ghalo[:, :, :, :].rearrange("p a b c -> p (a b c)"), 0.0), writes=["ghalo"])
              P.op("pool", lambda e: e.memset(ahalo[:, :, :, :].rearrange("p a b c -> p (a b c)"), 0.0), writes=["ahalo"])
          if sample:
              for l in range(4):
                  for t in range(2):
                      sv = stf[l, t, :].rearrange("(f q) -> q f", q=128)
                      for q4 in range(4):
                          P.dma("sp", lambda e, l=l, q4=q4, sv=sv, t=t: e.dma_start(out=ghalo[:, l, q4 * 11:(q4 + 1) * 11, t], in_=sv[:, q4 * 11:(q4 + 1) * 11],
                                                                                    allow_slow_non_contiguous=True), writes=["ghalo"])
              m = A.mark()
              stg = A.alloc("stage", [64, DFF], F32)
              atmp = A.alloc("atmp", [128, NCH, 30], F32)
              names[id(atmp)] = "atmp"
              for j in range(2):
                  rows_to_fm(stc[j, :, :], 30, D, atmp, stg)
                  P.op("dve", lambda e, j=j: e.tensor_copy(out=ahalo[:, j, :, :], in_=atmp[:, :, :]), reads=["atmp"], writes=["ahalo"])
              P.barrier()
              A.reset(m)
          CUR.update(p=p, N=N, mark=A.mark())
          load_x(p, N)
          ckpt('load_x')
          for i_layer in range(nlayers):
              j = i_layer // 2
              if i_layer % 2 == 0:
                  attn_layer(j, i_layer, p, N)
              else:
                  conv_layer(j, i_layer, p, N)
              ffn_layer(i_layer, p, N)
          store_y(p, N)
    except StopBuild:
        pass
    if DRY:
        return wrecs
    P.barrier()
    P.emit()
    return nc


def _consts():
    theta = np.float32(500000.0)
    invq = (theta ** (-np.arange(0, 32, 2, dtype=np.float32) / np.float32(32))).astype(np.float32)
    invi = (theta ** (-np.arange(0, 16, 2, dtype=np.float32) / np.float32(16))).astype(np.float32)
    pos_all = np.concatenate([np.arange(SEQ, dtype=np.float32), PAST + np.arange(128, dtype=np.float32)])
    angq = pos_all[:, None] * invq[None, :]
    angi = pos_all[:, None] * invi[None, :]
    ttab = np.concatenate([np.cos(angq), np.sin(angq), np.cos(angi), np.sin(angi)], axis=1).astype(np.float32)
    ropeq = np.zeros((5, 2, 128, 512), np.float32)
    ropei = np.zeros((5, 2, 128, 512), np.float32)
    ropeq[:, 0] = 1.0
    ropei[:, 0] = 1.0
    for p in range(5):
        rows = np.arange(p * 512, p * 512 + 512) if p < 4 else np.concatenate([np.arange(SEQ, SEQ + 128)] * 4)
        cq, sq = np.cos(angq[rows]).T, np.sin(angq[rows]).T
        ci, si = np.cos(angi[rows]).T, np.sin(angi[rows]).T
        ropeq[p, 0, 0:16] = cq; ropeq[p, 0, 16:32] = cq
        ropeq[p, 1, 0:16] = sq; ropeq[p, 1, 16:32] = sq
        for o in (0, 64):
            ropei[p, 0, o:o + 8] = ci; ropei[p, 0, o + 8:o + 16] = ci
            ropei[p, 1, o:o + 8] = si; ropei[p, 1, o + 8:o + 16] = si
    pq = np.zeros((128, 128), np.float32)
    for m in range(16):
        pq[m + 16, m] = -1.0
        pq[m, m + 16] = 1.0
    pi = np.zeros((128, 128), np.float32)
    for o in (0, 64):
        for m in range(8):
            pi[o + m + 8, o + m] = -1.0
            pi[o + m, o + m + 8] = 1.0
    cm = np.where(np.arange(128)[None, :] <= np.arange(128)[:, None], 0.0, NEG).astype(np.float32)
    return dict(c_ident=np.eye(128, dtype=np.float32), c_pq=pq, c_pi=pi, c_ropeq=ropeq, c_ropei=ropei,
                c_ttab=np.ascontiguousarray(ttab[:SEQ + 128]), c_cmask=cm,
                c_iota=np.ascontiguousarray(np.broadcast_to(np.arange(128, dtype=np.float32)[None, :], (128, 128))))


def make_in_maps(inp, ncores=8):
    f = lambda a: np.ascontiguousarray(np.asarray(a, dtype=np.float32))
    cst = _consts()
    vecD = np.concatenate([f(inp["b_dw"]), f(inp["ln_conv_g"]), f(inp["ln_conv_b"]), f(inp["b_pw2"]),
                           f(inp["ln_mix_g"]), f(inp["ln_mix_b"]), f(inp["ln_ffn_g"]), f(inp["ln_ffn_b"]),
                           f(inp["b_pw1"]).reshape(4, D)], axis=0)
    vecF = np.concatenate([np.concatenate([f(inp["w_ffn_conv"])[l], f(inp["b_ffn_conv"])[l][None]], axis=0) for l in range(4)], axis=0)
    shared = dict(
        ck=f(inp["cache_k"]).reshape(-1, 512), cv=f(inp["cache_v"]).reshape(-1, 512),
        cki=f(inp["cache_kidx"]).reshape(-1, 2048),
        w_in=f(inp["w_attn_in"]), w_out=f(inp["w_attn_out"]), w_pw1=f(inp["w_pw1"]), w_pw2=f(inp["w_pw2"]),
        w_g=f(inp["w_ffn_gate"]), w_u=f(inp["w_ffn_up"]), w_d=f(inp["w_ffn_down"]),
        vecD=np.ascontiguousarray(vecD), vecF=np.ascontiguousarray(vecF), wdw=f(inp["w_dw"]).reshape(62, D), **cst)
    maps = []
    for c in range(ncores):
        m = dict(shared)
        m["xp"] = f(inp["x_prompt"])[c % 4]
        m["xs"] = f(inp["x_sample"])[c]
        m["stc"] = np.ascontiguousarray(f(inp["state_conv"])[:, c])
        m["stf"] = np.ascontiguousarray(f(inp["state_ffn"])[:, c])
        m["pt"] = np.ascontiguousarray(np.asarray(inp["page_table"], dtype=np.int32)[c])
        maps.append(m)
    return maps


def assemble(res):
    r = res
    y_p = np.stack([r[c]["y_p"] for c in range(4)])
    y_s = np.stack([r[c]["y_s"] for c in range(8)])
    k_p = np.stack([r[c]["k_p"] for c in range(4)], axis=1).reshape(2, 4, SEQ, 4, 128)
    v_p = np.stack([r[c]["v_p"] for c in range(4)], axis=1).reshape(2, 4, SEQ, 4, 128)
    ki_p = np.stack([r[c]["ki_p"] for c in range(4)], axis=1)
    conv_p = np.stack([r[c]["conv_p"] for c in range(4)], axis=1)
    ffn_p = np.stack([r[c]["ffn_p"] for c in range(4)], axis=1)
    k_s = np.stack([r[c]["k_s"] for c in range(8)], axis=1).reshape(2, 8, 4, 4, 128)
    v_s = np.stack([r[c]["v_s"] for c in range(8)], axis=1).reshape(2, 8, 4, 4, 128)
    ki_s = np.stack([r[c]["ki_s"] for c in range(8)], axis=1)
    conv_s = np.stack([r[c]["conv_s"] for c in range(8)], axis=1)
    ffn_s = np.stack([r[c]["ffn_s"] for c in range(8)], axis=1)
    outs = (y_p, y_s, k_p, v_p, ki_p, conv_p, ffn_p, k_s, v_s, ki_s, conv_s, ffn_s)
    return tuple(np.ascontiguousarray(o, dtype=np.float32) for o in outs)


def kernel(**inputs):
    nc = build({})
    maps = make_in_maps(inputs)
    res = run_bass_kernel_spmd(nc, maps, core_ids=list(range(8)))
    return assemble(res.results)
```

```python
import numpy as np
import concourse.bass as bass
import concourse.mybir as mybir
from concourse.bass_utils import run_bass_kernel_spmd

F32 = mybir.dt.float32
BF16 = mybir.dt.bfloat16
I32 = mybir.dt.int32
ALU = mybir.AluOpType
AF = mybir.ActivationFunctionType
AX = mybir.AxisListType

ENGS = ("pe", "act", "dve", "pool", "sp")
EPOCH = 12000
NDMA = 24

D = 2048
NCH = 16
DFF = 5632
NFF = 44
SEQ = 2048
PAST = 16384
NPG = 128
ALPHA = float((2 * 4) ** 0.25)
EPS = 1e-5
NEG = -2048.0
IN_COLS = 4176


class Prog:
    def __init__(self, nc, dry=False):
        self.nc = nc
        self.dry = dry
        self.q = {e: [] for e in ENGS}
        self.cnt = {e: 0 for e in ENGS}
        self.last_w = {}
        self.readers = {}
        self.seen = {e: {} for e in ENGS}
        self.dma_i = 0
        self.dma_val = [0] * NDMA
        self.sems = {}
        self.h = {"pe": nc.tensor, "act": nc.scalar, "dve": nc.vector, "pool": nc.gpsimd, "sp": nc.sync}

    def sem(self, key):
        if key not in self.sems:
            self.sems[key] = self.nc.alloc_semaphore("s_%s_%s" % (key[0], key[1]))
        return self.sems[key]

    def _ev_sem(self, ev):
        return self.sem((ev[0], ev[1])), ev[2]

    def _need(self, eng, ev):
        key = (ev[0], ev[1])
        if self.seen[eng].get(key, 0) >= ev[2]:
            return False
        self.seen[eng][key] = ev[2]
        return True

    def _deps(self, eng, reads, writes):
        evs = []
        for r in reads:
            if r in self.last_w:
                evs.append(self.last_w[r])
        for w in writes:
            if w in self.last_w:
                evs.append(self.last_w[w])
            evs.extend(self.readers.get(w, ()))
        out = []
        for ev in evs:
            if ev[3] == eng and eng == "pe":
                continue
            if self._need(eng, ev):
                out.append(ev)
        return out

    def _record(self, ev, reads, writes):
        for r in reads:
            self.readers.setdefault(r, []).append(ev)
        for w in writes:
            self.last_w[w] = ev
            self.readers[w] = []

    def op(self, eng, fn, reads=(), writes=()):
        if self.dry:
            return
        psr = [r for r in reads if r.startswith("ps")]
        if psr:
            writes = list(writes) + [r for r in psr if r not in writes]
            reads = [r for r in reads if not r.startswith("ps")]
        deps = self._deps(eng, reads, writes)
        c = self.cnt[eng]
        ep, val = c // EPOCH, c % EPOCH + 1
        self.cnt[eng] = c + 1
        ev = (eng, ep, val, eng)
        waits = [self._ev_sem(d) for d in deps]
        mysem = self.sem((eng, ep))

        def run(e):
            for s, v in waits:
                e.wait_ge(s, v)
            ins = fn(e)
            ins.then_inc(mysem, 1)

        run(self.h[eng])
        self._record(ev, reads, writes)

    def dma(self, qeng, fn, reads=(), writes=()):
        if self.dry:
            return
        slot = self.dma_i % NDMA
        self.dma_i += 1
        deps = self._deps(qeng, reads, writes)
        prev = self.dma_val[slot]
        if prev > 0:
            pe = ("dma", slot, prev, "dma")
            if self._need(qeng, pe):
                deps.append(pe)
        val = prev + 16
        self.dma_val[slot] = val
        ev = ("dma", slot, val, "dma")
        waits = [self._ev_sem(d) for d in deps]
        mysem = self.sem(("dma", slot))

        def run(e):
            for s, v in waits:
                e.wait_ge(s, v)
            ins = fn(e)
            ins.then_inc(mysem, 16)

        run(self.h[qeng])
        self._record(ev, reads, writes)

    def barrier(self):
        if self.dry:
            return
        evs = []
        for e in ENGS:
            c = self.cnt[e]
            if c > 0:
                evs.append((e, (c - 1) // EPOCH, (c - 1) % EPOCH + 1, e))
        for s in range(NDMA):
            if self.dma_val[s] > 0:
                evs.append(("dma", s, self.dma_val[s], "dma"))
        for e in ENGS:
            waits = []
            for ev in evs:
                if ev[3] == e:
                    continue
                if self._need(e, ev):
                    waits.append(self._ev_sem(ev))
            if waits:
                def run(h, waits=waits):
                    for s, v in waits:
                        h.wait_ge(s, v)
                run(self.h[e])

    def emit(self):
        pass


def mm(out, lhsT, rhs, st, sp):
    return lambda e: e.matmul(out, lhsT=lhsT, rhs=rhs, start=st, stop=sp)


def seq(fns):
    def run(e):
        ins = None
        for f in fns:
            ins = f(e)
        return ins
    return run


class Arena:
    def __init__(self, nc):
        self.nc = nc
        rem = nc.sbuf_bytes_remaining
        size = (rem - 512) // 64 * 64
        r = nc.bump_sbuf(size)
        self.base = r[0]
        self.end = r[0] + size
        self.cur = self.base
        self.uid = 0

    def alloc(self, name, shape, dt):
        esz = 4 if dt in (F32, I32) else 2
        n = 1
        for s in shape[1:]:
            n *= s
        nbytes = (n * esz + 63) // 64 * 64
        assert self.cur + nbytes <= self.end, "SBUF arena overflow at %s (%d over)" % (name, self.cur + nbytes - self.end)
        self.uid += 1
        t = self.nc.alloc_sbuf_tensor_at("%s_%d" % (name, self.uid), list(shape), dt, offset=self.cur)
        self.cur += nbytes
        return t

    def mark(self):
        return self.cur

    def reset(self, m):
        self.cur = m


def build(cfg):
    if "_wrec" not in cfg and not cfg.get("_dry"):
        recs = build(dict(cfg, _dry=True))
        cfg = dict(cfg, _wrec=recs)
    DRY = bool(cfg.get("_dry"))
    WREC = cfg.get("_wrec")
    npass = cfg.get("npass", 5)
    nlayers = cfg.get("nlayers", 4)
    pass_list = cfg.get("passes", list(range(npass)))
    nc = bass.Bass("TRN2", target_bir_lowering=False)
    P = Prog(nc, dry=DRY)

    def din(name, shape, dt=F32):
        return nc.dram_tensor(name, list(shape), dt, kind="ExternalInput").ap()

    def dout(name, shape):
        return nc.dram_tensor(name, list(shape), F32, kind="ExternalOutput").ap()

    xp = din("xp", [SEQ, D]); xs = din("xs", [4, D])
    NPOOL = cfg.get("npool", 1280)
    ck = din("ck", [2 * NPOOL * 128, 512]); cv = din("cv", [2 * NPOOL * 128, 512]); cki = din("cki", [2 * NPOOL * 4, 2048])
    stc = din("stc", [2, 30, D]); stf = din("stf", [4, 2, DFF]); pt = din("pt", [128], I32)
    w_in = din("w_in", [2, D, IN_COLS]); w_out = din("w_out", [2, D, D]); w_pw1 = din("w_pw1", [2, D, 2 * D])
    w_pw2 = din("w_pw2", [2, D, D]); w_g = din("w_g", [4, D, DFF]); w_u = din("w_u", [4, D, DFF]); w_d = din("w_d", [4, DFF, D])
    WD = {"w_in": w_in, "w_out": w_out, "w_pw1": w_pw1, "w_pw2": w_pw2, "w_g": w_g, "w_u": w_u, "w_d": w_d}
    vecD = din("vecD", [28, D]); vecF = din("vecF", [16, DFF]); wdw = din("wdw", [62, D])
    c_ident = din("c_ident", [128, 128]); c_pq = din("c_pq", [128, 128]); c_pi = din("c_pi", [128, 128])
    c_ropeq = din("c_ropeq", [5, 2, 128, 512]); c_ropei = din("c_ropei", [5, 2, 128, 512])
    c_ttab = din("c_ttab", [SEQ + 128, 48]); c_cmask = din("c_cmask", [128, 128]); c_iota = din("c_iota", [128, 128])

    y_p = dout("y_p", [SEQ, D]); y_s = dout("y_s", [4, D])
    k_p = dout("k_p", [2, SEQ, 512]); v_p = dout("v_p", [2, SEQ, 512]); ki_p = dout("ki_p", [2, SEQ, 64])
    conv_p = dout("conv_p", [2, 30, D]); ffn_p = dout("ffn_p", [4, 2, DFF])
    k_s = dout("k_s", [2, 4, 512]); v_s = dout("v_s", [2, 4, 512]); ki_s = dout("ki_s", [2, 4, 64])
    conv_s = dout("conv_s", [2, 30, D]); ffn_s = dout("ffn_s", [4, 2, DFF])

    scrKT = nc.dram_tensor("scrKT", [2, 128, 4, SEQ], BF16).ap()
    scrV = nc.dram_tensor("scrV", [2, 128, 16, 512], BF16).ap()
    scrKI = nc.dram_tensor("scrKI", [2, 128, SEQ], BF16).ap()

    A = Arena(nc)
    PSN = ["psA", "psB", "psC", "psD", "psE", "psF", "psG", "psH"]
    ps = {n: nc.alloc_psum_tensor(n, [128, 512], F32) for n in PSN}

    identf = A.alloc("identf", [128, 128], F32)
    identb = A.alloc("identb", [128, 128], BF16)
    onesf = A.alloc("onesf", [128, 128], F32)
    onesb = A.alloc("onesb", [128, 128], BF16)
    pqb = A.alloc("pqb", [128, 128], BF16)
    pib = A.alloc("pib", [128, 128], BF16)
    cmask = A.alloc("cmask", [128, 128], F32)
    epsT = A.alloc("epsT", [128, 1], F32)
    cmaskp = A.alloc("cmaskp", [128, 512], BF16)
    vD = A.alloc("vD", [128, NCH, 28], F32)
    vF = A.alloc("vF", [128, NFF, 16], F32)
    wdwT = A.alloc("wdwT", [128, NCH, 62], F32)
    Wt = [A.alloc("W0", [128, 8192], BF16), A.alloc("W1", [128, 8192], BF16)]
    cmark = A.mark()
    xres = A.alloc("xres", [128, NCH, 512], F32)
    xbf = A.alloc("xbf", [128, NCH, 512], BF16)
    rq = A.alloc("rq", [128, 2, 512], F32)
    ri = A.alloc("ri", [128, 2, 512], F32)
    ttq = A.alloc("ttq", [128, 4, 48], F32)
    ghalo = A.alloc("ghalo", [128, 4, NFF, 2], F32)
    ahalo = A.alloc("ahalo", [128, 2, NCH, 30], BF16)
    pmark = A.mark()

    wst = {"ptr": 0, "issued": 0}
    wrecs = []

    WS = {}
    if not DRY:
        WS["T"] = len(WREC) // max(1, len(pass_list))
        WS["half"] = 96
        WS["scrs"] = [nc.dram_tensor("wscr%d" % q, [WS["half"], 128, 8192], BF16).ap() for q in range((max(1, WS["T"]) + 95) // 96)]
        WS["cache"] = (len(WREC) == WS["T"] * len(pass_list)) and len(pass_list) > 1 and not cfg.get("nowcache")

    def _wissue(i):
        rec = WREC[i]
        sl = i % 2
        wn_ = "W%d" % sl
        T_ = WS["T"]
        n_ = 4096 if rec[0] == "kiwi" else rec[3] * rec[5]
        if WS["cache"] and i >= T_:
            assert WREC[i % T_] == rec
            src_ = WS["scrs"][(i % T_) // 96][(i % T_) % 96]
            P.dma("sp", lambda e: e.dma_start(out=Wt[sl][:, 0:n_], in_=src_[:, 0:n_]), reads=["wscr%d" % (i % T_)], writes=[wn_])
            return
        if rec[0] == "kiwi":
            wj_ = w_in[rec[1]]
            vkw_ = Wt[sl][:, 0:16 * 256].rearrange("p (k n) -> p k n", k=16)
            srcki = wv(wj_, 0, 16, 4096, 64)
            P.dma("pool", lambda e: e.dma_start(out=vkw_[:, :, 0:64], in_=srcki), writes=[wn_])
            P.dma("pool", lambda e: e.dma_start(out=vkw_[:, :, 64:128], in_=srcki), writes=[wn_])
            P.dma("pool", lambda e: e.dma_start(out=vkw_[:, :, 128:208], in_=wv(wj_, 0, 16, 4096, 80)), writes=[wn_])
        else:
            kind, li, k0, kc, c0, ncol = rec
            view_ = Wt[sl][:, 0:kc * ncol].rearrange("p (k n) -> p k n", k=kc)
            src_ = wv(WD[kind][li], k0, kc, c0, ncol)
            P.dma("pool", lambda e: e.dma_start(out=view_, in_=src_), writes=[wn_])
        if WS["cache"]:
            dst_ = WS["scrs"][i // 96][i % 96]
            P.dma("sp", lambda e: e.dma_start(out=dst_[:, 0:n_], in_=Wt[sl][:, 0:n_]), reads=[wn_], writes=["wscr%d" % i])

    def wload(kind, li, k0=0, kc=16, c0=0, ncol=512):
        rec = (kind, li, k0, kc, c0, ncol)
        i = wst["ptr"]
        wst["ptr"] += 1
        wrecs.append(rec)
        sl = i % 2
        if kind == "kiwi":
            view = Wt[sl][:, 0:16 * 256].rearrange("p (k n) -> p k n", k=16)
        else:
            view = Wt[sl][:, 0:kc * ncol].rearrange("p (k n) -> p k n", k=kc)
        if DRY:
            return view, "W%d" % sl
        assert WREC[i] == rec, (i, WREC[i], rec)
        while wst["issued"] < min(i + 2, len(WREC)):
            _wissue(wst["issued"])
            wst["issued"] += 1
        return view, "W%d" % sl

    def wv(w2d, k0, kc, c0, ncol):
        return w2d.rearrange("(k p) n -> p k n", p=128)[:, k0:k0 + kc, c0:c0 + ncol]

    tmpc = A.alloc("tmpc", [128, 128], F32)
    P.dma("sp", lambda e: e.dma_start(out=identf[:, :], in_=c_ident[:, :]), writes=["identf"])
    P.dma("sp", lambda e: e.dma_start(out=cmask[:, :], in_=c_cmask[:, :]), writes=["cmask"])
    P.op("dve", lambda e: e.tensor_copy(out=identb[:, :], in_=identf[:, :]), reads=["identf"], writes=["identb"])
    P.op("dve", lambda e: e.memset(onesf[:, :], 1.0), writes=["onesf"])
    P.op("dve", lambda e: e.memset(onesb[:, :], 1.0), writes=["onesb"])
    P.op("dve", lambda e: e.memset(epsT[:, :], EPS), writes=["epsT"])
    P.op("dve", lambda e: e.memset(cmaskp[:, :], 0.0), writes=["cmaskp"])
    P.op("dve", lambda e: e.tensor_copy(out=cmaskp[:, 384:512], in_=cmask[:, :]), reads=["cmask"], writes=["cmaskp"])
    P.dma("sp", lambda e: e.dma_start(out=tmpc[:, :], in_=c_pq[:, :]), writes=["tmpc"])
    P.op("dve", lambda e: e.tensor_copy(out=pqb[:, :], in_=tmpc[:, :]), reads=["tmpc"], writes=["pqb"])
    P.dma("sp", lambda e: e.dma_start(out=tmpc[:, :], in_=c_pi[:, :]), reads=["tmpc"], writes=["tmpc"])
    P.op("dve", lambda e: e.tensor_copy(out=pib[:, :], in_=tmpc[:, :]), reads=["tmpc"], writes=["pib"])

    def rows_to_fm(rows_dram, R, L, dst, stage):
        P.dma("sp", lambda e: e.dma_start(out=stage[0:R, 0:L], in_=rows_dram), writes=["stage"])
        nchunk = L // 128
        per = 512 // R
        c = 0
        bi = 0
        while c < nchunk:
            n = min(per, nchunk - c)
            bank = ["psA", "psB"][bi % 2]
            bi += 1
            fns = []
            for i in range(n):
                fns.append(lambda e, i=i, c=c: e.transpose(out=ps[bank][:, i * R:(i + 1) * R],
                                                          in_=stage[0:R, (c + i) * 128:(c + i + 1) * 128],
                                                          identity=identf[0:R, 0:R]))
            P.op("pe", seq(fns), reads=["stage", "identf"], writes=[bank])
            P.op("dve", lambda e, c=c, n=n, bank=bank: e.tensor_copy(
                out=dst[:, c:c + n, 0:R], in_=ps[bank][:, 0:n * R].rearrange("p (i r) -> p i r", r=R)),
                reads=[bank], writes=[dst_name(dst)])
            c += n

    names = {}

    def dst_name(t):
        return names[id(t)]

    m0 = A.mark()
    stage = A.alloc("stage", [64, DFF], F32)
    names[id(vD)] = "vD"; names[id(vF)] = "vF"; names[id(wdwT)] = "wdwT"
    rows_to_fm(vecD[:, :], 28, D, vD, stage)
    rows_to_fm(vecF[:, :], 16, DFF, vF, stage)
    rows_to_fm(wdw[:, :], 62, D, wdwT, stage)
    P.barrier()
    A.reset(m0)

    def vcol(row, c):
        return vD[:, c, row:row + 1]

    PS_ROT = {"i": 0}

    def load_x(p, N):
        m = A.mark()
        xtm = A.alloc("xtm", [128, D], F32)
        ntt = (N + 127) // 128
        for tt in range(ntt):
            nt = min(128, N - tt * 128)
            src = xp[p * 512 + tt * 128: p * 512 + tt * 128 + nt, :] if p < 4 else xs[0:nt, :]
            P.dma("sp", lambda e, src=src, nt=nt: e.dma_start(out=xtm[0:nt, :], in_=src), writes=["xtm"])
            for c4 in range(4):
                bank = ["psA", "psB"][c4 % 2]
                fns = [lambda e, i=i, c4=c4, nt=nt: e.transpose(out=ps[bank][:, i * 128:i * 128 + nt],
                                                                 in_=xtm[0:nt, (c4 * 4 + i) * 128:(c4 * 4 + i + 1) * 128],
                                                                 identity=identf[0:nt, 0:nt]) for i in range(4)]
                P.op("pe", seq(fns), reads=["xtm", "identf"], writes=[bank])
                srcv = ps[bank][:, :].rearrange("p (i t) -> p i t", i=4)[:, :, 0:nt]
                P.op("dve", lambda e, c4=c4, tt=tt, nt=nt, srcv=srcv: e.tensor_copy(
                    out=xres[:, c4 * 4:c4 * 4 + 4, tt * 128:tt * 128 + nt], in_=srcv), reads=[bank], writes=["xres"])
                P.op("act", lambda e, c4=c4, tt=tt, nt=nt, srcv=srcv: e.copy(
                    out=xbf[:, c4 * 4:c4 * 4 + 4, tt * 128:tt * 128 + nt], in_=srcv), reads=[bank], writes=["xbf"])
        P.barrier()
        A.reset(m)

    def store_y(p, N):
        m = A.mark()
        ytm = A.alloc("ytm", [128, D], F32)
        ntt = (N + 127) // 128
        for tt in range(ntt):
            nt = min(128, N - tt * 128)
            for c4 in range(4):
                bank = ["psA", "psB"][c4 % 2]
                fns = [lambda e, i=i, c4=c4, nt=nt, tt=tt: e.transpose(out=ps[bank][0:nt, i * 128:(i + 1) * 128],
                                                                        in_=xres[:, c4 * 4 + i, tt * 128:tt * 128 + nt],
                                                                        identity=identf[:, :]) for i in range(4)]
                P.op("pe", seq(fns), reads=["xres", "identf"], writes=[bank])
                P.op("dve", lambda e, c4=c4, nt=nt: e.tensor_copy(out=ytm[0:nt, c4 * 512:(c4 + 1) * 512], in_=ps[bank][0:nt, :]),
                     reads=[bank], writes=["ytm"])
            dst = y_p[p * 512 + tt * 128: p * 512 + tt * 128 + nt, :] if p < 4 else y_s[0:nt, :]
            P.dma("sp", lambda e, dst=dst, nt=nt: e.dma_start(out=dst, in_=ytm[0:nt, :]), reads=["ytm"])
        P.barrier()
        A.reset(m)

    class LN:
        def __init__(self, N, src, resname):
            self.N = N
            self.src = src
            self.res = resname
            self.sq = [A.alloc("lnsq0", [128, 512], F32), A.alloc("lnsq1", [128, 512], F32)]
            self.mean = A.alloc("lnmean", [128, 512], F32)
            self.rstd = A.alloc("lnrstd", [128, 512], F32)
            self.nmr = A.alloc("lnnmr", [128, 512], F32)

        def stats(self, c):
            N = self.N
            sq = self.sq[c % 2]
            sqn = "lnsq%d" % (c % 2)
            z = self.src[:, c, 0:N]
            P.op("act", lambda e: e.activation(out=sq[:, 0:N], in_=z, func=AF.Square), reads=[self.res], writes=[sqn])
            P.op("pe", seq([mm(ps["psG"][:, 0:N], onesf[:, :], z, c == 0, c == NCH - 1),
                            mm(ps["psH"][:, 0:N], onesf[:, :], sq[:, 0:N], c == 0, c == NCH - 1)]),
                 reads=[self.res, sqn, "onesf"], writes=["psG", "psH"])

        def finalize(self):
            N = self.N
            mean, rstd, nmr = self.mean, self.rstd, self.nmr
            P.op("dve", lambda e: e.tensor_scalar(out=mean[:, 0:N], in0=ps["psG"][:, 0:N], scalar1=1.0 / D, scalar2=None, op0=ALU.mult),
                 reads=["psG"], writes=["lnmean"])
            P.op("dve", lambda e: e.tensor_tensor(out=nmr[:, 0:N], in0=mean[:, 0:N], in1=mean[:, 0:N], op=ALU.mult),
                 reads=["lnmean"], writes=["lnnmr"])
            P.op("dve", lambda e: e.scalar_tensor_tensor(out=rstd[:, 0:N], in0=ps["psH"][:, 0:N], scalar=1.0 / D, in1=nmr[:, 0:N],
                                                         op0=ALU.mult, op1=ALU.subtract), reads=["psH", "lnnmr"], writes=["lnrstd"])
            P.op("act", lambda e: e.activation(out=rstd[:, 0:N], in_=rstd[:, 0:N], func=AF.Sqrt, bias=epsT[:, 0:1], scale=1.0),
                 reads=["lnrstd", "epsT"], writes=["lnrstd"])
            P.op("dve", lambda e: e.reciprocal(out=rstd[:, 0:N], in_=rstd[:, 0:N]), reads=["lnrstd"], writes=["lnrstd"])
            P.op("dve", lambda e: e.tensor_tensor(out=nmr[:, 0:N], in0=mean[:, 0:N], in1=rstd[:, 0:N], op=ALU.mult),
                 reads=["lnmean", "lnrstd"], writes=["lnnmr"])

        def apply(self, c, g_ap, b_ap, out_f32=None, out_f32_name=None, out_bf=None, out_bf_name=None, func=None):
            N = self.N
            z = self.src[:, c, 0:N]
            t = self.sq[c % 2]
            tn = "lnsq%d" % (c % 2)
            P.op("dve", lambda e: e.tensor_tensor(out=t[:, 0:N], in0=z, in1=self.rstd[:, 0:N], op=ALU.mult),
                 reads=[self.res, "lnrstd"], writes=[tn])
            P.op("dve", lambda e: e.tensor_tensor(out=t[:, 0:N], in0=t[:, 0:N], in1=self.nmr[:, 0:N], op=ALU.subtract),
                 reads=[tn, "lnnmr"], writes=[tn])
            if out_f32 is not None:
                P.op("act", lambda e: e.activation(out=out_f32, in_=t[:, 0:N], func=AF.Identity, bias=b_ap, scale=g_ap),
                     reads=[tn, "vD"], writes=[out_f32_name])
                if out_bf is not None:
                    P.op("pool", lambda e: e.tensor_copy(out=out_bf, in_=out_f32), reads=[out_f32_name], writes=[out_bf_name])
            else:
                P.op("act", lambda e: e.activation(out=out_bf, in_=t[:, 0:N], func=func, bias=b_ap, scale=g_ap),
                     reads=[tn, "vD"], writes=[out_bf_name])

    def out_proj_ln(N, wkind, wli, rhs_t, rhs_name, nk, g_row, b_row, bias_row=None):
        m = A.mark()
        ln = LN(N, xres, "xres")
        btmp = A.alloc("btmp", [128, 512], F32)
        for g4 in range(4):
            k0 = 0
            banks = ["psA", "psB", "psC", "psD"]
            while k0 < nk:
                kc = min(16, nk - k0)
                view, wn = wload(wkind, wli, k0, kc, g4 * 512, 512)
                for i in range(4):
                    fns = [mm(ps[banks[i]][:, 0:N], view[:, k, i * 128:(i + 1) * 128], rhs_t[:, k0 + k, 0:N],
                              (k0 + k) == 0, (k0 + k) == nk - 1) for k in range(kc)]
                    P.op("pe", seq(fns), reads=[wn, rhs_name], writes=[banks[i]])
                k0 += kc
            for i in range(4):
                c = g4 * 4 + i
                bank = banks[i]
                if bias_row is not None:
                    P.op("act", lambda e, bank=bank, c=c: e.activation(out=btmp[:, 0:N], in_=ps[bank][:, 0:N], func=AF.Identity,
                                                                       bias=vcol(bias_row, c), scale=1.0),
                         reads=[bank, "vD"], writes=["btmp"])
                    P.op("dve", lambda e, c=c: e.scalar_tensor_tensor(out=xres[:, c, 0:N], in0=xres[:, c, 0:N], scalar=ALPHA,
                                                                      in1=btmp[:, 0:N], op0=ALU.mult, op1=ALU.add),
                         reads=["btmp", "xres"], writes=["xres"])
                else:
                    P.op("dve", lambda e, bank=bank, c=c: e.scalar_tensor_tensor(out=xres[:, c, 0:N], in0=xres[:, c, 0:N], scalar=ALPHA,
                                                                                 in1=ps[bank][:, 0:N], op0=ALU.mult, op1=ALU.add),
                         reads=[bank, "xres"], writes=["xres"])
                ln.stats(c)
        ln.finalize()
        for c in range(NCH):
            ln.apply(c, vcol(g_row, c), vcol(b_row, c), out_f32=xres[:, c, 0:N], out_f32_name="xres",
                     out_bf=xbf[:, c, 0:N], out_bf_name="xbf")
        P.barrier()
        A.reset(m)

    def ffn_layer(l, p, N):
        m = A.mark()
        hT = A.alloc("hT", [128, NFF, 512], BF16)
        gsb = [A.alloc("gsb0", [128, 516], F32), A.alloc("gsb1", [128, 516], F32)]
        cc = [A.alloc("cc%d" % i, [128, 512], F32) for i in range(4)]
        for ft in range(11):
            vg, gn = wload("w_g", l, 0, 16, ft * 512, 512)
            for i in range(4):
                f = ft * 4 + i
                bg = ["psA", "psC"][i % 2]
                P.op("pe", seq([mm(ps[bg][:, 0:N], vg[:, k, i * 128:(i + 1) * 128], xbf[:, k, 0:N], k == 0, k == 15) for k in range(16)]),
                     reads=[gn, "xbf"], writes=[bg])
                g = gsb[f % 2]
                gnm = "gsb%d" % (f % 2)
                c_ = cc[i]
                cn = "cc%d" % i
                P.op("dve", lambda e, g=g, f=f: e.tensor_copy(out=g[:, 0:2], in_=ghalo[:, l, f, :]), reads=["ghalo"], writes=[gnm])
                P.op("act", lambda e, g=g, bg=bg: e.copy(out=g[:, 2:2 + N], in_=ps[bg][:, 0:N]), reads=[bg], writes=[gnm])
                P.op("dve", lambda e, g=g, f=f: e.tensor_copy(out=ghalo[:, l, f, :], in_=g[:, N:N + 2]), reads=[gnm], writes=["ghalo"])
                P.op("dve", lambda e, g=g, c_=c_, f=f: e.tensor_scalar(out=c_[:, 0:N], in0=g[:, 2:2 + N], scalar1=vF[:, f, l * 4 + 2:l * 4 + 3],
                                                                      scalar2=vF[:, f, l * 4 + 3:l * 4 + 4], op0=ALU.mult, op1=ALU.add),
                     reads=[gnm, "vF"], writes=[cn])
                P.op("dve", lambda e, g=g, c_=c_, f=f: e.scalar_tensor_tensor(out=c_[:, 0:N], in0=g[:, 1:1 + N], scalar=vF[:, f, l * 4 + 1:l * 4 + 2],
                                                                             in1=c_[:, 0:N], op0=ALU.mult, op1=ALU.add),
                     reads=[gnm, "vF", cn], writes=[cn])
                P.op("dve", lambda e, g=g, c_=c_, f=f: e.scalar_tensor_tensor(out=c_[:, 0:N], in0=g[:, 0:N], scalar=vF[:, f, l * 4:l * 4 + 1],
                                                                             in1=c_[:, 0:N], op0=ALU.mult, op1=ALU.add),
                     reads=[gnm, "vF", cn], writes=[cn])
                P.op("act", lambda e, c_=c_: e.activation(out=c_[:, 0:N], in_=c_[:, 0:N], func=AF.Silu), reads=[cn], writes=[cn])
            vu, un = wload("w_u", l, 0, 16, ft * 512, 512)
            for i in range(4):
                f = ft * 4 + i
                bu = ["psB", "psD"][i % 2]
                c_ = cc[i]
                cn = "cc%d" % i
                P.op("pe", seq([mm(ps[bu][:, 0:N], vu[:, k, i * 128:(i + 1) * 128], xbf[:, k, 0:N], k == 0, k == 15) for k in range(16)]),
                     reads=[un, "xbf"], writes=[bu])
                P.op("dve", lambda e, c_=c_, f=f, bu=bu: e.tensor_tensor(out=hT[:, f, 0:N], in0=c_[:, 0:N], in1=ps[bu][:, 0:N], op=ALU.mult),
                     reads=[cn, bu], writes=["hT"])
        if p >= 3:
            dstt = ffn_p[l] if p == 3 else ffn_s[l]
            for t in range(2):
                dv = dstt[t, :].rearrange("(f q) -> q f", q=128)
                for q4 in range(4):
                    P.dma("sp", lambda e, q4=q4, dv=dv, t=t: e.dma_start(out=dv[:, q4 * 11:(q4 + 1) * 11], in_=ghalo[:, l, q4 * 11:(q4 + 1) * 11, t],
                                                                         allow_slow_non_contiguous=True), reads=["ghalo"])
        out_proj_ln(N, "w_d", l, hT, "hT", NFF, 16 + l, 20 + l)
        A.reset(m)

    def conv_layer(j, i_layer, p, N):
        m = A.mark()
        aTb = A.alloc("aTb", [128, NCH, 544], BF16)
        cv_ = A.alloc("cv", [128, NCH, 512], F32)
        sig = [A.alloc("sig%d" % i, [128, 512], F32) for i in range(4)]
        dgc = [A.alloc("dgc0", [128, 31, 128], BF16), A.alloc("dgc1", [128, 31, 128], BF16)]
        alast = A.alloc("alast", [128, NCH, 32], F32)
        atm = A.alloc("atm", [32, D], F32)
        P.op("pool", lambda e: e.tensor_copy(out=aTb[:, :, 0:30], in_=ahalo[:, j, :, :]), reads=["ahalo"], writes=["aTb"])
        nl = min(30, N)
        for g4 in range(4):
            vg, gn = wload("w_pw1", j, 0, 16, D + g4 * 512, 512)
            for i in range(4):
                c = g4 * 4 + i
                bg = ["psB", "psD"][i % 2]
                P.op("pe", seq([mm(ps[bg][:, 0:N], vg[:, k, i * 128:(i + 1) * 128], xbf[:, k, 0:N], k == 0, k == 15) for k in range(16)]),
                     reads=[gn, "xbf"], writes=[bg])
                s_ = sig[i]
                sn = "sig%d" % i
                P.op("act", lambda e, s_=s_, bg=bg, c=c: e.activation(out=s_[:, 0:N], in_=ps[bg][:, 0:N], func=AF.Sigmoid,
                                                                      bias=vcol(25 + 2 * j, c), scale=1.0), reads=[bg, "vD"], writes=[sn])
            va, an = wload("w_pw1", j, 0, 16, g4 * 512, 512)
            for i in range(4):
                c = g4 * 4 + i
                ba = ["psA", "psC"][i % 2]
                s_ = sig[i]
                sn = "sig%d" % i
                P.op("pe", seq([mm(ps[ba][:, 0:N], va[:, k, i * 128:(i + 1) * 128], xbf[:, k, 0:N], k == 0, k == 15) for k in range(16)]),
                     reads=[an, "xbf"], writes=[ba])
                P.op("dve", lambda e, s_=s_, ba=ba, c=c: e.scalar_tensor_tensor(out=aTb[:, c, 30:30 + N], in0=ps[ba][:, 0:N], scalar=vcol(24 + 2 * j, c),
                                                                                in1=s_[:, 0:N], op0=ALU.add, op1=ALU.mult),
                     reads=[ba, sn, "vD"], writes=["aTb"])
                if p >= 3:
                    P.op("dve", lambda e, s_=s_, ba=ba, c=c: e.scalar_tensor_tensor(out=alast[:, c, 0:nl], in0=ps[ba][:, N - nl:N], scalar=vcol(24 + 2 * j, c),
                                                                                    in1=s_[:, N - nl:N], op0=ALU.add, op1=ALU.mult),
                         reads=[ba, sn, "vD"], writes=["alast"])
        P.op("pool", lambda e: e.tensor_copy(out=ahalo[:, j, :, :], in_=aTb[:, :, N:N + 30]), reads=["aTb"], writes=["ahalo"])
        if p >= 3:
            for c4 in range(4):
                bank = ["psE", "psF"][c4 % 2]
                fns = [lambda e, ii=ii, c4=c4: e.transpose(out=ps[bank][0:nl, ii * 128:(ii + 1) * 128], in_=alast[:, c4 * 4 + ii, 0:nl],
                                                           identity=identf[:, :]) for ii in range(4)]
                P.op("pe", seq(fns), reads=["alast", "identf"], writes=[bank])
                P.op("act", lambda e, c4=c4, bank=bank: e.copy(out=atm[0:nl, c4 * 512:(c4 + 1) * 512], in_=ps[bank][0:nl, :]),
                     reads=[bank], writes=["atm"])
            if p == 3:
                P.dma("sp", lambda e: e.dma_start(out=conv_p[j, :, :], in_=atm[0:30, :]), reads=["atm"])
            else:
                P.dma("sp", lambda e: e.dma_start(out=conv_s[j, 26:30, :], in_=atm[0:4, :]), reads=["atm"])
                P.dma("sp", lambda e: e.dma_start(out=conv_s[j, 0:26, :], in_=stc[j, 4:30, :]))
        ln = LN(N, cv_, "cv")
        for c in range(NCH):
            dg = dgc[c % 2]
            dn = "dgc%d" % (c % 2)
            P.op("dve", lambda e, dg=dg, c=c: e.tensor_tensor(out=dg[:, :, :], in0=identf[:, :].unsqueeze(1).broadcast_to([128, 31, 128]),
                                                              in1=wdwT[:, c, j * 31:j * 31 + 31].unsqueeze(2).broadcast_to([128, 31, 128]), op=ALU.mult),
                 reads=["identf", "wdwT"], writes=[dn])
            bank = ["psA", "psB"][c % 2]
            P.op("pe", seq([mm(ps[bank][:, 0:N], dg[:, w, :], aTb[:, c, w:w + N], w == 0, w == 30) for w in range(31)]),
                 reads=[dn, "aTb"], writes=[bank])
            P.op("act", lambda e, c=c, bank=bank: e.activation(out=cv_[:, c, 0:N], in_=ps[bank][:, 0:N], func=AF.Identity,
                                                               bias=vcol(0 + j, c), scale=1.0), reads=[bank, "vD"], writes=["cv"])
            ln.stats(c)
        ln.finalize()
        for c in range(NCH):
            ln.apply(c, vcol(2 + j, c), vcol(4 + j, c), out_bf=xbf[:, c, 0:N], out_bf_name="xbf", func=AF.Silu)
        P.barrier()
        A.reset(m)
        out_proj_ln(N, "w_pw2", j, xbf, "xbf", 16, 8 + i_layer, 12 + i_layer, bias_row=6 + j)

    def rope_fm(bank, N, tab, pmat, dst_ap, dst_name, bset, sw_bank, rows=128):
        bi, rawb, t1, t2 = bset
        rn_, t1n, t2n = "rawb%d" % bi, "t1%d" % bi, "t2%d" % bi
        C = tab[0:rows, 0, 0:N]
        S = tab[0:rows, 1, 0:N]
        P.op("act", lambda e: e.copy(out=rawb[0:rows, 0:N], in_=ps[bank][0:rows, 0:N]), reads=[bank], writes=[rn_])
        P.op("pe", mm(ps[sw_bank][0:rows, 0:N], pmat[0:rows, 0:rows], rawb[0:rows, 0:N], True, True), reads=[rn_, "pqb", "pib"], writes=[sw_bank])
        P.op("dve", lambda e: e.tensor_tensor(out=t1[0:rows, 0:N], in0=ps[bank][0:rows, 0:N], in1=C, op=ALU.mult), reads=[bank, "rq", "ri"], writes=[t1n])
        P.op("dve", lambda e: e.tensor_tensor(out=t2[0:rows, 0:N], in0=ps[sw_bank][0:rows, 0:N], in1=S, op=ALU.mult), reads=[sw_bank, "rq", "ri"], writes=[t2n])
        P.op("pool", lambda e: e.tensor_tensor(out=dst_ap, in0=t1[0:rows, 0:N], in1=t2[0:rows, 0:N], op=ALU.add), reads=[t1n, t2n], writes=[dst_name])

    def rope_tm(x3, nt, tt, nh, half, coff, resname):
        cosb = ttq[0:nt, tt, coff:coff + half].unsqueeze(1).broadcast_to([nt, nh, half])
        sinb = ttq[0:nt, tt, coff + half:coff + 2 * half].unsqueeze(1).broadcast_to([nt, nh, half])
        x1 = x3[:, :, 0:half]
        x2 = x3[:, :, half:2 * half]
        rtmp = RT["rtmp"]
        ta, tb, tc_, td = (rtmp[0:nt, k, 0:nh * half].rearrange("p (h d) -> p h d", h=nh) for k in range(4))
        P.op("dve", lambda e: e.tensor_tensor(out=ta, in0=x1, in1=cosb, op=ALU.mult), reads=[resname, "ttq"], writes=["rtmp"])
        P.op("dve", lambda e: e.tensor_tensor(out=tb, in0=x2, in1=sinb, op=ALU.mult), reads=[resname, "ttq"], writes=["rtmp"])
        P.op("dve", lambda e: e.tensor_tensor(out=tc_, in0=x2, in1=cosb, op=ALU.mult), reads=[resname, "ttq"], writes=["rtmp"])
        P.op("dve", lambda e: e.tensor_tensor(out=td, in0=x1, in1=sinb, op=ALU.mult), reads=[resname, "ttq"], writes=["rtmp"])
        P.op("dve", lambda e: e.tensor_tensor(out=x1, in0=ta, in1=tb, op=ALU.subtract), reads=["rtmp"], writes=[resname])
        P.op("dve", lambda e: e.tensor_tensor(out=x2, in0=tc_, in1=td, op=ALU.add), reads=["rtmp"], writes=[resname])

    RT = {}

    def bisect(scores, sname, nq, nk, theta, work):
        lo, wd, mid, cntt, sel, junk, cnt4 = work["lo"], work["wd"], work["mid"], work["cnt"], work["sel"], work["junk"], work["cnt4"]
        jw = work["jw"]
        chunks = []
        c0 = 0
        while c0 < nk:
            w = min(jw, nk - c0)
            chunks.append((c0, w))
            c0 += w
        assert len(chunks) <= 8
        P.op("dve", lambda e: e.tensor_reduce(out=mid[0:nq, :], in_=scores[0:nq, 0:nk], axis=AX.X, op=ALU.max), reads=[sname], writes=["bs_mid"])
        P.op("dve", lambda e: e.memset(lo[0:nq, :], NEG - 1.0), writes=["bs_lo"])
        P.op("dve", lambda e: e.tensor_scalar(out=wd[0:nq, :], in0=mid[0:nq, :], scalar1=-(NEG - 1.0), scalar2=None, op0=ALU.add), reads=["bs_mid"], writes=["bs_wd"])
        nch = len(chunks)
        for it in range(23):
            hf = 0.5 ** (it + 1)
            P.op("dve", lambda e, hf=hf: e.tensor_scalar(out=mid[0:nq, :], in0=wd[0:nq, :], scalar1=hf, scalar2=lo[0:nq, 0:1], op0=ALU.mult, op1=ALU.add),
                 reads=["bs_lo", "bs_wd"], writes=["bs_mid"])
            P.op("dve", lambda e: e.memset(cnt4[0:nq, 0:nch], 0.0), writes=["bs_cnt4"])
            for k, (c0, w) in enumerate(chunks):
                P.op("dve", lambda e, k=k, c0=c0, w=w: e.tensor_scalar(out=junk[0:nq, 0:w], in0=scores[0:nq, c0:c0 + w], scalar1=mid[0:nq, 0:1], scalar2=0.0,
                                                                      op0=ALU.is_gt, op1=ALU.add, accum_out=cnt4[0:nq, k:k + 1]),
                     reads=[sname, "bs_mid", "bs_cnt4"], writes=["bs_junk", "bs_cnt4"])
            if nch > 1:
                P.op("dve", lambda e: e.tensor_reduce(out=cntt[0:nq, :], in_=cnt4[0:nq, 0:nch], axis=AX.X, op=ALU.add), reads=["bs_cnt4"], writes=["bs_cnt"])
                csrc, csn = cntt, "bs_cnt"
            else:
                csrc, csn = cnt4, "bs_cnt4"
            P.op("dve", lambda e, csrc=csrc: e.scalar_tensor_tensor(out=sel[0:nq, :], in0=csrc[0:nq, 0:1], scalar=255.5, in1=wd[0:nq, :], op0=ALU.is_gt, op1=ALU.mult),
                 reads=[csn, "bs_wd"], writes=["bs_sel"])
            P.op("dve", lambda e, hf=hf: e.scalar_tensor_tensor(out=lo[0:nq, :], in0=sel[0:nq, :], scalar=hf, in1=lo[0:nq, :], op0=ALU.mult, op1=ALU.add),
                 reads=["bs_lo", "bs_sel"], writes=["bs_lo"])
        P.op("dve", lambda e: e.tensor_copy(out=theta[0:nq, :], in_=lo[0:nq, :]), reads=["bs_lo"], writes=["theta"])

    def indexer_tile(nq, qi_lhs, ki_rhs, ncols, wq_ap, dgt, Rb, sc_out, sc_name, kin_name, diag_mask_cols=None):
        banks = ["psA", "psB"]
        def dmm(h):
            P.op("pe", mm(ps[banks[h % 2]][0:nq, 0:ncols], qi_lhs(h), ki_rhs(h), True, True), reads=["qiT", kin_name], writes=[banks[h % 2]])
        dmm(0)
        masked = diag_mask_cols is not None
        if masked:
            P.op("pe", mm(ps["psC"][0:nq, 0:ncols], identb[0:nq, 0:nq], cmaskp[0:nq, 512 - ncols:512], True, False), reads=["identb", "cmaskp"], writes=["psC"])
        for h in range(16):
            if h + 1 < 16:
                dmm(h + 1)
            R = Rb[h % 2]
            rn = "Rb%d" % (h % 2)
            P.op("act", lambda e, R=R, h=h: e.activation(out=R[0:nq, 0:ncols], in_=ps[banks[h % 2]][0:nq, 0:ncols], func=AF.Relu),
                 reads=[banks[h % 2]], writes=[rn])
            P.op("pe", mm(ps["psC"][0:nq, 0:ncols], dgt[0:nq, h, 0:nq], R[0:nq, 0:ncols], (h == 0) and not masked, h == 15), reads=[rn, "dgt"], writes=["psC"])
        P.op("act", lambda e: e.copy(out=sc_out, in_=ps["psC"][0:nq, 0:ncols]), reads=["psC"], writes=[sc_name])

    def attn_layer(j, i_layer, p, N):
        m = A.mark()
        sample = (p == 4)
        wj = w_in[j]
        qT = A.alloc("qT", [128, 16, 512 if not sample else 4], BF16)
        KT = A.alloc("KT", [128, 4, SEQ if not sample else 4], BF16)
        Vb = A.alloc("Vb", [128, 16 if not sample else 1, 512], BF16)
        kiT = A.alloc("kiT", [128, SEQ if not sample else 4], BF16)
        qiT = A.alloc("qiT", [128, 8, 512] if not sample else [64, 16, 4], BF16)
        wsb = A.alloc("wsb", [128, 4, 16], F32)
        mtmp = A.mark()
        rawb = A.alloc("rawb", [128, 512], BF16)
        t1 = A.alloc("t1", [128, 512], F32)
        t2 = A.alloc("t2", [128, 512], F32)
        BS = [(0, rawb, t1, t2), (1, A.alloc("rawb1", [128, 512], BF16), A.alloc("t11", [128, 512], F32), A.alloc("t21", [128, 512], F32))]
        RT["rtmp"] = A.alloc("rtmp", [128, 4, 64], F32)
        ksb = A.alloc("ksb", [128, 512], F32)
        vsb = A.alloc("vsb", [128, 512], F32)
        kisb = A.alloc("kisb", [128, 80], F32)
        kbase = p * 512 if not sample else 0
        if not sample and p > 0:
            P.dma("sp", lambda e: e.dma_start(out=KT[:, :, 0:kbase], in_=scrKT[j, :, :, 0:kbase]), writes=["KT"])
            P.dma("sp", lambda e: e.dma_start(out=Vb[:, 0:4 * p, :], in_=scrV[j, :, 0:4 * p, :]), writes=["Vb"])
            P.dma("sp", lambda e: e.dma_start(out=kiT[:, 0:kbase], in_=scrKI[j, :, 0:kbase]), writes=["kiT"])
        for g4 in range(4):
            view, wn = wload("w_in", j, 0, 16, g4 * 512, 512)
            for i in range(4):
                bank = ["psA", "psB"][i % 2]
                P.op("pe", seq([mm(ps[bank][:, 0:N], view[:, k, i * 128:(i + 1) * 128], xbf[:, k, 0:N], k == 0, k == 15) for k in range(16)]),
                     reads=[wn, "xbf"], writes=[bank])
                rope_fm(bank, N, rq, pqb, qT[:, g4 * 4 + i, 0:N], "qT", BS[i % 2], ["psC", "psD"][i % 2])
        vk, kn = wload("w_in", j, 0, 16, 2048, 512)
        for i in range(4):
            bank = ["psA", "psB"][i % 2]
            P.op("pe", seq([mm(ps[bank][:, 0:N], vk[:, k, i * 128:(i + 1) * 128], xbf[:, k, 0:N], k == 0, k == 15) for k in range(16)]),
                 reads=[kn, "xbf"], writes=[bank])
            rope_fm(bank, N, rq, pqb, KT[:, i, kbase:kbase + N], "KT", BS[i % 2], ["psC", "psD"][i % 2])
        ntt = (N + 127) // 128
        kdst = k_p if not sample else k_s
        vdst = v_p if not sample else v_s
        for tt in range(ntt):
            nt = min(128, N - tt * 128)
            r0 = kbase + tt * 128
            P.op("pe", seq([mm(ps["psE"][0:nt, :], xbf[:, k, tt * 128:tt * 128 + nt], vk[:, k, :], k == 0, k == 15) for k in range(16)]),
                 reads=[kn, "xbf"], writes=["psE"])
            P.op("act", lambda e, nt=nt: e.copy(out=ksb[0:nt, :], in_=ps["psE"][0:nt, :]), reads=["psE"], writes=["ksb"])
            rope_tm(ksb[0:nt, :].rearrange("p (h d) -> p h d", h=4), nt, tt, 4, 16, 0, "ksb")
            P.dma("sp", lambda e, nt=nt, r0=r0: e.dma_start(out=kdst[j, r0:r0 + nt, :], in_=ksb[0:nt, :]), reads=["ksb"])
        vv, vn = wload("w_in", j, 0, 16, 2560, 512)
        for tt in range(ntt):
            nt = min(128, N - tt * 128)
            r0 = kbase + tt * 128
            P.op("pe", seq([mm(ps["psF"][0:nt, :], xbf[:, k, tt * 128:tt * 128 + nt], vv[:, k, :], k == 0, k == 15) for k in range(16)]),
                 reads=[vn, "xbf"], writes=["psF"])
            P.op("act", lambda e, nt=nt: e.copy(out=vsb[0:nt, :], in_=ps["psF"][0:nt, :]), reads=["psF"], writes=["vsb"])
            P.op("pool", lambda e, nt=nt, tt=tt: e.tensor_copy(out=Vb[0:nt, (kbase // 128 + tt) if not sample else 0, :], in_=vsb[0:nt, :]),
                 reads=["vsb"], writes=["Vb"])
            P.dma("sp", lambda e, nt=nt, r0=r0: e.dma_start(out=vdst[j, r0:r0 + nt, :], in_=vsb[0:nt, :]), reads=["vsb"])
        if not sample:
            for g2 in range(2):
                view, wn = wload("w_in", j, 0, 16, 3072 + g2 * 512, 512)
                for i in range(4):
                    bank = ["psA", "psB"][i % 2]
                    P.op("pe", seq([mm(ps[bank][:, 0:N], view[:, k, i * 128:(i + 1) * 128], xbf[:, k, 0:N], k == 0, k == 15) for k in range(16)]),
                         reads=[wn, "xbf"], writes=[bank])
                    rope_fm(bank, N, ri, pib, qiT[:, g2 * 4 + i, 0:N], "qiT", BS[i % 2], ["psC", "psD"][i % 2])
        else:
            for g2 in range(2):
                view, wn = wload("w_in", j, 0, 16, 3072 + g2 * 512, 512)
                fns = []
                for hh in range(8):
                    for k in range(16):
                        fns.append(mm(ps["psA"][0:64, (g2 * 8 + hh) * 4:(g2 * 8 + hh) * 4 + 4], view[:, k, hh * 64:(hh + 1) * 64], xbf[:, k, 0:4], k == 0, k == 15))
                P.op("pe", seq(fns), reads=[wn, "xbf"], writes=["psA"])
            Cb = ri[0:64, 0, 0:4].unsqueeze(1).broadcast_to([64, 16, 4])
            Sb = ri[0:64, 1, 0:4].unsqueeze(1).broadcast_to([64, 16, 4])
            P.op("act", lambda e: e.copy(out=rawb[0:64, 0:64], in_=ps["psA"][0:64, 0:64]), reads=["psA"], writes=["rawb0"])
            P.op("pe", mm(ps["psC"][0:64, 0:64], pib[0:64, 0:64], rawb[0:64, 0:64], True, True), reads=["rawb0", "pib"], writes=["psC"])
            P.op("dve", lambda e: e.tensor_tensor(out=t1[0:64, 0:64].rearrange("p (h t) -> p h t", h=16),
                                                  in0=ps["psA"][0:64, 0:64].rearrange("p (h t) -> p h t", h=16), in1=Cb, op=ALU.mult),
                 reads=["psA", "ri"], writes=["t10"])
            P.op("dve", lambda e: e.tensor_tensor(out=t2[0:64, 0:64].rearrange("p (h t) -> p h t", h=16),
                                                  in0=ps["psC"][0:64, 0:64].rearrange("p (h t) -> p h t", h=16), in1=Sb, op=ALU.mult),
                 reads=["psC", "ri"], writes=["t20"])
            P.op("pool", lambda e: e.tensor_tensor(out=qiT[0:64, :, :].rearrange("p h t -> p (h t)"), in0=t1[0:64, 0:64], in1=t2[0:64, 0:64], op=ALU.add),
                 reads=["t10", "t20"], writes=["qiT"])
        vkw, wn = wload("kiwi", j)
        P.op("pe", seq([mm(ps["psA"][:, 0:N], vkw[:, k, 0:128], xbf[:, k, 0:N], k == 0, k == 15) for k in range(16)]), reads=[wn, "xbf"], writes=["psA"])
        rope_fm("psA", N, ri, pib, kiT[:, kbase:kbase + N], "kiT", BS[0], "psC")
        kidst = ki_p if not sample else ki_s
        for tt in range(ntt):
            nt = min(128, N - tt * 128)
            r0 = kbase + tt * 128
            P.op("pe", seq([mm(ps["psE"][0:nt, 0:80], xbf[:, k, tt * 128:tt * 128 + nt], vkw[:, k, 128:208], k == 0, k == 15) for k in range(16)]),
                 reads=[wn, "xbf"], writes=["psE"])
            P.op("act", lambda e, nt=nt: e.copy(out=kisb[0:nt, :], in_=ps["psE"][0:nt, 0:80]), reads=["psE"], writes=["kisb"])
            P.op("pool", lambda e, nt=nt, tt=tt: e.tensor_copy(out=wsb[0:nt, tt, :], in_=kisb[0:nt, 64:80]), reads=["kisb"], writes=["wsb"])
            rope_tm(kisb[0:nt, 0:64].rearrange("p (h d) -> p h d", h=1), nt, tt, 1, 8, 32, "kisb")
            P.dma("sp", lambda e, nt=nt, r0=r0: e.dma_start(out=kidst[j, r0:r0 + nt, :], in_=kisb[0:nt, 0:64]), reads=["kisb"])
        if not sample and p < 3:
            P.dma("sp", lambda e: e.dma_start(out=scrKT[j, :, :, kbase:kbase + N], in_=KT[:, :, kbase:kbase + N]), reads=["KT"])
            P.dma("sp", lambda e: e.dma_start(out=scrV[j, :, 4 * p:4 * p + 4, :], in_=Vb[:, 4 * p:4 * p + 4, :]), reads=["Vb"])
            P.dma("sp", lambda e: e.dma_start(out=scrKI[j, :, kbase:kbase + N], in_=kiT[:, kbase:kbase + N]), reads=["kiT"])

        ckpt('proj')
        P.barrier()
        A.reset(mtmp)
        dgt = A.alloc("dgt", [128, 16, 128], BF16)
        Rb = [A.alloc("Rb0", [128, 512], BF16), A.alloc("Rb1", [128, 512], BF16)]
        Eb = [A.alloc("Eb0", [128, 512], BF16), A.alloc("Eb1", [128, 512], BF16)]
        Pb = [A.alloc("Pb0", [128, 512], BF16), A.alloc("Pb1", [128, 512], BF16)]
        rden = A.alloc("rden", [128, 512], F32)
        theta = A.alloc("theta", [128, 1], F32)
        work = {k: A.alloc("bs_" + k, [128, 1], F32) for k in ("lo", "wd", "mid", "cnt", "sel")}
        work["cnt4"] = A.alloc("bs_cnt4", [128, 8], F32)
        scale = 128.0 ** -0.5
        if not sample:
            scb = [A.alloc("scores0", [128, SEQ], F32), A.alloc("scores1", [128, SEQ], F32)]
            work["junk"] = A.alloc("bs_junk", [128, SEQ], BF16)
            work["jw"] = SEQ
            maskf = A.alloc("maskf", [128, SEQ], BF16)
            maskT = A.alloc("maskT", [128, 16, 128], BF16)

            def run_indexer(tt):
                qt = p * 4 + tt
                nk = (qt + 1) * 128
                scores = scb[tt % 2]
                sname = "scores%d" % (tt % 2)
                P.op("pool", lambda e: e.tensor_tensor(out=dgt[:, :, :], in0=identf[:, :].unsqueeze(1).broadcast_to([128, 16, 128]),
                                                       in1=wsb[:, tt, :].unsqueeze(2).broadcast_to([128, 16, 128]), op=ALU.mult),
                     reads=["identf", "wsb"], writes=["dgt"])
                k0 = 0
                while k0 < nk:
                    ncols = min(512, nk - k0)
                    last = (k0 + ncols == nk)
                    indexer_tile(128, lambda h: qiT[(h % 2) * 64:(h % 2) * 64 + 64, h // 2, tt * 128:(tt + 1) * 128],
                                 lambda h, k0=k0, ncols=ncols: kiT[(h % 2) * 64:(h % 2) * 64 + 64, k0:k0 + ncols],
                                 ncols, None, dgt, Rb, scores[:, k0:k0 + ncols], sname, "kiT",
                                 diag_mask_cols=(ncols - 128) if last else None)
                    k0 += ncols

            run_indexer(0)
            for tt in range(4):
                qt = p * 4 + tt
                nk = (qt + 1) * 128
                scores = scb[tt % 2]
                sname = "scores%d" % (tt % 2)
                if tt + 1 < 4:
                    run_indexer(tt + 1)
                if qt >= 2:
                    bisect(scores, sname, 128, nk, theta, work)
                else:
                    P.op("dve", lambda e: e.memset(theta[:, :], NEG * 0.5), writes=["theta"])
                P.op("dve", lambda e, nk=nk, scores=scores: e.tensor_scalar(out=maskf[:, 0:nk], in0=scores[:, 0:nk], scalar1=theta[:, 0:1], scalar2=None, op0=ALU.is_gt),
                     reads=[sname, "theta"], writes=["maskf"])
                nk8 = qt + 1
                mtb = ps["psD"][:, :].bitcast(BF16)
                for b4 in range((nk8 + 3) // 4):
                    n4 = min(4, nk8 - b4 * 4)
                    fns = [lambda e, i=i, b4=b4: e.transpose(out=mtb[:, i * 128:(i + 1) * 128], in_=maskf[:, (b4 * 4 + i) * 128:(b4 * 4 + i + 1) * 128],
                                                             identity=identb[:, :]) for i in range(n4)]
                    P.op("pe", seq(fns), reads=["maskf", "identb"], writes=["psD"])
                    P.op("act", lambda e, b4=b4, n4=n4: e.copy(out=maskT[:, b4 * 4:b4 * 4 + n4, :], in_=mtb[:, 0:n4 * 128].rearrange("p (i q) -> p i q", i=n4)),
                         reads=["psD"], writes=["maskT"])
                for n in range(4):
                    for k8 in range(nk8):
                        sb = ["psE", "psF"][k8 % 2]
                        P.op("pe", mm(ps[sb][:, :], KT[:, n, k8 * 128:(k8 + 1) * 128], qT[:, 4 * n:4 * n + 4, tt * 128:(tt + 1) * 128], True, True),
                             reads=["KT", "qT"], writes=[sb])
                        E = Eb[k8 % 2]
                        en = "Eb%d" % (k8 % 2)
                        Pt = Pb[k8 % 2]
                        pn = "Pb%d" % (k8 % 2)
                        P.op("act", lambda e, E=E, sb=sb: e.activation(out=E[:, :], in_=ps[sb][:, :], func=AF.Exp, scale=scale), reads=[sb], writes=[en])
                        P.op("dve", lambda e, E=E, Pt=Pt, k8=k8: e.tensor_tensor(out=Pt[:, :].rearrange("p (g q) -> p g q", g=4),
                                                                                in0=E[:, :].rearrange("p (g q) -> p g q", g=4),
                                                                                in1=maskT[:, k8, :].unsqueeze(1).broadcast_to([128, 4, 128]), op=ALU.mult),
                             reads=[en, "maskT"], writes=[pn])
                        P.op("pe", seq([mm(ps["psG"][:, :], Vb[:, k8, n * 128:(n + 1) * 128], Pt[:, :], k8 == 0, k8 == nk8 - 1),
                                        mm(ps["psH"][:, :], onesb[:, :], Pt[:, :], k8 == 0, k8 == nk8 - 1)]),
                             reads=[pn, "Vb", "onesb"], writes=["psG", "psH"])
                    P.op("dve", lambda e: e.reciprocal(out=rden[:, :], in_=ps["psH"][:, :]), reads=["psH"], writes=["rden"])
                    P.op("dve", lambda e, n=n, tt=tt: e.tensor_tensor(out=xbf[:, 4 * n:4 * n + 4, tt * 128:(tt + 1) * 128],
                                                                      in0=ps["psG"][:, :].rearrange("p (g q) -> p g q", g=4),
                                                                      in1=rden[:, :].rearrange("p (g q) -> p g q", g=4), op=ALU.mult),
                         reads=["psG", "rden"], writes=["xbf"])
        else:
            sample_attn(j, qT, KT, Vb, kiT, qiT, wsb, dgt, Rb, Eb, Pb, rden, theta, work, scale)
        P.barrier()
        ckpt('attn')
        A.reset(m)
        out_proj_ln(N, "w_out", j, xbf, "xbf", 16, 8 + i_layer, 12 + i_layer)
        ckpt('attn_out')

    def sample_attn(j, qT, KT, Vb, kiT, qiT, wsb, dgt, Rb, Eb, Pb, rden, theta, work, scale):
        NKS = PAST + 4
        scores = A.alloc("scores", [4, NKS + 12], F32)
        work["junk"] = A.alloc("bs_junk", [4, 4100], BF16)
        work["jw"] = 4100
        maskc = A.alloc("maskc", [4, 512], BF16)
        maskT = A.alloc("maskT", [128, 129, 4], BF16)
        G = A.alloc("G", [128, 32, 64], F32)
        kiq = A.alloc("kiq", [64, 4096], BF16)
        ptc = A.alloc("ptc", [128, 1], I32)
        kpg = [A.alloc("kpg0", [128, 512], F32), A.alloc("kpg1", [128, 512], F32)]
        vpg = [A.alloc("vpg0", [128, 512], F32), A.alloc("vpg1", [128, 512], F32)]
        ktb = A.alloc("ktb", [128, 4, 128], BF16)
        vpb = A.alloc("vpb", [128, 512], BF16)
        oacc = A.alloc("oacc", [128, 128], F32)
        P.dma("sp", lambda e: e.dma_start(out=ptc[:, :], in_=pt.unsqueeze(1)), writes=["ptc"])
        A4 = A.alloc("A4", [4, 16, 4], BF16)
        Wsel = A.alloc("Wsel", [64, 4], BF16)
        P.op("dve", lambda e: e.tensor_tensor(out=A4[:, :, :], in0=identf[0:4, 0:4].unsqueeze(1).broadcast_to([4, 16, 4]),
                                              in1=wsb[0:4, 0, :].unsqueeze(2).broadcast_to([4, 16, 4]), op=ALU.mult),
             reads=["identf", "wsb"], writes=["A4"])
        mtb0 = ps["psD"][:, :].bitcast(BF16)
        P.op("pe", lambda e: e.transpose(out=mtb0[0:64, 0:4], in_=A4[:, :, :].rearrange("p h t -> p (h t)"), identity=identb[0:4, 0:4]),
             reads=["A4", "identb"], writes=["psD"])
        P.op("act", lambda e: e.copy(out=Wsel[:, :], in_=mtb0[0:64, 0:4]), reads=["psD"], writes=["Wsel"])
        qi64 = qiT[0:64, :, :].rearrange("p h t -> p (h t)")

        def packed_tile(ki_rhs, kin_name, ncols, sc_out, bi, mask_new=False):
            bd = ["psA", "psB"][bi % 2]
            R = Rb[bi % 2]
            rn = "Rb%d" % (bi % 2)
            P.op("pe", mm(ps[bd][0:64, 0:ncols], qi64, ki_rhs, True, True), reads=["qiT", kin_name], writes=[bd])
            P.op("act", lambda e: e.activation(out=R[0:64, 0:ncols], in_=ps[bd][0:64, 0:ncols], func=AF.Relu), reads=[bd], writes=[rn])
            P.op("pe", mm(ps["psC"][0:4, 0:ncols], Wsel[:, :], R[0:64, 0:ncols], True, True), reads=[rn, "Wsel"], writes=["psC"])
            if not mask_new:
                P.op("act", lambda e: e.copy(out=sc_out, in_=ps["psC"][0:4, 0:ncols]), reads=["psC"], writes=["scores"])
            else:
                P.op("dve", lambda e: e.tensor_tensor(out=sc_out, in0=ps["psC"][0:4, 0:ncols], in1=cmask[0:4, 0:ncols], op=ALU.add),
                     reads=["psC", "cmask"], writes=["scores"])

        ptf = A.alloc("ptf", [128, 1], F32)
        idxqf = A.alloc("idxqf", [128, 4], F32)
        idxq = A.alloc("idxq", [128, 4], I32)
        P.op("dve", lambda e: e.tensor_copy(out=ptf[:, :], in_=ptc[:, :]), reads=["ptc"], writes=["ptf"])
        for q in range(4):
            P.op("dve", lambda e, q=q: e.tensor_scalar(out=idxqf[:, q:q + 1], in0=ptf[:, :], scalar1=4.0, scalar2=float(j * NPOOL * 4 + q),
                                                       op0=ALU.mult, op1=ALU.add), reads=["ptf"], writes=["idxqf"])
        P.op("dve", lambda e: e.tensor_copy(out=idxq[:, :], in_=idxqf[:, :]), reads=["idxqf"], writes=["idxq"])
        for rq4 in range(4):
            P.dma("pool", lambda e, rq4=rq4: e.indirect_dma_start(out=G[:, :, :].rearrange("p r d -> p (r d)"), out_offset=None,
                                                                   in_=cki[:, :],
                                                                   in_offset=bass.IndirectOffsetOnAxis(ap=idxq[:, rq4:rq4 + 1], axis=0)),
                  reads=["idxq"], writes=["G"])
            for r4 in range(8):
                bank = ["psE", "psF"][r4 % 2]
                fns = [lambda e, i=i, r4=r4: e.transpose(out=ps[bank][0:64, i * 128:(i + 1) * 128], in_=G[:, r4 * 4 + i, :], identity=identf[:, :]) for i in range(4)]
                P.op("pe", seq(fns), reads=["G", "identf"], writes=[bank])
                P.op("dve", lambda e, r4=r4, bank=bank: e.tensor_copy(out=kiq[0:64, r4 * 512:(r4 + 1) * 512], in_=ps[bank][0:64, :]), reads=[bank], writes=["kiq"])
            for kt in range(8):
                packed_tile(kiq[0:64, kt * 512:(kt + 1) * 512], "kiq", 512,
                            scores[0:4, rq4 * 4096 + kt * 512: rq4 * 4096 + (kt + 1) * 512], kt)
        packed_tile(kiT[0:64, 0:4], "kiT", 4, scores[0:4, PAST:PAST + 4], 0, mask_new=True)
        bisect(scores, "scores", 4, NKS, theta, work)
        mtb = ps["psD"][:, :].bitcast(BF16)
        for kt5 in range(33):
            ncols = 512 if kt5 < 32 else 4
            c0 = kt5 * 512
            P.op("dve", lambda e, c0=c0, ncols=ncols: e.tensor_scalar(out=maskc[0:4, 0:ncols], in0=scores[0:4, c0:c0 + ncols], scalar1=theta[0:4, 0:1],
                                                                      scalar2=None, op0=ALU.is_gt), reads=["scores", "theta"], writes=["maskc"])
            nsub = (ncols + 127) // 128
            fns = [lambda e, i=i, ncols=ncols: e.transpose(out=mtb[0:min(128, ncols - i * 128), i * 4:i * 4 + 4], in_=maskc[0:4, i * 128:min((i + 1) * 128, ncols)],
                                                           identity=identb[0:4, 0:4]) for i in range(nsub)]
            P.op("pe", seq(fns), reads=["maskc", "identb"], writes=["psD"])
            rows = 128 if kt5 < 32 else 4
            P.op("act", lambda e, kt5=kt5, nsub=nsub, rows=rows: e.copy(out=maskT[0:rows, kt5 * 4:kt5 * 4 + nsub, :],
                                                                        in_=mtb[0:rows, 0:nsub * 4].rearrange("p (i q) -> p i q", i=nsub)),
                 reads=["psD"], writes=["maskT"])
        P.op("dve", lambda e: e.memset(oacc[:, :], 0.0), writes=["oacc"])
        idxT = A.alloc("idxT", [128, 128], I32)
        idxTf = A.alloc("idxTf", [128, 128], F32)
        rowi = A.alloc("rowi", [128, 128], F32)
        P.dma("sp", lambda e: e.dma_start(out=rowi[:, :], in_=c_iota[:, :]), writes=["rowi"])
        P.op("dve", lambda e: e.tensor_scalar(out=idxTf[:, 0:1], in0=ptf[:, :], scalar1=128.0, scalar2=float(j * NPOOL * 128), op0=ALU.mult, op1=ALU.add),
             reads=["ptf"], writes=["idxTf"])
        P.op("dve", lambda e: e.tensor_scalar(out=rowi[:, :], in0=rowi[:, :], scalar1=idxTf[:, 0:1], scalar2=None, op0=ALU.add), reads=["rowi", "idxTf"], writes=["rowi"])
        P.op("dve", lambda e: e.tensor_copy(out=idxT[:, :], in_=rowi[:, :]), reads=["rowi"], writes=["idxT"])
        ckj = ck
        cvj = cv
        for r in range(129):
            newt = (r == 128)
            rows = 128 if not newt else 4
            kp = kpg[r % 2]; kpn = "kpg%d" % (r % 2)
            vp = vpg[r % 2]; vpn = "vpg%d" % (r % 2)
            if not newt:
                P.dma("pool", lambda e, r=r, kp=kp: e.indirect_dma_start(out=kp[:, :], out_offset=None, in_=ckj[:, :],
                                                                          in_offset=bass.IndirectOffsetOnAxis(ap=idxT[:, r:r + 1], axis=0)),
                      reads=["idxT"], writes=[kpn])
                P.dma("pool", lambda e, r=r, vp=vp: e.indirect_dma_start(out=vp[:, :], out_offset=None, in_=cvj[:, :],
                                                                          in_offset=bass.IndirectOffsetOnAxis(ap=idxT[:, r:r + 1], axis=0)),
                      reads=["idxT"], writes=[vpn])
                fns = [lambda e, n=n, kp=kp: e.transpose(out=ps["psE"][:, n * 128:(n + 1) * 128], in_=kp[:, n * 128:(n + 1) * 128], identity=identf[:, :]) for n in range(4)]
                P.op("pe", seq(fns), reads=[kpn, "identf"], writes=["psE"])
                P.op("act", lambda e: e.copy(out=ktb[:, :, :].rearrange("p n k -> p (n k)"), in_=ps["psE"][:, :]), reads=["psE"], writes=["ktb"])
                P.op("pool", lambda e, vp=vp: e.tensor_copy(out=vpb[:, :], in_=vp[:, :]), reads=[vpn], writes=["vpb"])
                mt_idx = (r // 32) * 32 + (r % 32)
                KTn = lambda n: ktb[:, n, :]
                Vn = lambda n: vpb[:, n * 128:(n + 1) * 128]
                vname, kname = "vpb", "ktb"
            else:
                mt_idx = 128
                KTn = lambda n: KT[:, n, 0:4]
                Vn = lambda n: Vb[0:4, 0, n * 128:(n + 1) * 128]
                vname, kname = "Vb", "KT"
            fns = [mm(ps["psF"][0:rows, n * 16:(n + 1) * 16], KTn(n), qT[:, 4 * n:4 * n + 4, 0:4], True, True) for n in range(4)]
            P.op("pe", seq(fns), reads=[kname, "qT"], writes=["psF"])
            E = Eb[r % 2]; en = "Eb%d" % (r % 2)
            Pt = Pb[r % 2]; pn = "Pb%d" % (r % 2)
            P.op("act", lambda e, E=E, rows=rows: e.activation(out=E[0:rows, 0:64], in_=ps["psF"][0:rows, 0:64], func=AF.Exp, scale=scale), reads=["psF"], writes=[en])
            P.op("dve", lambda e, E=E, Pt=Pt, rows=rows, mt_idx=mt_idx: e.tensor_tensor(out=Pt[0:rows, 0:64].rearrange("p (a q) -> p a q", q=4),
                                                                                       in0=E[0:rows, 0:64].rearrange("p (a q) -> p a q", q=4),
                                                                                       in1=maskT[0:rows, mt_idx, :].unsqueeze(1).broadcast_to([rows, 16, 4]), op=ALU.mult),
                 reads=[en, "maskT"], writes=[pn])
            fns = [mm(ps["psG"][:, n * 16:(n + 1) * 16], Vn(n), Pt[0:rows, n * 16:(n + 1) * 16], True, True) for n in range(4)]
            fns.append(mm(ps["psG"][:, 64:128], onesb[0:rows, :], Pt[0:rows, 0:64], True, True))
            P.op("pe", seq(fns), reads=[pn, vname, "onesb"], writes=["psG"])
            P.op("dve", lambda e: e.tensor_tensor(out=oacc[:, :], in0=oacc[:, :], in1=ps["psG"][:, 0:128], op=ALU.add), reads=["psG", "oacc"], writes=["oacc"])
        P.op("dve", lambda e: e.reciprocal(out=rden[:, 0:64], in_=oacc[:, 64:128]), reads=["oacc"], writes=["rden"])
        P.op("dve", lambda e: e.tensor_tensor(out=xbf[:, :, 0:4], in0=oacc[:, 0:64].rearrange("p (h q) -> p h q", q=4),
                                              in1=rden[:, 0:64].rearrange("p (h q) -> p h q", q=4), op=ALU.mult), reads=["oacc", "rden"], writes=["xbf"])

    class StopBuild(Exception):
        pass

    CUR = {}

    def ckpt(name):
        if cfg.get("stop") == name:
            P.barrier()
            A.reset(CUR["mark"])
            store_y(CUR["p"], CUR["N"])
            raise StopBuild()

    try:
      for p in pass_list:
          sample = (p == 4)
          N = 4 if sample else 512
          if sample:
              P.barrier()
              A.reset(cmark)
              xres = A.alloc("xres", [128, NCH, 4], F32)
              xbf = A.alloc("xbf", [128, NCH, 4], BF16)
              rq = A.alloc("rq", [128, 2, 512], F32)
              ri = A.alloc("ri", [128, 2, 512], F32)
              ttq = A.alloc("ttq", [128, 4, 48], F32)
              ghalo = A.alloc("ghalo", [128, 4, NFF, 2], F32)
              ahalo = A.alloc("ahalo", [128, 2, NCH, 30], BF16)
          P.dma("sp", lambda e, p=p: e.dma_start(out=rq[:, :, :], in_=c_ropeq[p].rearrange("c p n -> p c n")), writes=["rq"])
          P.dma("sp", lambda e, p=p: e.dma_start(out=ri[:, :, :], in_=c_ropei[p].rearrange("c p n -> p c n")), writes=["ri"])
          r0 = p * 512 if not sample else SEQ
          if not sample:
              P.dma("sp", lambda e, r0=r0: e.dma_start(out=ttq[:, :, :], in_=c_ttab[r0:r0 + 512, :].rearrange("(t p) c -> p t c", p=128)), writes=["ttq"])
          else:
              P.dma("sp", lambda e: e.dma_start(out=ttq[:, 0, :], in_=c_ttab[SEQ:SEQ + 128, :]), writes=["ttq"])
          if p == 0:
              P.op("pool", lambda e: e.memset(ghalo[:, :, :, :].rearrange("p a b c -> p (a b c)"), 0.0), writes=["ghalo"])
              P.op("pool", lambda e: e.memset(ahalo[:, :, :, :].rearrange("p a b c -> p (a b c)"), 0.0), writes=["ahalo"])
          if sample:
              for l in range(4):
                  for t in range(2):
                      sv = stf[l, t, :].rearrange("(f q) -> q f", q=128)
                      for q4 in range(4):
                          P.dma("sp", lambda e, l=l, q4=q4, sv=sv, t=t: e.dma_start(out=ghalo[:, l, q4 * 11:(q4 + 1) * 11, t], in_=sv[:, q4 * 11:(q4 + 1) * 11],
                                                                                    allow_slow_non_contiguous=True), writes=["ghalo"])
              m = A.mark()
              stg = A.alloc("stage", [64, DFF], F32)
              atmp = A.alloc("atmp", [128, NCH, 30], F32)
              names[id(atmp)] = "atmp"
              for j in range(2):
                  rows_to_fm(stc[j, :, :], 30, D, atmp, stg)
                  P.op("dve", lambda e, j=j: e.tensor_copy(out=ahalo[:, j, :, :], in_=atmp[:, :, :]), reads=["atmp"], writes=["ahalo"])
              P.barrier()
              A.reset(m)
          CUR.update(p=p, N=N, mark=A.mark())
          load_x(p, N)
          ckpt('load_x')
          for i_layer in range(nlayers):
              j = i_layer // 2
              if i_layer % 2 == 0:
                  attn_layer(j, i_layer, p, N)
              else:
                  conv_layer(j, i_layer, p, N)
              ffn_layer(i_layer, p, N)
          store_y(p, N)
    except StopBuild:
        pass
    if DRY:
        return wrecs
    P.barrier()
    P.emit()
    return nc


def _consts():
    theta = np.float32(500000.0)
    invq = (theta ** (-np.arange(0, 32, 2, dtype=np.float32) / np.float32(32))).astype(np.float32)
    invi = (theta ** (-np.arange(0, 16, 2, dtype=np.float32) / np.float32(16))).astype(np.float32)
    pos_all = np.concatenate([np.arange(SEQ, dtype=np.float32), PAST + np.arange(128, dtype=np.float32)])
    angq = pos_all[:, None] * invq[None, :]
    angi = pos_all[:, None] * invi[None, :]
    ttab = np.concatenate([np.cos(angq), np.sin(angq), np.cos(angi), np.sin(angi)], axis=1).astype(np.float32)
    ropeq = np.zeros((5, 2, 128, 512), np.float32)
    ropei = np.zeros((5, 2, 128, 512), np.float32)
    ropeq[:, 0] = 1.0
    ropei[:, 0] = 1.0
    for p in range(5):
        rows = np.arange(p * 512, p * 512 + 512) if p < 4 else np.concatenate([np.arange(SEQ, SEQ + 128)] * 4)
        cq, sq = np.cos(angq[rows]).T, np.sin(angq[rows]).T
        ci, si = np.cos(angi[rows]).T, np.sin(angi[rows]).T
        ropeq[p, 0, 0:16] = cq; ropeq[p, 0, 16:32] = cq
        ropeq[p, 1, 0:16] = sq; ropeq[p, 1, 16:32] = sq
        for o in (0, 64):
            ropei[p, 0, o:o + 8] = ci; ropei[p, 0, o + 8:o + 16] = ci
            ropei[p, 1, o:o + 8] = si; ropei[p, 1, o + 8:o + 16] = si
    pq = np.zeros((128, 128), np.float32)
    for m in range(16):
        pq[m + 16, m] = -1.0
        pq[m, m + 16] = 1.0
    pi = np.zeros((128, 128), np.float32)
    for o in (0, 64):
        for m in range(8):
            pi[o + m + 8, o + m] = -1.0
            pi[o + m, o + m + 8] = 1.0
    cm = np.where(np.arange(128)[None, :] <= np.arange(128)[:, None], 0.0, NEG).astype(np.float32)
    return dict(c_ident=np.eye(128, dtype=np.float32), c_pq=pq, c_pi=pi, c_ropeq=ropeq, c_ropei=ropei,
                c_ttab=np.ascontiguousarray(ttab[:SEQ + 128]), c_cmask=cm,
                c_iota=np.ascontiguousarray(np.broadcast_to(np.arange(128, dtype=np.float32)[None, :], (128, 128))))


def make_in_maps(inp, ncores=8):
    f = lambda a: np.ascontiguousarray(np.asarray(a, dtype=np.float32))
    cst = _consts()
    vecD = np.concatenate([f(inp["b_dw"]), f(inp["ln_conv_g"]), f(inp["ln_conv_b"]), f(inp["b_pw2"]),
                           f(inp["ln_mix_g"]), f(inp["ln_mix_b"]), f(inp["ln_ffn_g"]), f(inp["ln_ffn_b"]),
                           f(inp["b_pw1"]).reshape(4, D)], axis=0)
    vecF = np.concatenate([np.concatenate([f(inp["w_ffn_conv"])[l], f(inp["b_ffn_conv"])[l][None]], axis=0) for l in range(4)], axis=0)
    shared = dict(
        ck=f(inp["cache_k"]).reshape(-1, 512), cv=f(inp["cache_v"]).reshape(-1, 512),
        cki=f(inp["cache_kidx"]).reshape(-1, 2048),
        w_in=f(inp["w_attn_in"]), w_out=f(inp["w_attn_out"]), w_pw1=f(inp["w_pw1"]), w_pw2=f(inp["w_pw2"]),
        w_g=f(inp["w_ffn_gate"]), w_u=f(inp["w_ffn_up"]), w_d=f(inp["w_ffn_down"]),
        vecD=np.ascontiguousarray(vecD), vecF=np.ascontiguousarray(vecF), wdw=f(inp["w_dw"]).reshape(62, D), **cst)
    maps = []
    for c in range(ncores):
        m = dict(shared)
        m["xp"] = f(inp["x_prompt"])[c % 4]
        m["xs"] = f(inp["x_sample"])[c]
        m["stc"] = np.ascontiguousarray(f(inp["state_conv"])[:, c])
        m["stf"] = np.ascontiguousarray(f(inp["state_ffn"])[:, c])
        m["pt"] = np.ascontiguousarray(np.asarray(inp["page_table"], dtype=np.int32)[c])
        maps.append(m)
    return maps


def assemble(res):
    r = res
    y_p = np.stack([r[c]["y_p"] for c in range(4)])
    y_s = np.stack([r[c]["y_s"] for c in range(8)])
    k_p = np.stack([r[c]["k_p"] for c in range(4)], axis=1).reshape(2, 4, SEQ, 4, 128)
    v_p = np.stack([r[c]["v_p"] for c in range(4)], axis=1).reshape(2, 4, SEQ, 4, 128)
    ki_p = np.stack([r[c]["ki_p"] for c in range(4)], axis=1)
    conv_p = np.stack([r[c]["conv_p"] for c in range(4)], axis=1)
    ffn_p = np.stack([r[c]["ffn_p"] for c in range(4)], axis=1)
    k_s = np.stack([r[c]["k_s"] for c in range(8)], axis=1).reshape(2, 8, 4, 4, 128)
    v_s = np.stack([r[c]["v_s"] for c in range(8)], axis=1).reshape(2, 8, 4, 4, 128)
    ki_s = np.stack([r[c]["ki_s"] for c in range(8)], axis=1)
    conv_s = np.stack([r[c]["conv_s"] for c in range(8)], axis=1)
    ffn_s = np.stack([r[c]["ffn_s"] for c in range(8)], axis=1)
    outs = (y_p, y_s, k_p, v_p, ki_p, conv_p, ffn_p, k_s, v_s, ki_s, conv_s, ffn_s)
    return tuple(np.ascontiguousarray(o, dtype=np.float32) for o in outs)


def kernel(**inputs):
    nc = build({})
    maps = make_in_maps(inputs)
    res = run_bass_kernel_spmd(nc, maps, core_ids=list(range(8)))
    return assemble(res.results)
```

```python
import numpy as np
import concourse.bass as bass
import concourse.mybir as mybir
from concourse.bass_utils import run_bass_kernel_spmd

F32 = mybir.dt.float32
BF16 = mybir.dt.bfloat16
I32 = mybir.dt.int32
ALU = mybir.AluOpType
AF = mybir.ActivationFunctionType
AX = mybir.AxisListType

ENGS = ("pe", "act", "dve", "pool", "sp")
EPOCH = 12000
NDMA = 24

D = 2048
NCH = 16
DFF = 5632
NFF = 44
SEQ = 2048
PAST = 16384
NPG = 128
ALPHA = float((2 * 4) ** 0.25)
EPS = 1e-5
NEG = -2048.0
IN_COLS = 4176


class Prog:
    def __init__(self, nc, dry=False):
        self.nc = nc
        self.dry = dry
        self.q = {e: [] for e in ENGS}
        self.cnt = {e: 0 for e in ENGS}
        self.last_w = {}
        self.readers = {}
        self.seen = {e: {} for e in ENGS}
        self.dma_i = 0
        self.dma_val = [0] * NDMA
        self.sems = {}
        self.h = {"pe": nc.tensor, "act": nc.scalar, "dve": nc.vector, "pool": nc.gpsimd, "sp": nc.sync}

    def sem(self, key):
        if key not in self.sems:
            self.sems[key] = self.nc.alloc_semaphore("s_%s_%s" % (key[0], key[1]))
        return self.sems[key]

    def _ev_sem(self, ev):
        return self.sem((ev[0], ev[1])), ev[2]

    def _need(self, eng, ev):
        key = (ev[0], ev[1])
        if self.seen[eng].get(key, 0) >= ev[2]:
            return False
        self.seen[eng][key] = ev[2]
        return True

    def _deps(self, eng, reads, writes):
        evs = []
        for r in reads:
            if r in self.last_w:
                evs.append(self.last_w[r])
        for w in writes:
            if w in self.last_w:
                evs.append(self.last_w[w])
            evs.extend(self.readers.get(w, ()))
        out = []
        for ev in evs:
            if ev[3] == eng and eng == "pe":
                continue
            if self._need(eng, ev):
                out.append(ev)
        return out

    def _record(self, ev, reads, writes):
        for r in reads:
            self.readers.setdefault(r, []).append(ev)
        for w in writes:
            self.last_w[w] = ev
            self.readers[w] = []

    def op(self, eng, fn, reads=(), writes=()):
        if self.dry:
            return
        psr = [r for r in reads if r.startswith("ps")]
        if psr:
            writes = list(writes) + [r for r in psr if r not in writes]
            reads = [r for r in reads if not r.startswith("ps")]
        deps = self._deps(eng, reads, writes)
        c = self.cnt[eng]
        ep, val = c // EPOCH, c % EPOCH + 1
        self.cnt[eng] = c + 1
        ev = (eng, ep, val, eng)
        waits = [self._ev_sem(d) for d in deps]
        mysem = self.sem((eng, ep))

        def run(e):
            for s, v in waits:
                e.wait_ge(s, v)
            ins = fn(e)
            ins.then_inc(mysem, 1)

        run(self.h[eng])
        self._record(ev, reads, writes)

    def dma(self, qeng, fn, reads=(), writes=()):
        if self.dry:
            return
        slot = self.dma_i % NDMA
        self.dma_i += 1
        deps = self._deps(qeng, reads, writes)
        prev = self.dma_val[slot]
        if prev > 0:
            pe = ("dma", slot, prev, "dma")
            if self._need(qeng, pe):
                deps.append(pe)
        val = prev + 16
        self.dma_val[slot] = val
        ev = ("dma", slot, val, "dma")
        waits = [self._ev_sem(d) for d in deps]
        mysem = self.sem(("dma", slot))

        def run(e):
            for s, v in waits:
                e.wait_ge(s, v)
            ins = fn(e)
            ins.then_inc(mysem, 16)

        run(self.h[qeng])
        self._record(ev, reads, writes)

    def barrier(self):
        if self.dry:
            return
        evs = []
        for e in ENGS:
            c = self.cnt[e]
            if c > 0:
                evs.append((e, (c - 1) // EPOCH, (c - 1) % EPOCH + 1, e))
        for s in range(NDMA):
            if self.dma_val[s] > 0:
                evs.append(("dma", s, self.dma_val[s], "dma"))
        for e in ENGS:
            waits = []
            for ev in evs:
                if ev[3] == e:
                    continue
                if self._need(e, ev):
                    waits.append(self._ev_sem(ev))
            if waits:
                def run(h, waits=waits):
                    for s, v in waits:
                        h.wait_ge(s, v)
                run(self.h[e])

    def emit(self):
        pass


def mm(out, lhsT, rhs, st, sp):
    return lambda e: e.matmul(out, lhsT=lhsT, rhs=rhs, start=st, stop=sp)


def seq(fns):
    def run(e):
        ins = None
        for f in fns:
            ins = f(e)
        return ins
    return run


class Arena:
    def __init__(self, nc):
        self.nc = nc
        rem = nc.sbuf_bytes_remaining
        size = (rem - 512) // 64 * 64
        r = nc.bump_sbuf(size)
        self.base = r[0]
        self.end = r[0] + size
        self.cur = self.base
        self.uid = 0

    def alloc(self, name, shape, dt):
        esz = 4 if dt in (F32, I32) else 2
        n = 1
        for s in shape[1:]:
            n *= s
        nbytes = (n * esz + 63) // 64 * 64
        assert self.cur + nbytes <= self.end, "SBUF arena overflow at %s (%d over)" % (name, self.cur + nbytes - self.end)
        self.uid += 1
        t = self.nc.alloc_sbuf_tensor_at("%s_%d" % (name, self.uid), list(shape), dt, offset=self.cur)
        self.cur += nbytes
        return t

    def mark(self):
        return self.cur

    def reset(self, m):
        self.cur = m


def build(cfg):
    if "_wrec" not in cfg and not cfg.get("_dry"):
        recs = build(dict(cfg, _dry=True))
        cfg = dict(cfg, _wrec=recs)
    DRY = bool(cfg.get("_dry"))
    WREC = cfg.get("_wrec")
    npass = cfg.get("npass", 5)
    nlayers = cfg.get("nlayers", 4)
    pass_list = cfg.get("passes", list(range(npass)))
    nc = bass.Bass("TRN2", target_bir_lowering=False)
    P = Prog(nc, dry=DRY)

    def din(name, shape, dt=F32):
        return nc.dram_tensor(name, list(shape), dt, kind="ExternalInput").ap()

    def dout(name, shape):
        return nc.dram_tensor(name, list(shape), F32, kind="ExternalOutput").ap()

    xp = din("xp", [SEQ, D]); xs = din("xs", [4, D])
    NPOOL = cfg.get("npool", 1280)
    ck = din("ck", [2 * NPOOL * 128, 512]); cv = din("cv", [2 * NPOOL * 128, 512]); cki = din("cki", [2 * NPOOL * 4, 2048])
    stc = din("stc", [2, 30, D]); stf = din("stf", [4, 2, DFF]); pt = din("pt", [128], I32)
    w_in = din("w_in", [2, D, IN_COLS]); w_out = din("w_out", [2, D, D]); w_pw1 = din("w_pw1", [2, D, 2 * D])
    w_pw2 = din("w_pw2", [2, D, D]); w_g = din("w_g", [4, D, DFF]); w_u = din("w_u", [4, D, DFF]); w_d = din("w_d", [4, DFF, D])
    WD = {"w_in": w_in, "w_out": w_out, "w_pw1": w_pw1, "w_pw2": w_pw2, "w_g": w_g, "w_u": w_u, "w_d": w_d}
    vecD = din("vecD", [28, D]); vecF = din("vecF", [16, DFF]); wdw = din("wdw", [62, D])
    c_ident = din("c_ident", [128, 128]); c_pq = din("c_pq", [128, 128]); c_pi = din("c_pi", [128, 128])
    c_ropeq = din("c_ropeq", [5, 2, 128, 512]); c_ropei = din("c_ropei", [5, 2, 128, 512])
    c_ttab = din("c_ttab", [SEQ + 128, 48]); c_cmask = din("c_cmask", [128, 128]); c_iota = din("c_iota", [128, 128])

    y_p = dout("y_p", [SEQ, D]); y_s = dout("y_s", [4, D])
    k_p = dout("k_p", [2, SEQ, 512]); v_p = dout("v_p", [2, SEQ, 512]); ki_p = dout("ki_p", [2, SEQ, 64])
    conv_p = dout("conv_p", [2, 30, D]); ffn_p = dout("ffn_p", [4, 2, DFF])
    k_s = dout("k_s", [2, 4, 512]); v_s = dout("v_s", [2, 4, 512]); ki_s = dout("ki_s", [2, 4, 64])
    conv_s = dout("conv_s", [2, 30, D]); ffn_s = dout("ffn_s", [4, 2, DFF])

    scrKT = nc.dram_tensor("scrKT", [2, 128, 4, SEQ], BF16).ap()
    scrV = nc.dram_tensor("scrV", [2, 128, 16, 512], BF16).ap()
    scrKI = nc.dram_tensor("scrKI", [2, 128, SEQ], BF16).ap()

    A = Arena(nc)
    PSN = ["psA", "psB", "psC", "psD", "psE", "psF", "psG", "psH"]
    ps = {n: nc.alloc_psum_tensor(n, [128, 512], F32) for n in PSN}

    identf = A.alloc("identf", [128, 128], F32)
    identb = A.alloc("identb", [128, 128], BF16)
    onesf = A.alloc("onesf", [128, 128], F32)
    onesb = A.alloc("onesb", [128, 128], BF16)
    pqb = A.alloc("pqb", [128, 128], BF16)
    pib = A.alloc("pib", [128, 128], BF16)
    cmask = A.alloc("cmask", [128, 128], F32)
    epsT = A.alloc("epsT", [128, 1], F32)
    cmaskp = A.alloc("cmaskp", [128, 512], BF16)
    vD = A.alloc("vD", [128, NCH, 28], F32)
    vF = A.alloc("vF", [128, NFF, 16], F32)
    wdwT = A.alloc("wdwT", [128, NCH, 62], F32)
    Wt = [A.alloc("W0", [128, 8192], BF16), A.alloc("W1", [128, 8192], BF16)]
    cmark = A.mark()
    xres = A.alloc("xres", [128, NCH, 512], F32)
    xbf = A.alloc("xbf", [128, NCH, 512], BF16)
    rq = A.alloc("rq", [128, 2, 512], F32)
    ri = A.alloc("ri", [128, 2, 512], F32)
    ttq = A.alloc("ttq", [128, 4, 48], F32)
    ghalo = A.alloc("ghalo", [128, 4, NFF, 2], F32)
    ahalo = A.alloc("ahalo", [128, 2, NCH, 30], BF16)
    pmark = A.mark()

    wst = {"ptr": 0, "issued": 0}
    wrecs = []

    WS = {}
    if not DRY:
        WS["T"] = len(WREC) // max(1, len(pass_list))
        WS["half"] = 96
        WS["scrs"] = [nc.dram_tensor("wscr%d" % q, [WS["half"], 128, 8192], BF16).ap() for q in range((max(1, WS["T"]) + 95) // 96)]
        WS["cache"] = (len(WREC) == WS["T"] * len(pass_list)) and len(pass_list) > 1 and not cfg.get("nowcache")

    def _wissue(i):
        rec = WREC[i]
        sl = i % 2
        wn_ = "W%d" % sl
        T_ = WS["T"]
        n_ = 4096 if rec[0] == "kiwi" else rec[3] * rec[5]
        if WS["cache"] and i >= T_:
            assert WREC[i % T_] == rec
            src_ = WS["scrs"][(i % T_) // 96][(i % T_) % 96]
            P.dma("sp", lambda e: e.dma_start(out=Wt[sl][:, 0:n_], in_=src_[:, 0:n_]), reads=["wscr%d" % (i % T_)], writes=[wn_])
            return
        if rec[0] == "kiwi":
            wj_ = w_in[rec[1]]
            vkw_ = Wt[sl][:, 0:16 * 256].rearrange("p (k n) -> p k n", k=16)
            srcki = wv(wj_, 0, 16, 4096, 64)
            P.dma("pool", lambda e: e.dma_start(out=vkw_[:, :, 0:64], in_=srcki), writes=[wn_])
            P.dma("pool", lambda e: e.dma_start(out=vkw_[:, :, 64:128], in_=srcki), writes=[wn_])
            P.dma("pool", lambda e: e.dma_start(out=vkw_[:, :, 128:208], in_=wv(wj_, 0, 16, 4096, 80)), writes=[wn_])
        else:
            kind, li, k0, kc, c0, ncol = rec
            view_ = Wt[sl][:, 0:kc * ncol].rearrange("p (k n) -> p k n", k=kc)
            src_ = wv(WD[kind][li], k0, kc, c0, ncol)
            P.dma("pool", lambda e: e.dma_start(out=view_, in_=src_), writes=[wn_])
        if WS["cache"]:
            dst_ = WS["scrs"][i // 96][i % 96]
            P.dma("sp", lambda e: e.dma_start(out=dst_[:, 0:n_], in_=Wt[sl][:, 0:n_]), reads=[wn_], writes=["wscr%d" % i])

    def wload(kind, li, k0=0, kc=16, c0=0, ncol=512):
        rec = (kind, li, k0, kc, c0, ncol)
        i = wst["ptr"]
        wst["ptr"] += 1
        wrecs.append(rec)
        sl = i % 2
        if kind == "kiwi":
            view = Wt[sl][:, 0:16 * 256].rearrange("p (k n) -> p k n", k=16)
        else:
            view = Wt[sl][:, 0:kc * ncol].rearrange("p (k n) -> p k n", k=kc)
        if DRY:
            return view, "W%d" % sl
        assert WREC[i] == rec, (i, WREC[i], rec)
        while wst["issued"] < min(i + 2, len(WREC)):
            _wissue(wst["issued"])
            wst["issued"] += 1
        return view, "W%d" % sl

    def wv(w2d, k0, kc, c0, ncol):
        return w2d.rearrange("(k p) n -> p k n", p=128)[:, k0:k0 + kc, c0:c0 + ncol]

    tmpc = A.alloc("tmpc", [128, 128], F32)
    P.dma("sp", lambda e: e.dma_start(out=identf[:, :], in_=c_ident[:, :]), writes=["identf"])
    P.dma("sp", lambda e: e.dma_start(out=cmask[:, :], in_=c_cmask[:, :]), writes=["cmask"])
    P.op("dve", lambda e: e.tensor_copy(out=identb[:, :], in_=identf[:, :]), reads=["identf"], writes=["identb"])
    P.op("dve", lambda e: e.memset(onesf[:, :], 1.0), writes=["onesf"])
    P.op("dve", lambda e: e.memset(onesb[:, :], 1.0), writes=["onesb"])
    P.op("dve", lambda e: e.memset(epsT[:, :], EPS), writes=["epsT"])
    P.op("dve", lambda e: e.memset(cmaskp[:, :], 0.0), writes=["cmaskp"])
    P.op("dve", lambda e: e.tensor_copy(out=cmaskp[:, 384:512], in_=cmask[:, :]), reads=["cmask"], writes=["cmaskp"])
    P.dma("sp", lambda e: e.dma_start(out=tmpc[:, :], in_=c_pq[:, :]), writes=["tmpc"])
    P.op("dve", lambda e: e.tensor_copy(out=pqb[:, :], in_=tmpc[:, :]), reads=["tmpc"], writes=["pqb"])
    P.dma("sp", lambda e: e.dma_start(out=tmpc[:, :], in_=c_pi[:, :]), reads=["tmpc"], writes=["tmpc"])
    P.op("dve", lambda e: e.tensor_copy(out=pib[:, :], in_=tmpc[:, :]), reads=["tmpc"], writes=["pib"])

    def rows_to_fm(rows_dram, R, L, dst, stage):
        P.dma("sp", lambda e: e.dma_start(out=stage[0:R, 0:L], in_=rows_dram), writes=["stage"])
        nchunk = L // 128
        per = 512 // R
        c = 0
        bi = 0
        while c < nchunk:
            n = min(per, nchunk - c)
            bank = ["psA", "psB"][bi % 2]
            bi += 1
            fns = []
            for i in range(n):
                fns.append(lambda e, i=i, c=c: e.transpose(out=ps[bank][:, i * R:(i + 1) * R],
                                                          in_=stage[0:R, (c + i) * 128:(c + i + 1) * 128],
                                                          identity=identf[0:R, 0:R]))
            P.op("pe", seq(fns), reads=["stage", "identf"], writes=[bank])
            P.op("dve", lambda e, c=c, n=n, bank=bank: e.tensor_copy(
                out=dst[:, c:c + n, 0:R], in_=ps[bank][:, 0:n * R].rearrange("p (i r) -> p i r", r=R)),
                reads=[bank], writes=[dst_name(dst)])
            c += n

    names = {}

    def dst_name(t):
        return names[id(t)]

    m0 = A.mark()
    stage = A.alloc("stage", [64, DFF], F32)
    names[id(vD)] = "vD"; names[id(vF)] = "vF"; names[id(wdwT)] = "wdwT"
    rows_to_fm(vecD[:, :], 28, D, vD, stage)
    rows_to_fm(vecF[:, :], 16, DFF, vF, stage)
    rows_to_fm(wdw[:, :], 62, D, wdwT, stage)
    P.barrier()
    A.reset(m0)

    def vcol(row, c):
        return vD[:, c, row:row + 1]

    PS_ROT = {"i": 0}

    def load_x(p, N):
        m = A.mark()
        xtm = A.alloc("xtm", [128, D], F32)
        ntt = (N + 127) // 128
        for tt in range(ntt):
            nt = min(128, N - tt * 128)
            src = xp[p * 512 + tt * 128: p * 512 + tt * 128 + nt, :] if p < 4 else xs[0:nt, :]
            P.dma("sp", lambda e, src=src, nt=nt: e.dma_start(out=xtm[0:nt, :], in_=src), writes=["xtm"])
            for c4 in range(4):
                bank = ["psA", "psB"][c4 % 2]
                fns = [lambda e, i=i, c4=c4, nt=nt: e.transpose(out=ps[bank][:, i * 128:i * 128 + nt],
                                                                 in_=xtm[0:nt, (c4 * 4 + i) * 128:(c4 * 4 + i + 1) * 128],
                                                                 identity=identf[0:nt, 0:nt]) for i in range(4)]
                P.op("pe", seq(fns), reads=["xtm", "identf"], writes=[bank])
                srcv = ps[bank][:, :].rearrange("p (i t) -> p i t", i=4)[:, :, 0:nt]
                P.op("dve", lambda e, c4=c4, tt=tt, nt=nt, srcv=srcv: e.tensor_copy(
                    out=xres[:, c4 * 4:c4 * 4 + 4, tt * 128:tt * 128 + nt], in_=srcv), reads=[bank], writes=["xres"])
                P.op("act", lambda e, c4=c4, tt=tt, nt=nt, srcv=srcv: e.copy(
                    out=xbf[:, c4 * 4:c4 * 4 + 4, tt * 128:tt * 128 + nt], in_=srcv), reads=[bank], writes=["xbf"])
        P.barrier()
        A.reset(m)

    def store_y(p, N):
        m = A.mark()
        ytm = A.alloc("ytm", [128, D], F32)
        ntt = (N + 127) // 128
        for tt in range(ntt):
            nt = min(128, N - tt * 128)
            for c4 in range(4):
                bank = ["psA", "psB"][c4 % 2]
                fns = [lambda e, i=i, c4=c4, nt=nt, tt=tt: e.transpose(out=ps[bank][0:nt, i * 128:(i + 1) * 128],
                                                                        in_=xres[:, c4 * 4 + i, tt * 128:tt * 128 + nt],
                                                                        identity=identf[:, :]) for i in range(4)]
                P.op("pe", seq(fns), reads=["xres", "identf"], writes=[bank])
                P.op("dve", lambda e, c4=c4, nt=nt: e.tensor_copy(out=ytm[0:nt, c4 * 512:(c4 + 1) * 512], in_=ps[bank][0:nt, :]),
                     reads=[bank], writes=["ytm"])
            dst = y_p[p * 512 + tt * 128: p * 512 + tt * 128 + nt, :] if p < 4 else y_s[0:nt, :]
            P.dma("sp", lambda e, dst=dst, nt=nt: e.dma_start(out=dst, in_=ytm[0:nt, :]), reads=["ytm"])
        P.barrier()
        A.reset(m)

    class LN:
        def __init__(self, N, src, resname):
            self.N = N
            self.src = src
            self.res = resname
            self.sq = [A.alloc("lnsq0", [128, 512], F32), A.alloc("lnsq1", [128, 512], F32)]
            self.mean = A.alloc("lnmean", [128, 512], F32)
            self.rstd = A.alloc("lnrstd", [128, 512], F32)
            self.nmr = A.alloc("lnnmr", [128, 512], F32)

        def stats(self, c):
            N = self.N
            sq = self.sq[c % 2]
            sqn = "lnsq%d" % (c % 2)
            z = self.src[:, c, 0:N]
            P.op("act", lambda e: e.activation(out=sq[:, 0:N], in_=z, func=AF.Square), reads=[self.res], writes=[sqn])
            P.op("pe", seq([mm(ps["psG"][:, 0:N], onesf[:, :], z, c == 0, c == NCH - 1),
                            mm(ps["psH"][:, 0:N], onesf[:, :], sq[:, 0:N], c == 0, c == NCH - 1)]),
                 reads=[self.res, sqn, "onesf"], writes=["psG", "psH"])

        def finalize(self):
            N = self.N
            mean, rstd, nmr = self.mean, self.rstd, self.nmr
            P.op("dve", lambda e: e.tensor_scalar(out=mean[:, 0:N], in0=ps["psG"][:, 0:N], scalar1=1.0 / D, scalar2=None, op0=ALU.mult),
                 reads=["psG"], writes=["lnmean"])
            P.op("dve", lambda e: e.tensor_tensor(out=nmr[:, 0:N], in0=mean[:, 0:N], in1=mean[:, 0:N], op=ALU.mult),
                 reads=["lnmean"], writes=["lnnmr"])
            P.op("dve", lambda e: e.scalar_tensor_tensor(out=rstd[:, 0:N], in0=ps["psH"][:, 0:N], scalar=1.0 / D, in1=nmr[:, 0:N],
                                                         op0=ALU.mult, op1=ALU.subtract), reads=["psH", "lnnmr"], writes=["lnrstd"])
            P.op("act", lambda e: e.activation(out=rstd[:, 0:N], in_=rstd[:, 0:N], func=AF.Sqrt, bias=epsT[:, 0:1], scale=1.0),
                 reads=["lnrstd", "epsT"], writes=["lnrstd"])
            P.op("dve", lambda e: e.reciprocal(out=rstd[:, 0:N], in_=rstd[:, 0:N]), reads=["lnrstd"], writes=["lnrstd"])
            P.op("dve", lambda e: e.tensor_tensor(out=nmr[:, 0:N], in0=mean[:, 0:N], in1=rstd[:, 0:N], op=ALU.mult),
                 reads=["lnmean", "lnrstd"], writes=["lnnmr"])

        def apply(self, c, g_ap, b_ap, out_f32=None, out_f32_name=None, out_bf=None, out_bf_name=None, func=None):
            N = self.N
            z = self.src[:, c, 0:N]
            t = self.sq[c % 2]
            tn = "lnsq%d" % (c % 2)
            P.op("dve", lambda e: e.tensor_tensor(out=t[:, 0:N], in0=z, in1=self.rstd[:, 0:N], op=ALU.mult),
                 reads=[self.res, "lnrstd"], writes=[tn])
            P.op("dve", lambda e: e.tensor_tensor(out=t[:, 0:N], in0=t[:, 0:N], in1=self.nmr[:, 0:N], op=ALU.subtract),
                 reads=[tn, "lnnmr"], writes=[tn])
            if out_f32 is not None:
                P.op("act", lambda e: e.activation(out=out_f32, in_=t[:, 0:N], func=AF.Identity, bias=b_ap, scale=g_ap),
                     reads=[tn, "vD"], writes=[out_f32_name])
                if out_bf is not None:
                    P.op("pool", lambda e: e.tensor_copy(out=out_bf, in_=out_f32), reads=[out_f32_name], writes=[out_bf_name])
            else:
                P.op("act", lambda e: e.activation(out=out_bf, in_=t[:, 0:N], func=func, bias=b_ap, scale=g_ap),
                     reads=[tn, "vD"], writes=[out_bf_name])

    def out_proj_ln(N, wkind, wli, rhs_t, rhs_name, nk, g_row, b_row, bias_row=None):
        m = A.mark()
        ln = LN(N, xres, "xres")
        btmp = A.alloc("btmp", [128, 512], F32)
        for g4 in range(4):
            k0 = 0
            banks = ["psA", "psB", "psC", "psD"]
            while k0 < nk:
                kc = min(16, nk - k0)
                view, wn = wload(wkind, wli, k0, kc, g4 * 512, 512)
                for i in range(4):
                    fns = [mm(ps[banks[i]][:, 0:N], view[:, k, i * 128:(i + 1) * 128], rhs_t[:, k0 + k, 0:N],
                              (k0 + k) == 0, (k0 + k) == nk - 1) for k in range(kc)]
                    P.op("pe", seq(fns), reads=[wn, rhs_name], writes=[banks[i]])
                k0 += kc
            for i in range(4):
                c = g4 * 4 + i
                bank = banks[i]
                if bias_row is not None:
                    P.op("act", lambda e, bank=bank, c=c: e.activation(out=btmp[:, 0:N], in_=ps[bank][:, 0:N], func=AF.Identity,
                                                                       bias=vcol(bias_row, c), scale=1.0),
                         reads=[bank, "vD"], writes=["btmp"])
                    P.op("dve", lambda e, c=c: e.scalar_tensor_tensor(out=xres[:, c, 0:N], in0=xres[:, c, 0:N], scalar=ALPHA,
                                                                      in1=btmp[:, 0:N], op0=ALU.mult, op1=ALU.add),
                         reads=["btmp", "xres"], writes=["xres"])
                else:
                    P.op("dve", lambda e, bank=bank, c=c: e.scalar_tensor_tensor(out=xres[:, c, 0:N], in0=xres[:, c, 0:N], scalar=ALPHA,
                                                                                 in1=ps[bank][:, 0:N], op0=ALU.mult, op1=ALU.add),
                         reads=[bank, "xres"], writes=["xres"])
                ln.stats(c)
        ln.finalize()
        for c in range(NCH):
            ln.apply(c, vcol(g_row, c), vcol(b_row, c), out_f32=xres[:, c, 0:N], out_f32_name="xres",
                     out_bf=xbf[:, c, 0:N], out_bf_name="xbf")
        P.barrier()
        A.reset(m)

    def ffn_layer(l, p, N):
        m = A.mark()
        hT = A.alloc("hT", [128, NFF, 512], BF16)
        gsb = [A.alloc("gsb0", [128, 516], F32), A.alloc("gsb1", [128, 516], F32)]
        cc = [A.alloc("cc%d" % i, [128, 512], F32) for i in range(4)]
        for ft in range(11):
            vg, gn = wload("w_g", l, 0, 16, ft * 512, 512)
            for i in range(4):
                f = ft * 4 + i
                bg = ["psA", "psC"][i % 2]
                P.op("pe", seq([mm(ps[bg][:, 0:N], vg[:, k, i * 128:(i + 1) * 128], xbf[:, k, 0:N], k == 0, k == 15) for k in range(16)]),
                     reads=[gn, "xbf"], writes=[bg])
                g = gsb[f % 2]
                gnm = "gsb%d" % (f % 2)
                c_ = cc[i]
                cn = "cc%d" % i
                P.op("dve", lambda e, g=g, f=f: e.tensor_copy(out=g[:, 0:2], in_=ghalo[:, l, f, :]), reads=["ghalo"], writes=[gnm])
                P.op("act", lambda e, g=g, bg=bg: e.copy(out=g[:, 2:2 + N], in_=ps[bg][:, 0:N]), reads=[bg], writes=[gnm])
                P.op("dve", lambda e, g=g, f=f: e.tensor_copy(out=ghalo[:, l, f, :], in_=g[:, N:N + 2]), reads=[gnm], writes=["ghalo"])
                P.op("dve", lambda e, g=g, c_=c_, f=f: e.tensor_scalar(out=c_[:, 0:N], in0=g[:, 2:2 + N], scalar1=vF[:, f, l * 4 + 2:l * 4 + 3],
                                                                      scalar2=vF[:, f, l * 4 + 3:l * 4 + 4], op0=ALU.mult, op1=ALU.add),
                     reads=[gnm, "vF"], writes=[cn])
                P.op("dve", lambda e, g=g, c_=c_, f=f: e.scalar_tensor_tensor(out=c_[:, 0:N], in0=g[:, 1:1 + N], scalar=vF[:, f, l * 4 + 1:l * 4 + 2],
                                                                             in1=c_[:, 0:N], op0=ALU.mult, op1=ALU.add),
                     reads=[gnm, "vF", cn], writes=[cn])
                P.op("dve", lambda e, g=g, c_=c_, f=f: e.scalar_tensor_tensor(out=c_[:, 0:N], in0=g[:, 0:N], scalar=vF[:, f, l * 4:l * 4 + 1],
                                                                             in1=c_[:, 0:N], op0=ALU.mult, op1=ALU.add),
                     reads=[gnm, "vF", cn], writes=[cn])
                P.op("act", lambda e, c_=c_: e.activation(out=c_[:, 0:N], in_=c_[:, 0:N], func=AF.Silu), reads=[cn], writes=[cn])
            vu, un = wload("w_u", l, 0, 16, ft * 512, 512)
            for i in range(4):
                f = ft * 4 + i
                bu = ["psB", "psD"][i % 2]
                c_ = cc[i]
                cn = "cc%d" % i
                P.op("pe", seq([mm(ps[bu][:, 0:N], vu[:, k, i * 128:(i + 1) * 128], xbf[:, k, 0:N], k == 0, k == 15) for k in range(16)]),
                     reads=[un, "xbf"], writes=[bu])
                P.op("dve", lambda e, c_=c_, f=f, bu=bu: e.tensor_tensor(out=hT[:, f, 0:N], in0=c_[:, 0:N], in1=ps[bu][:, 0:N], op=ALU.mult),
                     reads=[cn, bu], writes=["hT"])
        if p >= 3:
            dstt = ffn_p[l] if p == 3 else ffn_s[l]
            for t in range(2):
                dv = dstt[t, :].rearrange("(f q) -> q f", q=128)
                for q4 in range(4):
                    P.dma("sp", lambda e, q4=q4, dv=dv, t=t: e.dma_start(out=dv[:, q4 * 11:(q4 + 1) * 11], in_=ghalo[:, l, q4 * 11:(q4 + 1) * 11, t],
                                                                         allow_slow_non_contiguous=True), reads=["ghalo"])
        out_proj_ln(N, "w_d", l, hT, "hT", NFF, 16 + l, 20 + l)
        A.reset(m)

    def conv_layer(j, i_layer, p, N):
        m = A.mark()
        aTb = A.alloc("aTb", [128, NCH, 544], BF16)
        cv_ = A.alloc("cv", [128, NCH, 512], F32)
        sig = [A.alloc("sig%d" % i, [128, 512], F32) for i in range(4)]
        dgc = [A.alloc("dgc0", [128, 31, 128], BF16), A.alloc("dgc1", [128, 31, 128], BF16)]
        alast = A.alloc("alast", [128, NCH, 32], F32)
        atm = A.alloc("atm", [32, D], F32)
        P.op("pool", lambda e: e.tensor_copy(out=aTb[:, :, 0:30], in_=ahalo[:, j, :, :]), reads=["ahalo"], writes=["aTb"])
        nl = min(30, N)
        for g4 in range(4):
            vg, gn = wload("w_pw1", j, 0, 16, D + g4 * 512, 512)
            for i in range(4):
                c = g4 * 4 + i
                bg = ["psB", "psD"][i % 2]
                P.op("pe", seq([mm(ps[bg][:, 0:N], vg[:, k, i * 128:(i + 1) * 128], xbf[:, k, 0:N], k == 0, k == 15) for k in range(16)]),
                     reads=[gn, "xbf"], writes=[bg])
                s_ = sig[i]
                sn = "sig%d" % i
                P.op("act", lambda e, s_=s_, bg=bg, c=c: e.activation(out=s_[:, 0:N], in_=ps[bg][:, 0:N], func=AF.Sigmoid,
                                                                      bias=vcol(25 + 2 * j, c), scale=1.0), reads=[bg, "vD"], writes=[sn])
            va, an = wload("w_pw1", j, 0, 16, g4 * 512, 512)
            for i in range(4):
                c = g4 * 4 + i
                ba = ["psA", "psC"][i % 2]
                s_ = sig[i]
                sn = "sig%d" % i
                P.op("pe", seq([mm(ps[ba][:, 0:N], va[:, k, i * 128:(i + 1) * 128], xbf[:, k, 0:N], k == 0, k == 15) for k in range(16)]),
                     reads=[an, "xbf"], writes=[ba])
                P.op("dve", lambda e, s_=s_, ba=ba, c=c: e.scalar_tensor_tensor(out=aTb[:, c, 30:30 + N], in0=ps[ba][:, 0:N], scalar=vcol(24 + 2 * j, c),
                                                                                in1=s_[:, 0:N], op0=ALU.add, op1=ALU.mult),
                     reads=[ba, sn, "vD"], writes=["aTb"])
                if p >= 3:
                    P.op("dve", lambda e, s_=s_, ba=ba, c=c: e.scalar_tensor_tensor(out=alast[:, c, 0:nl], in0=ps[ba][:, N - nl:N], scalar=vcol(24 + 2 * j, c),
                                                                                    in1=s_[:, N - nl:N], op0=ALU.add, op1=ALU.mult),
                         reads=[ba, sn, "vD"], writes=["alast"])
        P.op("pool", lambda e: e.tensor_copy(out=ahalo[:, j, :, :], in_=aTb[:, :, N:N + 30]), reads=["aTb"], writes=["ahalo"])
        if p >= 3:
            for c4 in range(4):
                bank = ["psE", "psF"][c4 % 2]
                fns = [lambda e, ii=ii, c4=c4: e.transpose(out=ps[bank][0:nl, ii * 128:(ii + 1) * 128], in_=alast[:, c4 * 4 + ii, 0:nl],
                                                           identity=identf[:, :]) for ii in range(4)]
                P.op("pe", seq(fns), reads=["alast", "identf"], writes=[bank])
                P.op("act", lambda e, c4=c4, bank=bank: e.copy(out=atm[0:nl, c4 * 512:(c4 + 1) * 512], in_=ps[bank][0:nl, :]),
                     reads=[bank], writes=["atm"])
            if p == 3:
                P.dma("sp", lambda e: e.dma_start(out=conv_p[j, :, :], in_=atm[0:30, :]), reads=["atm"])
            else:
                P.dma("sp", lambda e: e.dma_start(out=conv_s[j, 26:30, :], in_=atm[0:4, :]), reads=["atm"])
                P.dma("sp", lambda e: e.dma_start(out=conv_s[j, 0:26, :], in_=stc[j, 4:30, :]))
        ln = LN(N, cv_, "cv")
        for c in range(NCH):
            dg = dgc[c % 2]
            dn = "dgc%d" % (c % 2)
            P.op("dve", lambda e, dg=dg, c=c: e.tensor_tensor(out=dg[:, :, :], in0=identf[:, :].unsqueeze(1).broadcast_to([128, 31, 128]),
                                                              in1=wdwT[:, c, j * 31:j * 31 + 31].unsqueeze(2).broadcast_to([128, 31, 128]), op=ALU.mult),
                 reads=["identf", "wdwT"], writes=[dn])
            bank = ["psA", "psB"][c % 2]
            P.op("pe", seq([mm(ps[bank][:, 0:N], dg[:, w, :], aTb[:, c, w:w + N], w == 0, w == 30) for w in range(31)]),
                 reads=[dn, "aTb"], writes=[bank])
            P.op("act", lambda e, c=c, bank=bank: e.activation(out=cv_[:, c, 0:N], in_=ps[bank][:, 0:N], func=AF.Identity,
                                                               bias=vcol(0 + j, c), scale=1.0), reads=[bank, "vD"], writes=["cv"])
            ln.stats(c)
        ln.finalize()
        for c in range(NCH):
            ln.apply(c, vcol(2 + j, c), vcol(4 + j, c), out_bf=xbf[:, c, 0:N], out_bf_name="xbf", func=AF.Silu)
        P.barrier()
        A.reset(m)
        out_proj_ln(N, "w_pw2", j, xbf, "xbf", 16, 8 + i_layer, 12 + i_layer, bias_row=6 + j)

    def rope_fm(bank, N, tab, pmat, dst_ap, dst_name, bset, sw_bank, rows=128):
        bi, rawb, t1, t2 = bset
        rn_, t1n, t2n = "rawb%d" % bi, "t1%d" % bi, "t2%d" % bi
        C = tab[0:rows, 0, 0:N]
        S = tab[0:rows, 1, 0:N]
        P.op("act", lambda e: e.copy(out=rawb[0:rows, 0:N], in_=ps[bank][0:rows, 0:N]), reads=[bank], writes=[rn_])
        P.op("pe", mm(ps[sw_bank][0:rows, 0:N], pmat[0:rows, 0:rows], rawb[0:rows, 0:N], True, True), reads=[rn_, "pqb", "pib"], writes=[sw_bank])
        P.op("dve", lambda e: e.tensor_tensor(out=t1[0:rows, 0:N], in0=ps[bank][0:rows, 0:N], in1=C, op=ALU.mult), reads=[bank, "rq", "ri"], writes=[t1n])
        P.op("dve", lambda e: e.tensor_tensor(out=t2[0:rows, 0:N], in0=ps[sw_bank][0:rows, 0:N], in1=S, op=ALU.mult), reads=[sw_bank, "rq", "ri"], writes=[t2n])
        P.op("pool", lambda e: e.tensor_tensor(out=dst_ap, in0=t1[0:rows, 0:N], in1=t2[0:rows, 0:N], op=ALU.add), reads=[t1n, t2n], writes=[dst_name])

    def rope_tm(x3, nt, tt, nh, half, coff, resname):
        cosb = ttq[0:nt, tt, coff:coff + half].unsqueeze(1).broadcast_to([nt, nh, half])
        sinb = ttq[0:nt, tt, coff + half:coff + 2 * half].unsqueeze(1).broadcast_to([nt, nh, half])
        x1 = x3[:, :, 0:half]
        x2 = x3[:, :, half:2 * half]
        rtmp = RT["rtmp"]
        ta, tb, tc_, td = (rtmp[0:nt, k, 0:nh * half].rearrange("p (h d) -> p h d", h=nh) for k in range(4))
        P.op("dve", lambda e: e.tensor_tensor(out=ta, in0=x1, in1=cosb, op=ALU.mult), reads=[resname, "ttq"], writes=["rtmp"])
        P.op("dve", lambda e: e.tensor_tensor(out=tb, in0=x2, in1=sinb, op=ALU.mult), reads=[resname, "ttq"], writes=["rtmp"])
        P.op("dve", lambda e: e.tensor_tensor(out=tc_, in0=x2, in1=cosb, op=ALU.mult), reads=[resname, "ttq"], writes=["rtmp"])
        P.op("dve", lambda e: e.tensor_tensor(out=td, in0=x1, in1=sinb, op=ALU.mult), reads=[resname, "ttq"], writes=["rtmp"])
        P.op("dve", lambda e: e.tensor_tensor(out=x1, in0=ta, in1=tb, op=ALU.subtract), reads=["rtmp"], writes=[resname])
        P.op("dve", lambda e: e.tensor_tensor(out=x2, in0=tc_, in1=td, op=ALU.add), reads=["rtmp"], writes=[resname])

    RT = {}

    def bisect(scores, sname, nq, nk, theta, work):
        lo, wd, mid, cntt, sel, junk, cnt4 = work["lo"], work["wd"], work["mid"], work["cnt"], work["sel"], work["junk"], work["cnt4"]
        jw = work["jw"]
        chunks = []
        c0 = 0
        while c0 < nk:
            w = min(jw, nk - c0)
            chunks.append((c0, w))
            c0 += w
        assert len(chunks) <= 8
        P.op("dve", lambda e: e.tensor_reduce(out=mid[0:nq, :], in_=scores[0:nq, 0:nk], axis=AX.X, op=ALU.max), reads=[sname], writes=["bs_mid"])
        P.op("dve", lambda e: e.memset(lo[0:nq, :], NEG - 1.0), writes=["bs_lo"])
        P.op("dve", lambda e: e.tensor_scalar(out=wd[0:nq, :], in0=mid[0:nq, :], scalar1=-(NEG - 1.0), scalar2=None, op0=ALU.add), reads=["bs_mid"], writes=["bs_wd"])
        nch = len(chunks)
        for it in range(20):
            hf = 0.5 ** (it + 1)
            P.op("dve", lambda e, hf=hf: e.tensor_scalar(out=mid[0:nq, :], in0=wd[0:nq, :], scalar1=hf, scalar2=lo[0:nq, 0:1], op0=ALU.mult, op1=ALU.add),
                 reads=["bs_lo", "bs_wd"], writes=["bs_mid"])
            P.op("dve", lambda e: e.memset(cnt4[0:nq, 0:nch], 0.0), writes=["bs_cnt4"])
            for k, (c0, w) in enumerate(chunks):
                P.op("dve", lambda e, k=k, c0=c0, w=w: e.tensor_scalar(out=junk[0:nq, 0:w], in0=scores[0:nq, c0:c0 + w], scalar1=mid[0:nq, 0:1], scalar2=0.0,
                                                                      op0=ALU.is_gt, op1=ALU.add, accum_out=cnt4[0:nq, k:k + 1]),
                     reads=[sname, "bs_mid", "bs_cnt4"], writes=["bs_junk", "bs_cnt4"])
            if nch > 1:
                P.op("dve", lambda e: e.tensor_reduce(out=cntt[0:nq, :], in_=cnt4[0:nq, 0:nch], axis=AX.X, op=ALU.add), reads=["bs_cnt4"], writes=["bs_cnt"])
                csrc, csn = cntt, "bs_cnt"
            else:
                csrc, csn = cnt4, "bs_cnt4"
            P.op("dve", lambda e, csrc=csrc: e.scalar_tensor_tensor(out=sel[0:nq, :], in0=csrc[0:nq, 0:1], scalar=255.5, in1=wd[0:nq, :], op0=ALU.is_gt, op1=ALU.mult),
                 reads=[csn, "bs_wd"], writes=["bs_sel"])
            P.op("dve", lambda e, hf=hf: e.scalar_tensor_tensor(out=lo[0:nq, :], in0=sel[0:nq, :], scalar=hf, in1=lo[0:nq, :], op0=ALU.mult, op1=ALU.add),
                 reads=["bs_lo", "bs_sel"], writes=["bs_lo"])
        P.op("dve", lambda e: e.tensor_copy(out=theta[0:nq, :], in_=lo[0:nq, :]), reads=["bs_lo"], writes=["theta"])

    def indexer_tile(nq, qi_lhs, ki_rhs, ncols, wq_ap, dgt, Rb, sc_out, sc_name, kin_name, diag_mask_cols=None):
        banks = ["psA", "psB"]
        def dmm(h):
            P.op("pe", mm(ps[banks[h % 2]][0:nq, 0:ncols], qi_lhs(h), ki_rhs(h), True, True), reads=["qiT", kin_name], writes=[banks[h % 2]])
        dmm(0)
        masked = diag_mask_cols is not None
        if masked:
            P.op("pe", mm(ps["psC"][0:nq, 0:ncols], identb[0:nq, 0:nq], cmaskp[0:nq, 512 - ncols:512], True, False), reads=["identb", "cmaskp"], writes=["psC"])
        for h in range(16):
            if h + 1 < 16:
                dmm(h + 1)
            R = Rb[h % 2]
            rn = "Rb%d" % (h % 2)
            P.op("act", lambda e, R=R, h=h: e.activation(out=R[0:nq, 0:ncols], in_=ps[banks[h % 2]][0:nq, 0:ncols], func=AF.Relu),
                 reads=[banks[h % 2]], writes=[rn])
            P.op("pe", mm(ps["psC"][0:nq, 0:ncols], dgt[0:nq, h, 0:nq], R[0:nq, 0:ncols], (h == 0) and not masked, h == 15), reads=[rn, "dgt"], writes=["psC"])
        P.op("act", lambda e: e.copy(out=sc_out, in_=ps["psC"][0:nq, 0:ncols]), reads=["psC"], writes=[sc_name])

    def attn_layer(j, i_layer, p, N):
        m = A.mark()
        sample = (p == 4)
        wj = w_in[j]
        qT = A.alloc("qT", [128, 16, 512 if not sample else 4], BF16)
        KT = A.alloc("KT", [128, 4, SEQ if not sample else 4], BF16)
        Vb = A.alloc("Vb", [128, 16 if not sample else 1, 512], BF16)
        kiT = A.alloc("kiT", [128, SEQ if not sample else 4], BF16)
        qiT = A.alloc("qiT", [128, 8, 512] if not sample else [64, 16, 4], BF16)
        wsb = A.alloc("wsb", [128, 4, 16], F32)
        mtmp = A.mark()
        rawb = A.alloc("rawb", [128, 512], BF16)
        t1 = A.alloc("t1", [128, 512], F32)
        t2 = A.alloc("t2", [128, 512], F32)
        BS = [(0, rawb, t1, t2), (1, A.alloc("rawb1", [128, 512], BF16), A.alloc("t11", [128, 512], F32), A.alloc("t21", [128, 512], F32))]
        RT["rtmp"] = A.alloc("rtmp", [128, 4, 64], F32)
        ksb = A.alloc("ksb", [128, 512], F32)
        vsb = A.alloc("vsb", [128, 512], F32)
        kisb = A.alloc("kisb", [128, 80], F32)
        kbase = p * 512 if not sample else 0
        if not sample and p > 0:
            P.dma("sp", lambda e: e.dma_start(out=KT[:, :, 0:kbase], in_=scrKT[j, :, :, 0:kbase]), writes=["KT"])
            P.dma("sp", lambda e: e.dma_start(out=Vb[:, 0:4 * p, :], in_=scrV[j, :, 0:4 * p, :]), writes=["Vb"])
            P.dma("sp", lambda e: e.dma_start(out=kiT[:, 0:kbase], in_=scrKI[j, :, 0:kbase]), writes=["kiT"])
        for g4 in range(4):
            view, wn = wload("w_in", j, 0, 16, g4 * 512, 512)
            for i in range(4):
                bank = ["psA", "psB"][i % 2]
                P.op("pe", seq([mm(ps[bank][:, 0:N], view[:, k, i * 128:(i + 1) * 128], xbf[:, k, 0:N], k == 0, k == 15) for k in range(16)]),
                     reads=[wn, "xbf"], writes=[bank])
                rope_fm(bank, N, rq, pqb, qT[:, g4 * 4 + i, 0:N], "qT", BS[i % 2], ["psC", "psD"][i % 2])
        vk, kn = wload("w_in", j, 0, 16, 2048, 512)
        for i in range(4):
            bank = ["psA", "psB"][i % 2]
            P.op("pe", seq([mm(ps[bank][:, 0:N], vk[:, k, i * 128:(i + 1) * 128], xbf[:, k, 0:N], k == 0, k == 15) for k in range(16)]),
                 reads=[kn, "xbf"], writes=[bank])
            rope_fm(bank, N, rq, pqb, KT[:, i, kbase:kbase + N], "KT", BS[i % 2], ["psC", "psD"][i % 2])
        ntt = (N + 127) // 128
        kdst = k_p if not sample else k_s
        vdst = v_p if not sample else v_s
        for tt in range(ntt):
            nt = min(128, N - tt * 128)
            r0 = kbase + tt * 128
            P.op("pe", seq([mm(ps["psE"][0:nt, :], xbf[:, k, tt * 128:tt * 128 + nt], vk[:, k, :], k == 0, k == 15) for k in range(16)]),
                 reads=[kn, "xbf"], writes=["psE"])
            P.op("act", lambda e, nt=nt: e.copy(out=ksb[0:nt, :], in_=ps["psE"][0:nt, :]), reads=["psE"], writes=["ksb"])
            rope_tm(ksb[0:nt, :].rearrange("p (h d) -> p h d", h=4), nt, tt, 4, 16, 0, "ksb")
            P.dma("sp", lambda e, nt=nt, r0=r0: e.dma_start(out=kdst[j, r0:r0 + nt, :], in_=ksb[0:nt, :]), reads=["ksb"])
        vv, vn = wload("w_in", j, 0, 16, 2560, 512)
        for tt in range(ntt):
            nt = min(128, N - tt * 128)
            r0 = kbase + tt * 128
            P.op("pe", seq([mm(ps["psF"][0:nt, :], xbf[:, k, tt * 128:tt * 128 + nt], vv[:, k, :], k == 0, k == 15) for k in range(16)]),
                 reads=[vn, "xbf"], writes=["psF"])
            P.op("act", lambda e, nt=nt: e.copy(out=vsb[0:nt, :], in_=ps["psF"][0:nt, :]), reads=["psF"], writes=["vsb"])
            P.op("pool", lambda e, nt=nt, tt=tt: e.tensor_copy(out=Vb[0:nt, (kbase // 128 + tt) if not sample else 0, :], in_=vsb[0:nt, :]),
                 reads=["vsb"], writes=["Vb"])
            P.dma("sp", lambda e, nt=nt, r0=r0: e.dma_start(out=vdst[j, r0:r0 + nt, :], in_=vsb[0:nt, :]), reads=["vsb"])
        if not sample:
            for g2 in range(2):
                view, wn = wload("w_in", j, 0, 16, 3072 + g2 * 512, 512)
                for i in range(4):
                    bank = ["psA", "psB"][i % 2]
                    P.op("pe", seq([mm(ps[bank][:, 0:N], view[:, k, i * 128:(i + 1) * 128], xbf[:, k, 0:N], k == 0, k == 15) for k in range(16)]),
                         reads=[wn, "xbf"], writes=[bank])
                    rope_fm(bank, N, ri, pib, qiT[:, g2 * 4 + i, 0:N], "qiT", BS[i % 2], ["psC", "psD"][i % 2])
        else:
            for g2 in range(2):
                view, wn = wload("w_in", j, 0, 16, 3072 + g2 * 512, 512)
                fns = []
                for hh in range(8):
                    for k in range(16):
                        fns.append(mm(ps["psA"][0:64, (g2 * 8 + hh) * 4:(g2 * 8 + hh) * 4 + 4], view[:, k, hh * 64:(hh + 1) * 64], xbf[:, k, 0:4], k == 0, k == 15))
                P.op("pe", seq(fns), reads=[wn, "xbf"], writes=["psA"])
            Cb = ri[0:64, 0, 0:4].unsqueeze(1).broadcast_to([64, 16, 4])
            Sb = ri[0:64, 1, 0:4].unsqueeze(1).broadcast_to([64, 16, 4])
            P.op("act", lambda e: e.copy(out=rawb[0:64, 0:64], in_=ps["psA"][0:64, 0:64]), reads=["psA"], writes=["rawb0"])
            P.op("pe", mm(ps["psC"][0:64, 0:64], pib[0:64, 0:64], rawb[0:64, 0:64], True, True), reads=["rawb0", "pib"], writes=["psC"])
            P.op("dve", lambda e: e.tensor_tensor(out=t1[0:64, 0:64].rearrange("p (h t) -> p h t", h=16),
                                                  in0=ps["psA"][0:64, 0:64].rearrange("p (h t) -> p h t", h=16), in1=Cb, op=ALU.mult),
                 reads=["psA", "ri"], writes=["t10"])
            P.op("dve", lambda e: e.tensor_tensor(out=t2[0:64, 0:64].rearrange("p (h t) -> p h t", h=16),
                                                  in0=ps["psC"][0:64, 0:64].rearrange("p (h t) -> p h t", h=16), in1=Sb, op=ALU.mult),
                 reads=["psC", "ri"], writes=["t20"])
            P.op("pool", lambda e: e.tensor_tensor(out=qiT[0:64, :, :].rearrange("p h t -> p (h t)"), in0=t1[0:64, 0:64], in1=t2[0:64, 0:64], op=ALU.add),
                 reads=["t10", "t20"], writes=["qiT"])
        vkw, wn = wload("kiwi", j)
        P.op("pe", seq([mm(ps["psA"][:, 0:N], vkw[:, k, 0:128], xbf[:, k, 0:N], k == 0, k == 15) for k in range(16)]), reads=[wn, "xbf"], writes=["psA"])
        rope_fm("psA", N, ri, pib, kiT[:, kbase:kbase + N], "kiT", BS[0], "psC")
        kidst = ki_p if not sample else ki_s
        for tt in range(ntt):
            nt = min(128, N - tt * 128)
            r0 = kbase + tt * 128
            P.op("pe", seq([mm(ps["psE"][0:nt, 0:80], xbf[:, k, tt * 128:tt * 128 + nt], vkw[:, k, 128:208], k == 0, k == 15) for k in range(16)]),
                 reads=[wn, "xbf"], writes=["psE"])
            P.op("act", lambda e, nt=nt: e.copy(out=kisb[0:nt, :], in_=ps["psE"][0:nt, 0:80]), reads=["psE"], writes=["kisb"])
            P.op("pool", lambda e, nt=nt, tt=tt: e.tensor_copy(out=wsb[0:nt, tt, :], in_=kisb[0:nt, 64:80]), reads=["kisb"], writes=["wsb"])
            rope_tm(kisb[0:nt, 0:64].rearrange("p (h d) -> p h d", h=1), nt, tt, 1, 8, 32, "kisb")
            P.dma("sp", lambda e, nt=nt, r0=r0: e.dma_start(out=kidst[j, r0:r0 + nt, :], in_=kisb[0:nt, 0:64]), reads=["kisb"])
        if not sample and p < 3:
            P.dma("sp", lambda e: e.dma_start(out=scrKT[j, :, :, kbase:kbase + N], in_=KT[:, :, kbase:kbase + N]), reads=["KT"])
            P.dma("sp", lambda e: e.dma_start(out=scrV[j, :, 4 * p:4 * p + 4, :], in_=Vb[:, 4 * p:4 * p + 4, :]), reads=["Vb"])
            P.dma("sp", lambda e: e.dma_start(out=scrKI[j, :, kbase:kbase + N], in_=kiT[:, kbase:kbase + N]), reads=["kiT"])

        ckpt('proj')
        P.barrier()
        A.reset(mtmp)
        dgt = A.alloc("dgt", [128, 16, 128], BF16)
        Rb = [A.alloc("Rb0", [128, 512], BF16), A.alloc("Rb1", [128, 512], BF16)]
        Eb = [A.alloc("Eb0", [128, 512], BF16), A.alloc("Eb1", [128, 512], BF16)]
        Pb = [A.alloc("Pb0", [128, 512], BF16), A.alloc("Pb1", [128, 512], BF16)]
        rden = A.alloc("rden", [128, 512], F32)
        theta = A.alloc("theta", [128, 1], F32)
        work = {k: A.alloc("bs_" + k, [128, 1], F32) for k in ("lo", "wd", "mid", "cnt", "sel")}
        work["cnt4"] = A.alloc("bs_cnt4", [128, 8], F32)
        scale = 128.0 ** -0.5
        if not sample:
            scb = [A.alloc("scores0", [128, SEQ], F32), A.alloc("scores1", [128, SEQ], F32)]
            work["junk"] = A.alloc("bs_junk", [128, SEQ], BF16)
            work["jw"] = SEQ
            maskf = A.alloc("maskf", [128, SEQ], BF16)
            maskT = A.alloc("maskT", [128, 16, 128], BF16)

            def run_indexer(tt):
                qt = p * 4 + tt
                nk = (qt + 1) * 128
                scores = scb[tt % 2]
                sname = "scores%d" % (tt % 2)
                P.op("pool", lambda e: e.tensor_tensor(out=dgt[:, :, :], in0=identf[:, :].unsqueeze(1).broadcast_to([128, 16, 128]),
                                                       in1=wsb[:, tt, :].unsqueeze(2).broadcast_to([128, 16, 128]), op=ALU.mult),
                     reads=["identf", "wsb"], writes=["dgt"])
                k0 = 0
                while k0 < nk:
                    ncols = min(512, nk - k0)
                    last = (k0 + ncols == nk)
                    indexer_tile(128, lambda h: qiT[(h % 2) * 64:(h % 2) * 64 + 64, h // 2, tt * 128:(tt + 1) * 128],
                                 lambda h, k0=k0, ncols=ncols: kiT[(h % 2) * 64:(h % 2) * 64 + 64, k0:k0 + ncols],
                                 ncols, None, dgt, Rb, scores[:, k0:k0 + ncols], sname, "kiT",
                                 diag_mask_cols=(ncols - 128) if last else None)
                    k0 += ncols

            run_indexer(0)
            for tt in range(4):
                qt = p * 4 + tt
                nk = (qt + 1) * 128
                scores = scb[tt % 2]
                sname = "scores%d" % (tt % 2)
                if tt + 1 < 4:
                    run_indexer(tt + 1)
                if qt >= 2:
                    bisect(scores, sname, 128, nk, theta, work)
                else:
                    P.op("dve", lambda e: e.memset(theta[:, :], NEG * 0.5), writes=["theta"])
                P.op("dve", lambda e, nk=nk, scores=scores: e.tensor_scalar(out=maskf[:, 0:nk], in0=scores[:, 0:nk], scalar1=theta[:, 0:1], scalar2=None, op0=ALU.is_gt),
                     reads=[sname, "theta"], writes=["maskf"])
                nk8 = qt + 1
                mtb = ps["psD"][:, :].bitcast(BF16)
                for b4 in range((nk8 + 3) // 4):
                    n4 = min(4, nk8 - b4 * 4)
                    fns = [lambda e, i=i, b4=b4: e.transpose(out=mtb[:, i * 128:(i + 1) * 128], in_=maskf[:, (b4 * 4 + i) * 128:(b4 * 4 + i + 1) * 128],
                                                             identity=identb[:, :]) for i in range(n4)]
                    P.op("pe", seq(fns), reads=["maskf", "identb"], writes=["psD"])
                    P.op("act", lambda e, b4=b4, n4=n4: e.copy(out=maskT[:, b4 * 4:b4 * 4 + n4, :], in_=mtb[:, 0:n4 * 128].rearrange("p (i q) -> p i q", i=n4)),
                         reads=["psD"], writes=["maskT"])
                for n in range(4):
                    for k8 in range(nk8):
                        sb = ["psE", "psF"][k8 % 2]
                        P.op("pe", mm(ps[sb][:, :], KT[:, n, k8 * 128:(k8 + 1) * 128], qT[:, 4 * n:4 * n + 4, tt * 128:(tt + 1) * 128], True, True),
                             reads=["KT", "qT"], writes=[sb])
                        E = Eb[k8 % 2]
                        en = "Eb%d" % (k8 % 2)
                        Pt = Pb[k8 % 2]
                        pn = "Pb%d" % (k8 % 2)
                        P.op("act", lambda e, E=E, sb=sb: e.activation(out=E[:, :], in_=ps[sb][:, :], func=AF.Exp, scale=scale), reads=[sb], writes=[en])
                        P.op("dve", lambda e, E=E, Pt=Pt, k8=k8: e.tensor_tensor(out=Pt[:, :].rearrange("p (g q) -> p g q", g=4),
                                                                                in0=E[:, :].rearrange("p (g q) -> p g q", g=4),
                                                                                in1=maskT[:, k8, :].unsqueeze(1).broadcast_to([128, 4, 128]), op=ALU.mult),
                             reads=[en, "maskT"], writes=[pn])
                        P.op("pe", seq([mm(ps["psG"][:, :], Vb[:, k8, n * 128:(n + 1) * 128], Pt[:, :], k8 == 0, k8 == nk8 - 1),
                                        mm(ps["psH"][:, :], onesb[:, :], Pt[:, :], k8 == 0, k8 == nk8 - 1)]),
                             reads=[pn, "Vb", "onesb"], writes=["psG", "psH"])
                    P.op("dve", lambda e: e.reciprocal(out=rden[:, :], in_=ps["psH"][:, :]), reads=["psH"], writes=["rden"])
                    P.op("dve", lambda e, n=n, tt=tt: e.tensor_tensor(out=xbf[:, 4 * n:4 * n + 4, tt * 128:(tt + 1) * 128],
                                                                      in0=ps["psG"][:, :].rearrange("p (g q) -> p g q", g=4),
                                                                      in1=rden[:, :].rearrange("p (g q) -> p g q", g=4), op=ALU.mult),
                         reads=["psG", "rden"], writes=["xbf"])
        else:
            sample_attn(j, qT, KT, Vb, kiT, qiT, wsb, dgt, Rb, Eb, Pb, rden, theta, work, scale)
        P.barrier()
        ckpt('attn')
        A.reset(m)
        out_proj_ln(N, "w_out", j, xbf, "xbf", 16, 8 + i_layer, 12 + i_layer)
        ckpt('attn_out')

    def sample_attn(j, qT, KT, Vb, kiT, qiT, wsb, dgt, Rb, Eb, Pb, rden, theta, work, scale):
        NKS = PAST + 4
        scores = A.alloc("scores", [4, NKS + 12], F32)
        work["junk"] = A.alloc("bs_junk", [4, 4100], BF16)
        work["jw"] = 4100
        maskc = A.alloc("maskc", [4, 512], BF16)
        maskT = A.alloc("maskT", [128, 129, 4], BF16)
        G = A.alloc("G", [128, 32, 64], F32)
        kiq = A.alloc("kiq", [64, 4096], BF16)
        ptc = A.alloc("ptc", [128, 1], I32)
        kpg = [A.alloc("kpg0", [128, 512], F32), A.alloc("kpg1", [128, 512], F32)]
        vpg = [A.alloc("vpg0", [128, 512], F32), A.alloc("vpg1", [128, 512], F32)]
        ktb = A.alloc("ktb", [128, 4, 128], BF16)
        vpb = A.alloc("vpb", [128, 512], BF16)
        oacc = A.alloc("oacc", [128, 128], F32)
        P.dma("sp", lambda e: e.dma_start(out=ptc[:, :], in_=pt.unsqueeze(1)), writes=["ptc"])
        A4 = A.alloc("A4", [4, 16, 4], BF16)
        Wsel = A.alloc("Wsel", [64, 4], BF16)
        P.op("dve", lambda e: e.tensor_tensor(out=A4[:, :, :], in0=identf[0:4, 0:4].unsqueeze(1).broadcast_to([4, 16, 4]),
                                              in1=wsb[0:4, 0, :].unsqueeze(2).broadcast_to([4, 16, 4]), op=ALU.mult),
             reads=["identf", "wsb"], writes=["A4"])
        mtb0 = ps["psD"][:, :].bitcast(BF16)
        P.op("pe", lambda e: e.transpose(out=mtb0[0:64, 0:4], in_=A4[:, :, :].rearrange("p h t -> p (h t)"), identity=identb[0:4, 0:4]),
             reads=["A4", "identb"], writes=["psD"])
        P.op("act", lambda e: e.copy(out=Wsel[:, :], in_=mtb0[0:64, 0:4]), reads=["psD"], writes=["Wsel"])
        qi64 = qiT[0:64, :, :].rearrange("p h t -> p (h t)")

        def packed_tile(ki_rhs, kin_name, ncols, sc_out, bi, mask_new=False):
            bd = ["psA", "psB"][bi % 2]
            R = Rb[bi % 2]
            rn = "Rb%d" % (bi % 2)
            P.op("pe", mm(ps[bd][0:64, 0:ncols], qi64, ki_rhs, True, True), reads=["qiT", kin_name], writes=[bd])
            P.op("act", lambda e: e.activation(out=R[0:64, 0:ncols], in_=ps[bd][0:64, 0:ncols], func=AF.Relu), reads=[bd], writes=[rn])
            P.op("pe", mm(ps["psC"][0:4, 0:ncols], Wsel[:, :], R[0:64, 0:ncols], True, True), reads=[rn, "Wsel"], writes=["psC"])
            if not mask_new:
                P.op("act", lambda e: e.copy(out=sc_out, in_=ps["psC"][0:4, 0:ncols]), reads=["psC"], writes=["scores"])
            else:
                P.op("dve", lambda e: e.tensor_tensor(out=sc_out, in0=ps["psC"][0:4, 0:ncols], in1=cmask[0:4, 0:ncols], op=ALU.add),
                     reads=["psC", "cmask"], writes=["scores"])

        ptf = A.alloc("ptf", [128, 1], F32)
        idxqf = A.alloc("idxqf", [128, 4], F32)
        idxq = A.alloc("idxq", [128, 4], I32)
        P.op("dve", lambda e: e.tensor_copy(out=ptf[:, :], in_=ptc[:, :]), reads=["ptc"], writes=["ptf"])
        for q in range(4):
            P.op("dve", lambda e, q=q: e.tensor_scalar(out=idxqf[:, q:q + 1], in0=ptf[:, :], scalar1=4.0, scalar2=float(j * NPOOL * 4 + q),
                                                       op0=ALU.mult, op1=ALU.add), reads=["ptf"], writes=["idxqf"])
        P.op("dve", lambda e: e.tensor_copy(out=idxq[:, :], in_=idxqf[:, :]), reads=["idxqf"], writes=["idxq"])
        for rq4 in range(4):
            P.dma("pool", lambda e, rq4=rq4: e.indirect_dma_start(out=G[:, :, :].rearrange("p r d -> p (r d)"), out_offset=None,
                                                                   in_=cki[:, :],
                                                                   in_offset=bass.IndirectOffsetOnAxis(ap=idxq[:, rq4:rq4 + 1], axis=0)),
                  reads=["idxq"], writes=["G"])
            for r4 in range(8):
                bank = ["psE", "psF"][r4 % 2]
                fns = [lambda e, i=i, r4=r4: e.transpose(out=ps[bank][0:64, i * 128:(i + 1) * 128], in_=G[:, r4 * 4 + i, :], identity=identf[:, :]) for i in range(4)]
                P.op("pe", seq(fns), reads=["G", "identf"], writes=[bank])
                P.op("dve", lambda e, r4=r4, bank=bank: e.tensor_copy(out=kiq[0:64, r4 * 512:(r4 + 1) * 512], in_=ps[bank][0:64, :]), reads=[bank], writes=["kiq"])
            for kt in range(8):
                packed_tile(kiq[0:64, kt * 512:(kt + 1) * 512], "kiq", 512,
                            scores[0:4, rq4 * 4096 + kt * 512: rq4 * 4096 + (kt + 1) * 512], kt)
        packed_tile(kiT[0:64, 0:4], "kiT", 4, scores[0:4, PAST:PAST + 4], 0, mask_new=True)
        bisect(scores, "scores", 4, NKS, theta, work)
        mtb = ps["psD"][:, :].bitcast(BF16)
        for kt5 in range(33):
            ncols = 512 if kt5 < 32 else 4
            c0 = kt5 * 512
            P.op("dve", lambda e, c0=c0, ncols=ncols: e.tensor_scalar(out=maskc[0:4, 0:ncols], in0=scores[0:4, c0:c0 + ncols], scalar1=theta[0:4, 0:1],
                                                                      scalar2=None, op0=ALU.is_gt), reads=["scores", "theta"], writes=["maskc"])
            nsub = (ncols + 127) // 128
            fns = [lambda e, i=i, ncols=ncols: e.transpose(out=mtb[0:min(128, ncols - i * 128), i * 4:i * 4 + 4], in_=maskc[0:4, i * 128:min((i + 1) * 128, ncols)],
                                                           identity=identb[0:4, 0:4]) for i in range(nsub)]
            P.op("pe", seq(fns), reads=["maskc", "identb"], writes=["psD"])
            rows = 128 if kt5 < 32 else 4
            P.op("act", lambda e, kt5=kt5, nsub=nsub, rows=rows: e.copy(out=maskT[0:rows, kt5 * 4:kt5 * 4 + nsub, :],
                                                                        in_=mtb[0:rows, 0:nsub * 4].rearrange("p (i q) -> p i q", i=nsub)),
                 reads=["psD"], writes=["maskT"])
        P.op("dve", lambda e: e.memset(oacc[:, :], 0.0), writes=["oacc"])
        idxT = A.alloc("idxT", [128, 128], I32)
        idxTf = A.alloc("idxTf", [128, 128], F32)
        rowi = A.alloc("rowi", [128, 128], F32)
        P.dma("sp", lambda e: e.dma_start(out=rowi[:, :], in_=c_iota[:, :]), writes=["rowi"])
        P.op("dve", lambda e: e.tensor_scalar(out=idxTf[:, 0:1], in0=ptf[:, :], scalar1=128.0, scalar2=float(j * NPOOL * 128), op0=ALU.mult, op1=ALU.add),
             reads=["ptf"], writes=["idxTf"])
        P.op("dve", lambda e: e.tensor_scalar(out=rowi[:, :], in0=rowi[:, :], scalar1=idxTf[:, 0:1], scalar2=None, op0=ALU.add), reads=["rowi", "idxTf"], writes=["rowi"])
        P.op("dve", lambda e: e.tensor_copy(out=idxT[:, :], in_=rowi[:, :]), reads=["rowi"], writes=["idxT"])
        ckj = ck
        cvj = cv
        for r in range(129):
            newt = (r == 128)
            rows = 128 if not newt else 4
            kp = kpg[r % 2]; kpn = "kpg%d" % (r % 2)
            vp = vpg[r % 2]; vpn = "vpg%d" % (r % 2)
            if not newt:
                P.dma("pool", lambda e, r=r, kp=kp: e.indirect_dma_start(out=kp[:, :], out_offset=None, in_=ckj[:, :],
                                                                          in_offset=bass.IndirectOffsetOnAxis(ap=idxT[:, r:r + 1], axis=0)),
                      reads=["idxT"], writes=[kpn])
                P.dma("pool", lambda e, r=r, vp=vp: e.indirect_dma_start(out=vp[:, :], out_offset=None, in_=cvj[:, :],
                                                                          in_offset=bass.IndirectOffsetOnAxis(ap=idxT[:, r:r + 1], axis=0)),
                      reads=["idxT"], writes=[vpn])
                fns = [lambda e, n=n, kp=kp: e.transpose(out=ps["psE"][:, n * 128:(n + 1) * 128], in_=kp[:, n * 128:(n + 1) * 128], identity=identf[:, :]) for n in range(4)]
                P.op("pe", seq(fns), reads=[kpn, "identf"], writes=["psE"])
                P.op("act", lambda e: e.copy(out=ktb[:, :, :].rearrange("p n k -> p (n k)"), in_=ps["psE"][:, :]), reads=["psE"], writes=["ktb"])
                P.op("pool", lambda e, vp=vp: e.tensor_copy(out=vpb[:, :], in_=vp[:, :]), reads=[vpn], writes=["vpb"])
                mt_idx = (r // 32) * 32 + (r % 32)
                KTn = lambda n: ktb[:, n, :]
                Vn = lambda n: vpb[:, n * 128:(n + 1) * 128]
                vname, kname = "vpb", "ktb"
            else:
                mt_idx = 128
                KTn = lambda n: KT[:, n, 0:4]
                Vn = lambda n: Vb[0:4, 0, n * 128:(n + 1) * 128]
                vname, kname = "Vb", "KT"
            fns = [mm(ps["psF"][0:rows, n * 16:(n + 1) * 16], KTn(n), qT[:, 4 * n:4 * n + 4, 0:4], True, True) for n in range(4)]
            P.op("pe", seq(fns), reads=[kname, "qT"], writes=["psF"])
            E = Eb[r % 2]; en = "Eb%d" % (r % 2)
            Pt = Pb[r % 2]; pn = "Pb%d" % (r % 2)
            P.op("act", lambda e, E=E, rows=rows: e.activation(out=E[0:rows, 0:64], in_=ps["psF"][0:rows, 0:64], func=AF.Exp, scale=scale), reads=["psF"], writes=[en])
            P.op("dve", lambda e, E=E, Pt=Pt, rows=rows, mt_idx=mt_idx: e.tensor_tensor(out=Pt[0:rows, 0:64].rearrange("p (a q) -> p a q", q=4),
                                                                                       in0=E[0:rows, 0:64].rearrange("p (a q) -> p a q", q=4),
                                                                                       in1=maskT[0:rows, mt_idx, :].unsqueeze(1).broadcast_to([rows, 16, 4]), op=ALU.mult),
                 reads=[en, "maskT"], writes=[pn])
            fns = [mm(ps["psG"][:, n * 16:(n + 1) * 16], Vn(n), Pt[0:rows, n * 16:(n + 1) * 16], True, True) for n in range(4)]
            fns.append(mm(ps["psG"][:, 64:128], onesb[0:rows, :], Pt[0:rows, 0:64], True, True))
            P.op("pe", seq(fns), reads=[pn, vname, "onesb"], writes=["psG"])
            P.op("dve", lambda e: e.tensor_tensor(out=oacc[:, :], in0=oacc[:, :], in1=ps["psG"][:, 0:128], op=ALU.add), reads=["psG", "oacc"], writes=["oacc"])
        P.op("dve", lambda e: e.reciprocal(out=rden[:, 0:64], in_=oacc[:, 64:128]), reads=["oacc"], writes=["rden"])
        P.op("dve", lambda e: e.tensor_tensor(out=xbf[:, :, 0:4], in0=oacc[:, 0:64].rearrange("p (h q) -> p h q", q=4),
                                              in1=rden[:, 0:64].rearrange("p (h q) -> p h q", q=4), op=ALU.mult), reads=["oacc", "rden"], writes=["xbf"])

    class StopBuild(Exception):
        pass

    CUR = {}

    def ckpt(name):
        if cfg.get("stop") == name:
            P.barrier()
            A.reset(CUR["mark"])
            store_y(CUR["p"], CUR["N"])
            raise StopBuild()

    try:
      for p in pass_list:
          sample = (p == 4)
          N = 4 if sample else 512
          if sample:
              P.barrier()
              A.reset(cmark)
              xres = A.alloc("xres", [128, NCH, 4], F32)
              xbf = A.alloc("xbf", [128, NCH, 4], BF16)
              rq = A.alloc("rq", [128, 2, 512], F32)
              ri = A.alloc("ri", [128, 2, 512], F32)
              ttq = A.alloc("ttq", [128, 4, 48], F32)
              ghalo = A.alloc("ghalo", [128, 4, NFF, 2], F32)
              ahalo = A.alloc("ahalo", [128, 2, NCH, 30], BF16)
          P.dma("sp", lambda e, p=p: e.dma_start(out=rq[:, :, :], in_=c_ropeq[p].rearrange("c p n -> p c n")), writes=["rq"])
          P.dma("sp", lambda e, p=p: e.dma_start(out=ri[:, :, :], in_=c_ropei[p].rearrange("c p n -> p c n")), writes=["ri"])
          r0 = p * 512 if not sample else SEQ
          if not sample:
              P.dma("sp", lambda e, r0=r0: e.dma_start(out=ttq[:, :, :], in_=c_ttab[r0:r0 + 512, :].rearrange("(t p) c -> p t c", p=128)), writes=["ttq"])
          else:
              P.dma("sp", lambda e: e.dma_start(out=ttq[:, 0, :], in_=c_ttab[SEQ:SEQ + 128, :]), writes=["ttq"])
          if p == 0:
              P.op("pool", lambda e: e.memset(ghalo[:, :, :, :].rearrange("p a b c -> p (a b c)"), 0.0), writes=["ghalo"])
              P.op("pool", lambda e: e.memset(ahalo[:, :, :, :].rearrange("p a b c -> p (a b c)"), 0.0), writes=["ahalo"])
          if sample:
              for l in range(4):
                  for t in range(2):
                      sv = stf[l, t, :].rearrange("(f q) -> q f", q=128)
                      for q4 in range(4):
                          P.dma("sp", lambda e, l=l, q4=q4, sv=sv, t=t: e.dma_start(out=ghalo[:, l, q4 * 11:(q4 + 1) * 11, t], in_=sv[:, q4 * 11:(q4 + 1) * 11],
                                                                                    allow_slow_non_contiguous=True), writes=["ghalo"])
              m = A.mark()
              stg = A.alloc("stage", [64, DFF], F32)
              atmp = A.alloc("atmp", [128, NCH, 30], F32)
              names[id(atmp)] = "atmp"
              for j in range(2):
                  rows_to_fm(stc[j, :, :], 30, D, atmp, stg)
                  P.op("dve", lambda e, j=j: e.tensor_copy(out=ahalo[:, j, :, :], in_=atmp[:, :, :]), reads=["atmp"], writes=["ahalo"])
              P.barrier()
              A.reset(m)
          CUR.update(p=p, N=N, mark=A.mark())
          load_x(p, N)
          ckpt('load_x')
          for i_layer in range(nlayers):
              j = i_layer // 2
              if i_layer % 2 == 0:
                  attn_layer(j, i_layer, p, N)
              else:
                  conv_layer(j, i_layer, p, N)
              ffn_layer(i_layer, p, N)
          store_y(p, N)
    except StopBuild:
        pass
    if DRY:
        return wrecs
    P.barrier()
    P.emit()
    return nc


def _consts():
    theta = np.float32(500000.0)
    invq = (theta ** (-np.arange(0, 32, 2, dtype=np.float32) / np.float32(32))).astype(np.float32)
    invi = (theta ** (-np.arange(0, 16, 2, dtype=np.float32) / np.float32(16))).astype(np.float32)
    pos_all = np.concatenate([np.arange(SEQ, dtype=np.float32), PAST + np.arange(128, dtype=np.float32)])
    angq = pos_all[:, None] * invq[None, :]
    angi = pos_all[:, None] * invi[None, :]
    ttab = np.concatenate([np.cos(angq), np.sin(angq), np.cos(angi), np.sin(angi)], axis=1).astype(np.float32)
    ropeq = np.zeros((5, 2, 128, 512), np.float32)
    ropei = np.zeros((5, 2, 128, 512), np.float32)
    ropeq[:, 0] = 1.0
    ropei[:, 0] = 1.0
    for p in range(5):
        rows = np.arange(p * 512, p * 512 + 512) if p < 4 else np.concatenate([np.arange(SEQ, SEQ + 128)] * 4)
        cq, sq = np.cos(angq[rows]).T, np.sin(angq[rows]).T
        ci, si = np.cos(angi[rows]).T, np.sin(angi[rows]).T
        ropeq[p, 0, 0:16] = cq; ropeq[p, 0, 16:32] = cq
        ropeq[p, 1, 0:16] = sq; ropeq[p, 1, 16:32] = sq
        for o in (0, 64):
            ropei[p, 0, o:o + 8] = ci; ropei[p, 0, o + 8:o + 16] = ci
            ropei[p, 1, o:o + 8] = si; ropei[p, 1, o + 8:o + 16] = si
    pq = np.zeros((128, 128), np.float32)
    for m in range(16):
        pq[m + 16, m] = -1.0
        pq[m, m + 16] = 1.0
    pi = np.zeros((128, 128), np.float32)
    for o in (0, 64):
        for m in range(8):
            pi[o + m + 8, o + m] = -1.0
            pi[o + m, o + m + 8] = 1.0
    cm = np.where(np.arange(128)[None, :] <= np.arange(128)[:, None], 0.0, NEG).astype(np.float32)
    return dict(c_ident=np.eye(128, dtype=np.float32), c_pq=pq, c_pi=pi, c_ropeq=ropeq, c_ropei=ropei,
                c_ttab=np.ascontiguousarray(ttab[:SEQ + 128]), c_cmask=cm,
                c_iota=np.ascontiguousarray(np.broadcast_to(np.arange(128, dtype=np.float32)[None, :], (128, 128))))


def make_in_maps(inp, ncores=8):
    f = lambda a: np.ascontiguousarray(np.asarray(a, dtype=np.float32))
    cst = _consts()
    vecD = np.concatenate([f(inp["b_dw"]), f(inp["ln_conv_g"]), f(inp["ln_conv_b"]), f(inp["b_pw2"]),
                           f(inp["ln_mix_g"]), f(inp["ln_mix_b"]), f(inp["ln_ffn_g"]), f(inp["ln_ffn_b"]),
                           f(inp["b_pw1"]).reshape(4, D)], axis=0)
    vecF = np.concatenate([np.concatenate([f(inp["w_ffn_conv"])[l], f(inp["b_ffn_conv"])[l][None]], axis=0) for l in range(4)], axis=0)
    shared = dict(
        ck=f(inp["cache_k"]).reshape(-1, 512), cv=f(inp["cache_v"]).reshape(-1, 512),
        cki=f(inp["cache_kidx"]).reshape(-1, 2048),
        w_in=f(inp["w_attn_in"]), w_out=f(inp["w_attn_out"]), w_pw1=f(inp["w_pw1"]), w_pw2=f(inp["w_pw2"]),
        w_g=f(inp["w_ffn_gate"]), w_u=f(inp["w_ffn_up"]), w_d=f(inp["w_ffn_down"]),
        vecD=np.ascontiguousarray(vecD), vecF=np.ascontiguousarray(vecF), wdw=f(inp["w_dw"]).reshape(62, D), **cst)
    maps = []
    for c in range(ncores):
        m = dict(shared)
        m["xp"] = f(inp["x_prompt"])[c % 4]
        m["xs"] = f(inp["x_sample"])[c]
        m["stc"] = np.ascontiguousarray(f(inp["state_conv"])[:, c])
        m["stf"] = np.ascontiguousarray(f(inp["state_ffn"])[:, c])
        m["pt"] = np.ascontiguousarray(np.asarray(inp["page_table"], dtype=np.int32)[c])
        maps.append(m)
    return maps


def assemble(res):
    r = res
    y_p = np.stack([r[c]["y_p"] for c in range(4)])
    y_s = np.stack([r[c]["y_s"] for c in range(8)])
    k_p = np.stack([r[c]["k_p"] for c in range(4)], axis=1).reshape(2, 4, SEQ, 4, 128)
    v_p = np.stack([r[c]["v_p"] for c in range(4)], axis=1).reshape(2, 4, SEQ, 4, 128)
    ki_p = np.stack([r[c]["ki_p"] for c in range(4)], axis=1)
    conv_p = np.stack([r[c]["conv_p"] for c in range(4)], axis=1)
    ffn_p = np.stack([r[c]["ffn_p"] for c in range(4)], axis=1)
    k_s = np.stack([r[c]["k_s"] for c in range(8)], axis=1).reshape(2, 8, 4, 4, 128)
    v_s = np.stack([r[c]["v_s"] for c in range(8)], axis=1).reshape(2, 8, 4, 4, 128)
    ki_s = np.stack([r[c]["ki_s"] for c in range(8)], axis=1)
    conv_s = np.stack([r[c]["conv_s"] for c in range(8)], axis=1)
    ffn_s = np.stack([r[c]["ffn_s"] for c in range(8)], axis=1)
    outs = (y_p, y_s, k_p, v_p, ki_p, conv_p, ffn_p, k_s, v_s, ki_s, conv_s, ffn_s)
    return tuple(np.ascontiguousarray(o, dtype=np.float32) for o in outs)


def kernel(**inputs):
    nc = build({})
    maps = make_in_maps(inputs)
    res = run_bass_kernel_spmd(nc, maps, core_ids=list(range(8)))
    return assemble(res.results)
```
